# Optimizing a Trainium2 kernel written in Bass

```python
import math
import jax, jax.numpy as jnp
from jax import lax
import numpy as np

D_MODEL = 1024
BATCH = 8
SEQ = 2048
DEPTH = 1
DEC_BATCH = 128
DEC_SEQ = 4
PAST_LEN = 16384
PAGE_SIZE = 128

D_PLE = 256
MIX_A = 512
MIX_B = 512
N_HEADS_A = 4
HEAD_DIM_A = MIX_A // N_HEADS_A
CONV_W = 4
CHUNK_A = 64
S5_GROUP = 16
N_GROUPS_B = MIX_B // S5_GROUP
S5_STATE = 64
EPS = 1e-6
IN_COLS = 3 * MIX_A + 2 * N_HEADS_A + 2 * MIX_B
SPLIT_IDX = (MIX_A, 2 * MIX_A, 3 * MIX_A, 3 * MIX_A + N_HEADS_A,
             3 * MIX_A + 2 * N_HEADS_A, 3 * MIX_A + 2 * N_HEADS_A + MIX_B)

kernel_name = 'hymba_mlstm_s5_decode_step'


def rmsnorm(x, g):
    xf = x.astype(jnp.float32)
    y = xf * lax.rsqrt(jnp.mean(xf * xf, axis=-1, keepdims=True) + EPS)
    return (y * g.astype(jnp.float32)).astype(x.dtype)


def causal_conv(x, buf, w, b):
    L = x.shape[1]
    xp = jnp.concatenate([buf, x], axis=1)
    y = b.astype(jnp.float32)
    for j in range(CONV_W):
        y = y + xp[:, j:j + L] * w[j].astype(jnp.float32)
    return y, xp[:, L:]


def mlstm_chunkwise(q, k, v, i_t, logf, C0, n0, m0):
    Bn, L, H, d = q.shape
    c = math.gcd(L, CHUNK_A)
    nc = L // c

    def to_chunks(t):
        t = t.reshape((Bn, nc, c) + t.shape[2:])
        return jnp.moveaxis(t, (1, 3), (0, 2))

    causal = jnp.tril(jnp.ones((c, c), dtype=bool))

    def step(carry, xs):
        C, n, m = carry
        qc, kc, vc, ic, fc = xs
        b = jnp.cumsum(fc, axis=-1)
        dmat = b[..., :, None] - b[..., None, :] + ic[..., None, :]
        dmat = jnp.where(causal, dmat, -jnp.inf)
        m_inter = b + m[..., None]
        m_t = jnp.maximum(m_inter, jnp.max(dmat, axis=-1))
        s = jnp.einsum('bhtk,bhsk->bhts', qc, kc) * jnp.exp(dmat - m_t[..., None])
        decay = jnp.exp(m_inter - m_t)
        num = jnp.einsum('bhts,bhsv->bhtv', s, vc) + decay[..., None] * jnp.einsum('bhtk,bhkv->bhtv', qc, C)
        qn = jnp.sum(s, axis=-1) + decay * jnp.einsum('bhtk,bhk->bht', qc, n)
        h = num / jnp.maximum(jnp.abs(qn), jnp.exp(-m_t))[..., None]
        m_new = m_t[..., -1]
        wl = jnp.exp(b[..., -1:] - b + ic - m_new[..., None])
        dl = jnp.exp(b[..., -1] + m - m_new)
        C_new = dl[..., None, None] * C + jnp.einsum('bhs,bhsk,bhsv->bhkv', wl, kc, vc)
        n_new = dl[..., None] * n + jnp.einsum('bhs,bhsk->bhk', wl, kc)
        return (C_new, n_new, m_new), h

    xs = (to_chunks(q), to_chunks(k), to_chunks(v), to_chunks(i_t), to_chunks(logf))
    (C1, n1, m1), h = lax.scan(step, (C0, n0, m0), xs)
    h = jnp.moveaxis(h, (0, 2), (1, 3)).reshape(Bn, L, H, d)
    return h, C1, n1, m1


def _cplx_combine(e1, e2):
    a1r, a1i, b1r, b1i = e1
    a2r, a2i, b2r, b2i = e2
    return (a2r * a1r - a2i * a1i, a2r * a1i + a2i * a1r,
            a2r * b1r - a2i * b1i + b2r, a2r * b1i + a2i * b1r + b2i)


def s5_scan(u, h0_re, h0_im, lam_re, lam_im, log_dt, B_re, B_im, C_re, C_im, d_skip):
    f32 = jnp.float32
    Bn, L, _ = u.shape
    ug = u.reshape(Bn, L, N_GROUPS_B, S5_GROUP)
    lr = jnp.minimum(lam_re.astype(f32), -1e-4)
    li = lam_im.astype(f32)
    dt = jnp.exp(log_dt.astype(f32))
    mag = jnp.exp(lr * dt)
    a_re = mag * jnp.cos(li * dt)
    a_im = mag * jnp.sin(li * dt)
    den = lr * lr + li * li
    xr = a_re - 1.0
    g_re = (xr * lr + a_im * li) / den
    g_im = (a_im * lr - xr * li) / den
    Br = B_re.astype(f32)
    Bi = B_im.astype(f32)
    Bb_re = g_re[..., None] * Br - g_im[..., None] * Bi
    Bb_im = g_re[..., None] * Bi + g_im[..., None] * Br
    bu_re = jnp.einsum('blgc,gpc->blgp', ug, Bb_re)
    bu_im = jnp.einsum('blgc,gpc->blgp', ug, Bb_im)
    h0r = h0_re.astype(f32)
    h0i = h0_im.astype(f32)
    bu_re = bu_re.at[:, 0].add(a_re * h0r - a_im * h0i)
    bu_im = bu_im.at[:, 0].add(a_re * h0i + a_im * h0r)
    ar = jnp.broadcast_to(a_re, bu_re.shape)
    ai = jnp.broadcast_to(a_im, bu_re.shape)
    _, _, xs_re, xs_im = lax.associative_scan(_cplx_combine, (ar, ai, bu_re, bu_im), axis=1)
    y = (jnp.einsum('blgp,gcp->blgc', xs_re, C_re.astype(f32))
         - jnp.einsum('blgp,gcp->blgc', xs_im, C_im.astype(f32)))
    y = y.reshape(Bn, L, MIX_B) + d_skip.astype(f32) * u
    return y, xs_re[:, -1], xs_im[:, -1]


def hybrid_layer(h, p_l, C0, n0, m0, conv0, sre0, sim0,
                 ln_mix, w_in, b_igate, b_fgate, conv_w, conv_b, w_q, w_k, w_v, ln_head, skip_a,
                 lam_re, lam_im, log_dt, B_re, B_im, C_re, C_im, s5_D, w_glu, b_glu,
                 w_out, w_ple, ln_ple, w_ple_gate):
    f32 = jnp.float32
    Bn, L, _ = h.shape
    a = rmsnorm(h, ln_mix)
    proj = a @ w_in
    x_m, z_m, o_m, i_pre, f_pre, x_s, z_s = jnp.split(proj, SPLIT_IDX, axis=-1)

    x_m = x_m.astype(f32)
    x_conv, conv_new = causal_conv(x_m, conv0.astype(f32), conv_w, conv_b)
    x_conv = jax.nn.silu(x_conv)
    xc_h = x_conv.reshape(Bn, L, N_HEADS_A, HEAD_DIM_A)
    xm_h = x_m.reshape(Bn, L, N_HEADS_A, HEAD_DIM_A)
    q = jnp.einsum('blhd,hde->blhe', xc_h, w_q.astype(f32))
    k = jnp.einsum('blhd,hde->blhe', xc_h, w_k.astype(f32)) * (HEAD_DIM_A ** -0.5)
    v = jnp.einsum('blhd,hde->blhe', xm_h, w_v.astype(f32))
    i_t = i_pre.astype(f32) + b_igate.astype(f32)
    logf = jax.nn.log_sigmoid(f_pre.astype(f32) + b_fgate.astype(f32))
    hA, C1, n1, m1 = mlstm_chunkwise(q, k, v, i_t, logf, C0.astype(f32), n0.astype(f32), m0.astype(f32))
    hA = hA * jax.nn.sigmoid(o_m.astype(f32)).reshape(Bn, L, N_HEADS_A, HEAD_DIM_A)
    mu = jnp.mean(hA, axis=-1, keepdims=True)
    var = jnp.mean(jnp.square(hA - mu), axis=-1, keepdims=True)
    hA = ((hA - mu) * lax.rsqrt(var + EPS)).reshape(Bn, L, MIX_A) * ln_head.astype(f32)
    hA = (hA + skip_a.astype(f32) * x_conv) * jax.nn.silu(z_m.astype(f32))

    yB, sre1, sim1 = s5_scan(x_s.astype(f32), sre0, sim0, lam_re, lam_im, log_dt,
                             B_re, B_im, C_re, C_im, s5_D)
    yB = jax.nn.gelu(yB)
    yB = yB * jax.nn.sigmoid(yB @ w_glu.astype(f32) + b_glu.astype(f32))
    yB = yB * jax.nn.silu(z_s.astype(f32))

    h = h + jnp.concatenate([hA, yB], axis=-1).astype(h.dtype) @ w_out
    e = rmsnorm(p_l @ w_ple, ln_ple)
    h = h + jax.nn.sigmoid(h @ w_ple_gate) * e
    return h, C1, n1, m1, conv_new, sre1, sim1


def setup_inputs(seed: int = 0) -> dict:
    key = jax.random.key(seed)
    ks = iter(jax.random.split(key, 48))
    f32 = jnp.float32

    def nrm(shape, s):
        return s * jax.random.normal(next(ks), shape, f32)

    H, dh, G, P = N_HEADS_A, HEAD_DIM_A, N_GROUPS_B, S5_STATE
    inp = {}
    inp['x_prompt'] = nrm((BATCH, SEQ, D_MODEL), 1.0)
    inp['x_sample'] = nrm((DEC_BATCH, DEC_SEQ, D_MODEL), 1.0)
    inp['p_prompt'] = nrm((DEPTH, BATCH, SEQ, D_PLE), 1.0)
    inp['p_sample'] = nrm((DEPTH, DEC_BATCH, DEC_SEQ, D_PLE), 1.0)
    inp['state_mlstm_C'] = nrm((DEPTH, DEC_BATCH, H, dh, dh), 0.1)
    inp['state_mlstm_n'] = nrm((DEPTH, DEC_BATCH, H, dh), 0.5)
    inp['state_mlstm_m'] = nrm((DEPTH, DEC_BATCH, H), 1.0)
    inp['state_conv'] = nrm((DEPTH, DEC_BATCH, CONV_W - 1, MIX_A), 1.0)
    inp['state_s5_re'] = nrm((DEPTH, DEC_BATCH, G, P), 0.3)
    inp['state_s5_im'] = nrm((DEPTH, DEC_BATCH, G, P), 0.3)
    inp['ln_mix'] = 1.0 + nrm((DEPTH, D_MODEL), 0.02)
    inp['w_in'] = nrm((DEPTH, D_MODEL, IN_COLS), D_MODEL ** -0.5)
    inp['b_igate'] = nrm((DEPTH, H), 0.1)
    inp['b_fgate'] = jnp.linspace(3.0, 6.0, H, dtype=f32)[None, :] + nrm((DEPTH, H), 0.1)
    inp['conv_w'] = nrm((DEPTH, CONV_W, MIX_A), CONV_W ** -0.5)
    inp['conv_b'] = nrm((DEPTH, MIX_A), 0.02)
    inp['w_q'] = nrm((DEPTH, H, dh, dh), dh ** -0.5)
    inp['w_k'] = nrm((DEPTH, H, dh, dh), dh ** -0.5)
    inp['w_v'] = nrm((DEPTH, H, dh, dh), dh ** -0.5)
    inp['ln_head'] = 1.0 + nrm((DEPTH, MIX_A), 0.02)
    inp['skip_a'] = 1.0 + nrm((DEPTH, MIX_A), 0.02)
    inp['s5_lam_re'] = -0.5 + nrm((DEPTH, G, P), 0.01)
    inp['s5_lam_im'] = math.pi * jnp.arange(P, dtype=f32)[None, None, :] + nrm((DEPTH, G, P), 0.01)
    lo, hi = math.log(0.001), math.log(0.1)
    inp['s5_log_dt'] = lo + (hi - lo) * jax.random.uniform(next(ks), (DEPTH, G, P), f32)
    inp['s5_B_re'] = nrm((DEPTH, G, P, S5_GROUP), (2 * S5_GROUP) ** -0.5)
    inp['s5_B_im'] = nrm((DEPTH, G, P, S5_GROUP), (2 * S5_GROUP) ** -0.5)
    inp['s5_C_re'] = nrm((DEPTH, G, S5_GROUP, P), 0.5)
    inp['s5_C_im'] = nrm((DEPTH, G, S5_GROUP, P), 0.5)
    inp['s5_D'] = nrm((DEPTH, MIX_B), 1.0)
    inp['w_glu'] = nrm((DEPTH, MIX_B, MIX_B), MIX_B ** -0.5)
    inp['b_glu'] = nrm((DEPTH, MIX_B), 0.02)
    inp['w_out'] = nrm((DEPTH, MIX_A + MIX_B, D_MODEL), (MIX_A + MIX_B) ** -0.5)
    inp['w_ple'] = nrm((DEPTH, D_PLE, D_MODEL), D_PLE ** -0.5)
    inp['ln_ple'] = 1.0 + nrm((DEPTH, D_MODEL), 0.02)
    inp['w_ple_gate'] = nrm((DEPTH, D_MODEL, D_MODEL), D_MODEL ** -0.5)
    inp['ln_final'] = 1.0 + nrm((D_MODEL,), 0.02)
    return inp


def reference(x_prompt, x_sample, p_prompt, p_sample, state_mlstm_C, state_mlstm_n, state_mlstm_m,
              state_conv, state_s5_re, state_s5_im, ln_mix, w_in, b_igate, b_fgate, conv_w, conv_b,
              w_q, w_k, w_v, ln_head, skip_a, s5_lam_re, s5_lam_im, s5_log_dt, s5_B_re, s5_B_im,
              s5_C_re, s5_C_im, s5_D, w_glu, b_glu, w_out, w_ple, ln_ple, w_ple_gate, ln_final):
    f32 = jnp.float32
    H, dh, G, P = N_HEADS_A, HEAD_DIM_A, N_GROUPS_B, S5_STATE
    hp, hs = x_prompt, x_sample
    pr = [[] for _ in range(6)]
    sa = [[] for _ in range(6)]
    for l in range(DEPTH):
        wl = (ln_mix[l], w_in[l], b_igate[l], b_fgate[l], conv_w[l], conv_b[l], w_q[l], w_k[l], w_v[l],
              ln_head[l], skip_a[l], s5_lam_re[l], s5_lam_im[l], s5_log_dt[l], s5_B_re[l], s5_B_im[l],
              s5_C_re[l], s5_C_im[l], s5_D[l], w_glu[l], b_glu[l], w_out[l], w_ple[l], ln_ple[l],
              w_ple_gate[l])
        Bp = hp.shape[0]
        outp = hybrid_layer(hp, p_prompt[l],
                            jnp.zeros((Bp, H, dh, dh), f32), jnp.zeros((Bp, H, dh), f32),
                            jnp.zeros((Bp, H), f32), jnp.zeros((Bp, CONV_W - 1, MIX_A), f32),
                            jnp.zeros((Bp, G, P), f32), jnp.zeros((Bp, G, P), f32), *wl)
        outs = hybrid_layer(hs, p_sample[l], state_mlstm_C[l], state_mlstm_n[l], state_mlstm_m[l],
                            state_conv[l], state_s5_re[l], state_s5_im[l], *wl)
        hp, hs = outp[0], outs[0]
        for j in range(6):
            pr[j].append(outp[j + 1])
            sa[j].append(outs[j + 1])
    y_prompt = rmsnorm(hp, ln_final)
    y_sample = rmsnorm(hs, ln_final)
    pC, pn, pm, pconv, pre, pim = [jnp.stack(t, axis=0) for t in pr]
    sC, sn, sm, sconv, sre, sim = [jnp.stack(t, axis=0) for t in sa]
    return (y_prompt, y_sample, pC, pn, pm, pconv, pre, pim, sC, sn, sm, sconv, sre, sim)
```

```python
import math
from contextlib import ExitStack
import numpy as np
import concourse.bass as bass
import concourse.mybir as mybir
from concourse.bass_utils import run_bass_kernel_spmd

F32 = mybir.dt.float32
BF16 = mybir.dt.bfloat16
I32 = mybir.dt.int32
AF = mybir.ActivationFunctionType
ALU = mybir.AluOpType
ENGS = ("sync", "scalar", "vector", "gpsimd", "tensor")

NCORES = 8
SEQ = 2048
NPT = 16
SB = 16
DM = 1024
INC = 2568
EPS = 1e-6
BIG = 1.0e30
SAME_ENGINE_SYNC = True
DBG_CUT = None
DBG_VAR = 0
DBG_STAGE = None


class Buf:
    __slots__ = ("name", "ws", "base", "readers")

    def __init__(self, name, like=None):
        self.name = name
        self.ws = []
        self.base = []
        self.readers = []
        if like is not None:
            for l in like:
                self.ws += l.ws
                self.base += l.base
                self.readers += l.readers


class Prog:
    def __init__(self, nc, n_dma_sems=24):
        self.nc = nc
        self.ops = []
        self.n_dma_sems = n_dma_sems
        self.final_dma = []
        self.es = ExitStack()

    def sbuf(self, name, shape, dtype=F32):
        return self.es.enter_context(self.nc.sbuf_tensor(name, list(shape), dtype))

    def psum(self, name, shape, dtype=F32):
        return self.es.enter_context(self.nc.psum_tensor(name, list(shape), dtype))

    def op(self, eng, fn, reads=(), writes=(), dma=False, final=False, join=False):
        i = len(self.ops)
        deps = set()
        for b in reads:
            deps.update(b.ws)
        for b in writes:
            if not join:
                deps.update(b.ws)
                deps.update(b.readers)
            else:
                if not b.base and (b.ws or b.readers):
                    b.base = list(b.ws) + list(b.readers)
                deps.update(b.base)
                deps.update(b.readers)
        deps.discard(i)
        self.ops.append(dict(eng=eng, fn=fn, deps=deps, dma=dma))
        for b in reads:
            b.readers.append(i)
        for b in writes:
            if join:
                b.ws = b.ws + [i]
            else:
                b.ws = [i]
                b.base = []
            b.readers = []
        if final:
            self.final_dma.append(i)
        return i

    def build(self):
        nc = self.nc
        ops = self.ops
        n = len(ops)
        needed = [False] * n
        for i, o in enumerate(ops):
            for d in o["deps"]:
                if ops[d]["eng"] != o["eng"] or SAME_ENGINE_SYNC or ops[d]["dma"]:
                    needed[d] = True
        with self.es as es:
            esem = {e: es.enter_context(nc.semaphore("s_" + e)) for e in ENGS}
            dsem = [es.enter_context(nc.semaphore("d%d" % k)) for k in range(self.n_dma_sems)]
            tok = [None] * n
            ecount = {e: 0 for e in ENGS}
            dcount = [0] * self.n_dma_sems
            dlast = [None] * self.n_dma_sems
            dk = 0
            prev_same_sem = {}
            for i, o in enumerate(ops):
                if o["dma"]:
                    k = dk % self.n_dma_sems
                    dk += 1
                    dcount[k] += 16
                    tok[i] = ("d", k, dcount[k])
                    if dlast[k] is not None:
                        prev_same_sem[i] = dlast[k]
                    dlast[k] = i
                elif needed[i]:
                    ecount[o["eng"]] += 1
                    tok[i] = ("e", o["eng"], ecount[o["eng"]])
                else:
                    tok[i] = ("e", o["eng"], ecount[o["eng"]] + 1)
            per_eng = {e: [] for e in ENGS}
            waited = {e: {} for e in ENGS}
            for i, o in enumerate(ops):
                e = o["eng"]
                deps = set(o["deps"])
                if i in prev_same_sem:
                    deps.add(prev_same_sem[i])
                waits = []
                for d in sorted(deps):
                    od = ops[d]
                    if (not od["dma"]) and od["eng"] == e and not SAME_ENGINE_SYNC:
                        continue
                    t = tok[d]
                    key = (t[0], t[1])
                    if waited[e].get(key, 0) >= t[2]:
                        continue
                    waited[e][key] = t[2]
                    sem = dsem[t[1]] if t[0] == "d" else esem[t[1]]
                    waits.append((sem, t[2]))
                sig = None
                if o["dma"]:
                    sig = (dsem[tok[i][1]], 16)
                elif needed[i]:
                    sig = (esem[e], 1)
                per_eng[e].append((waits, o["fn"], sig))
            fin = [(dsem[tok[i][1]], tok[i][2]) for i in self.final_dma]
            self.stats = {e: len(per_eng[e]) for e in ENGS}

            def run(engobj, lst, is_sync=False):
                for waits, fn, sig in lst:
                    for (s, v) in waits:
                        engobj.wait_ge(s, v)
                    ins = fn(engobj)
                    if sig is not None:
                        ins.then_inc(sig[0], sig[1])
                if is_sync:
                    done = {}
                    for (s, v) in fin:
                        engobj.wait_ge(s, v)

            with nc.Block() as block:
                @block.sync
                def _(e):
                    run(e, per_eng["sync"], True)

                @block.scalar
                def _(e):
                    run(e, per_eng["scalar"])

                @block.vector
                def _(e):
                    run(e, per_eng["vector"])

                @block.gpsimd
                def _(e):
                    run(e, per_eng["gpsimd"])

                @block.tensor
                def _(e):
                    run(e, per_eng["tensor"])
        return nc


class _Rec:
    def __getattr__(self, name):
        def f(*a, **k):
            return lambda e: getattr(e, name)(*a, **k)
        return f


E = _Rec()


class TL:
    def __init__(self, t, b):
        self.t = t
        self.b = b


def make_consts():
    c = {}
    c["ident"] = np.eye(128, dtype=np.float32)
    s = np.arange(128)[:, None]
    t = np.arange(128)[None, :]
    c["posmask_p"] = np.where(s <= t, 0.0, BIG).astype(np.float32)
    s6 = np.arange(64)[:, None]
    t6 = np.arange(64)[None, :]
    pm = np.where((s6 <= t6) & (s6 // 4 == t6 // 4), 0.0, BIG).astype(np.float32)
    c["posmask_s"] = pm
    sel = np.zeros((4, 4, 128), np.float32)
    for h in range(4):
        sel[h, h, :] = 1.0
    c["sel"] = sel.reshape(4, 512)
    sm = np.ones((128, 64), np.float32)
    sm[:, 0::4] = 0.0
    c["seqmask1"] = sm
    nb = np.zeros((4, 64), np.float32)
    nb[:, 0::4] = -BIG
    c["seqnegbig"] = nb
    oh = np.zeros((64, 16), np.float32)
    for s_ in range(64):
        oh[s_, s_ // 4] = 1.0
    c["onehot_s"] = oh
    return c


def build_program():
    nc = bass.Bass("TRN2", target_bir_lowering=False)
    p = Prog(nc)
    try:
        return _build_body(nc, p)
    except Exception as ex:
        if type(ex).__name__ != "_Cut":
            raise
        p.build()
        return nc, p


def _build_body(nc, p):
    din, dout = {}, {}

    def inp(name, shape):
        din[name] = nc.dram_tensor(name, list(shape), F32, kind="ExternalInput").ap()
        return din[name]

    def outp(name, shape):
        dout[name] = nc.dram_tensor(name, list(shape), F32, kind="ExternalOutput").ap()
        return dout[name]

    xp_d = inp("xp", [SEQ, DM]); xs_d = inp("xs", [64, DM])
    pp_d = inp("pp", [SEQ, 256]); psm_d = inp("psm", [64, 256])
    sC_d = inp("sC", [SB, 4, 128, 128]); sn_d = inp("sn", [SB, 4, 128]); sm_d = inp("sm", [SB, 4])
    sconv_d = inp("sconv", [SB, 3, 512]); sre_d = inp("sre", [SB, 32, 64]); sim_d = inp("sim", [SB, 32, 64])
    ln_mix_d = inp("ln_mix", [DM]); w_in_d = inp("w_in", [DM, INC])
    b_ig_d = inp("b_igate", [4]); b_fg_d = inp("b_fgate", [4])
    conv_w_d = inp("conv_w", [4, 512]); conv_b_d = inp("conv_b", [512])
    wq_d = inp("w_q", [4, 128, 128]); wk_d = inp("w_k", [4, 128, 128]); wv_d = inp("w_v", [4, 128, 128])
    ln_head_d = inp("ln_head", [512]); skip_d = inp("skip_a", [512])
    lamre_d = inp("s5_lam_re", [32, 64]); lamim_d = inp("s5_lam_im", [32, 64]); logdt_d = inp("s5_log_dt", [32, 64])
    Bre_d = inp("s5_B_re", [32, 64, 16]); Bim_d = inp("s5_B_im", [32, 64, 16])
    Cre_d = inp("s5_C_re", [32, 16, 64]); Cim_d = inp("s5_C_im", [32, 16, 64])
    s5D_d = inp("s5_D", [512]); wglu_d = inp("w_glu", [512, 512]); bglu_d = inp("b_glu", [512])
    wout_d = inp("w_out", [DM, DM]); wple_d = inp("w_ple", [256, DM]); lnple_d = inp("ln_ple", [DM])
    wgate_d = inp("w_ple_gate", [DM, DM]); lnfin_d = inp("ln_final", [DM])
    c_ident = inp("c_ident", [128, 128]); c_pmp = inp("c_posmask_p", [128, 128]); c_pms = inp("c_posmask_s", [64, 64])
    c_sel = inp("c_sel", [4, 512]); c_seqm = inp("c_seqmask1", [128, 64]); c_snb = inp("c_seqnegbig", [4, 64])
    c_oh = inp("c_onehot_s", [64, 16])

    yp_o = outp("yp", [SEQ, DM]); ys_o = outp("ys", [64, DM])
    pC_o = outp("pC", [4, 128, 128]); pn_o = outp("pn", [4, 128]); pm_o = outp("pm", [4])
    pconv_o = outp("pconv", [3, 512]); pre_o = outp("pre", [32, 64]); pim_o = outp("pim", [32, 64])
    oC_o = outp("oC", [SB, 4, 128, 128]); on_o = outp("on", [SB, 4, 128]); om_o = outp("om", [SB, 4])
    oconv_o = outp("oconv", [SB, 3, 512]); ore_o = outp("ore", [SB, 32, 64]); oim_o = outp("oim", [SB, 32, 64])

    def mk(name, shape, dtype=F32, nb=1):
        t = p.sbuf(name, shape, dtype)
        if nb == 1:
            return TL(t, Buf(name))
        return TL(t, [Buf("%s%d" % (name, i)) for i in range(nb)])

    def bl(x):
        out = []
        for a in x:
            if isinstance(a, TL):
                out += a.b if isinstance(a.b, list) else [a.b]
            elif isinstance(a, Buf):
                out.append(a)
            elif isinstance(a, (list, tuple)):
                out += bl(a)
            elif a is not None:
                raise TypeError(a)
        return out

    def V(fn, r, w, **kw): p.op("vector", fn, bl(r), bl(w), **kw)
    def A(fn, r, w, **kw): p.op("scalar", fn, bl(r), bl(w), **kw)
    def G(fn, r, w, **kw): p.op("gpsimd", fn, bl(r), bl(w), **kw)
    def T(fn, r, w, **kw): p.op("tensor", fn, bl(r), bl(w), **kw)
    def D(fn, r, w, **kw): p.op("sync", fn, bl(r), bl(w), dma=True, **kw)
    def D2(fn, r, w, **kw): p.op("scalar", fn, bl(r), bl(w), dma=True, **kw)

    NPS = 8
    psb = [TL(p.psum("ps%d" % i, [128, 512], F32), Buf("ps%d" % i)) for i in range(NPS)]
    psi = [0]

    def PS():
        x = psb[psi[0] % 4]
        psi[0] += 1
        return x

    class _Cut(Exception):
        pass

    def CUT(k, tl):
        if DBG_CUT == k:
            D(E.dma_start(out=yp_o[0:tl.t.shape[0], 0:16], in_=tl.t[:, 0:16]), [tl], [], final=True)
            raise _Cut()

    def dma_in(dst_ap, src_ap, wbuf, slow=False, join=False):
        if slow:
            D(E.dma_start(out=dst_ap, in_=src_ap, allow_slow_non_contiguous=True), [], [wbuf], join=join)
        else:
            D(E.dma_start(out=dst_ap, in_=src_ap), [], [wbuf], join=join)

    def dma_out(dst_ap, src_ap, rbufs, slow=False):
        if slow:
            D(E.dma_start(out=dst_ap, in_=src_ap, allow_slow_non_contiguous=True), rbufs, [], final=True)
        else:
            D(E.dma_start(out=dst_ap, in_=src_ap), rbufs, [], final=True)

    def col512(name, src_d):
        t = mk(name, [128, 4])
        dma_in(t.t[:, :], src_d.rearrange("(c p) -> p c", p=128), t.b, slow=True)
        return t

    stg = [mk("stg%d" % i, [128, INC]) for i in range(2)]
    ident = mk("ident", [128, 128]); dma_in(ident.t[:], c_ident[:, :], ident.b)
    pmp = mk("pmp", [128, 128]); dma_in(pmp.t[:], c_pmp[:, :], pmp.b)
    pms = mk("pms", [64, 64]); dma_in(pms.t[:], c_pms[:, :], pms.b)
    sel = mk("sel", [4, 512]); dma_in(sel.t[:], c_sel[:, :], sel.b)
    seqm = mk("seqm", [128, 64]); dma_in(seqm.t[:], c_seqm[:, :], seqm.b)
    snb = mk("snb", [4, 64]); dma_in(snb.t[:], c_snb[:, :], snb.b)
    oneh = mk("oneh", [64, 16]); dma_in(oneh.t[:], c_oh[:, :], oneh.b)
    ones4 = mk("ones4", [128, 128]); G(E.memset(ones4.t[:], 1.0), [], [ones4])

    convw = mk("convw", [128, 4, 4])
    for c in range(4):
        dma_in(convw.t[:, c, :], conv_w_d[:, c * 128:(c + 1) * 128].rearrange("j p -> p j"), convw.b, slow=True, join=True)
    convb = col512("convb", conv_b_d); lnh = col512("lnh", ln_head_d); skp = col512("skp", skip_d)
    s5D = col512("s5D", s5D_d); bglu = col512("bglu", bglu_d)
    gmix = mk("gmix", [128, 8]); dma_in(gmix.t[:, :], ln_mix_d.rearrange("(k p) -> p k", p=128), gmix.b, slow=True)
    big_ = mk("big_", [4, 1]); dma_in(big_.t[:, :], b_ig_d.rearrange("(h o) -> h o", o=1), big_.b, slow=True)
    bfg = mk("bfg", [4, 1]); dma_in(bfg.t[:, :], b_fg_d.rearrange("(h o) -> h o", o=1), bfg.b, slow=True)
    nbfg = mk("nbfg", [4, 1]); V(E.tensor_scalar(out=nbfg.t[:], in0=bfg.t[:], scalar1=-1.0, scalar2=None, op0=ALU.mult), [bfg], [nbfg])
    lnple = mk("lnple", [128, DM]); dma_in(lnple.t[:], lnple_d.rearrange("(o d) -> o d", o=1).partition_broadcast(128), lnple.b)
    lnfin = mk("lnfin", [128, DM]); dma_in(lnfin.t[:], lnfin_d.rearrange("(o d) -> o d", o=1).partition_broadcast(128), lnfin.b)
    epsc = mk("epsc", [128, 1]); G(E.memset(epsc.t[:], EPS), [], [epsc])

    wq = mk("wq", [128, 4, 128], BF16); wk = mk("wk", [128, 4, 128], BF16); wv = mk("wv", [128, 4, 128], BF16)


    def ld16(name, src):
        t = mk(name, [128, 16])
        dma_in(t.t[:, :], src.rearrange("(j g) q -> (g q) j", g=2), t.b, slow=True)
        return t
    lamre = ld16("lamre", lamre_d); lamim = ld16("lamim", lamim_d); logdt = ld16("logdt", logdt_d)
    s5tmp = [mk("s5tmp%d" % i, [128, 16]) for i in range(8)]
    lr = mk("lr", [128, 16]); dtt = mk("dtt", [128, 16]); mag = mk("mag", [128, 16])
    cs1 = mk("cs1", [128, 16]); sn1 = mk("sn1", [128, 16]); are = mk("are", [128, 16]); aim = mk("aim", [128, 16])
    gre = mk("gre", [128, 16]); gim = mk("gim", [128, 16])
    V(E.tensor_scalar(out=lr.t[:], in0=lamre.t[:], scalar1=-1e-4, scalar2=None, op0=ALU.min), [lamre], [lr])
    A(E.activation(out=dtt.t[:], in_=logdt.t[:], func=AF.Exp), [logdt], [dtt])
    t0, t1, t2, t3, t4, t5, t6, t7 = s5tmp
    V(E.tensor_tensor(out=t0.t[:], in0=lr.t[:], in1=dtt.t[:], op=ALU.mult), [lr, dtt], [t0])
    A(E.activation(out=mag.t[:], in_=t0.t[:], func=AF.Exp), [t0], [mag])
    th = mk("th", [128, 16])
    V(E.tensor_tensor(out=th.t[:], in0=lamim.t[:], in1=dtt.t[:], op=ALU.mult), [lamim, dtt], [th])
    ti32 = TL(p.sbuf("ti32", [128, 16], I32), Buf("ti32"))
    TWO_PI = 2.0 * math.pi

    def sin_reduced(dst, shift):
        V(E.tensor_scalar(out=t1.t[:], in0=th.t[:], scalar1=shift, scalar2=None, op0=ALU.add), [th], [t1])
        V(E.tensor_scalar(out=t2.t[:], in0=t1.t[:], scalar1=1.0 / TWO_PI, scalar2=0.5, op0=ALU.mult, op1=ALU.add), [t1], [t2])
        V(E.tensor_copy(out=ti32.t[:], in_=t2.t[:]), [t2], [ti32])
        V(E.tensor_copy(out=t3.t[:], in_=ti32.t[:]), [ti32], [t3])
        V(E.tensor_tensor(out=t4.t[:], in0=t3.t[:], in1=t2.t[:], op=ALU.is_gt), [t3, t2], [t4])
        V(E.tensor_tensor(out=t3.t[:], in0=t3.t[:], in1=t4.t[:], op=ALU.subtract), [t3, t4], [t3])
        V(E.scalar_tensor_tensor(out=t1.t[:], in0=t3.t[:], scalar=-TWO_PI, in1=t1.t[:], op0=ALU.mult, op1=ALU.add), [t3, t1], [t1])
        V(E.tensor_scalar(out=t1.t[:], in0=t1.t[:], scalar1=-math.pi, scalar2=math.pi, op0=ALU.max, op1=ALU.min), [t1], [t1])
        A(E.activation(out=dst.t[:], in_=t1.t[:], func=AF.Sin), [t1], [dst])
    sin_reduced(sn1, 0.0)
    sin_reduced(cs1, 0.5 * math.pi)
    V(E.tensor_tensor(out=are.t[:], in0=mag.t[:], in1=cs1.t[:], op=ALU.mult), [mag, cs1], [are])
    V(E.tensor_tensor(out=aim.t[:], in0=mag.t[:], in1=sn1.t[:], op=ALU.mult), [mag, sn1], [aim])
    V(E.tensor_tensor(out=t5.t[:], in0=lr.t[:], in1=lr.t[:], op=ALU.mult), [lr], [t5])
    V(E.tensor_tensor(out=t6.t[:], in0=lamim.t[:], in1=lamim.t[:], op=ALU.mult), [lamim], [t6])
    V(E.tensor_tensor(out=t5.t[:], in0=t5.t[:], in1=t6.t[:], op=ALU.add), [t5, t6], [t5])
    V(E.reciprocal(out=t5.t[:], in_=t5.t[:]), [t5], [t5])
    V(E.tensor_scalar(out=t6.t[:], in0=are.t[:], scalar1=-1.0, scalar2=None, op0=ALU.add), [are], [t6])
    V(E.tensor_tensor(out=t7.t[:], in0=t6.t[:], in1=lr.t[:], op=ALU.mult), [t6, lr], [t7])
    V(E.tensor_tensor(out=t0.t[:], in0=aim.t[:], in1=lamim.t[:], op=ALU.mult), [aim, lamim], [t0])
    V(E.tensor_tensor(out=t7.t[:], in0=t7.t[:], in1=t0.t[:], op=ALU.add), [t7, t0], [t7])
    V(E.tensor_tensor(out=gre.t[:], in0=t7.t[:], in1=t5.t[:], op=ALU.mult), [t7, t5], [gre])
    V(E.tensor_tensor(out=t7.t[:], in0=aim.t[:], in1=lr.t[:], op=ALU.mult), [aim, lr], [t7])
    V(E.tensor_tensor(out=t0.t[:], in0=t6.t[:], in1=lamim.t[:], op=ALU.mult), [t6, lamim], [t0])
    V(E.tensor_tensor(out=t7.t[:], in0=t7.t[:], in1=t0.t[:], op=ALU.subtract), [t7, t0], [t7])
    V(E.tensor_tensor(out=gim.t[:], in0=t7.t[:], in1=t5.t[:], op=ALU.mult), [t7, t5], [gim])

    CUT(2, gim)
    Ec = mk("Ec", [128, 16, 128]); Es = mk("Es", [128, 16, 128])
    pc = mk("pc", [128, 16]); psn = mk("psn", [128, 16])
    G(E.memset(Ec.t[:, :, 0:1], 1.0), [], [Ec])
    G(E.memset(Es.t[:, :, 0:1], 0.0), [], [Es])
    V(E.tensor_copy(out=pc.t[:], in_=cs1.t[:]), [cs1], [pc])
    V(E.tensor_copy(out=psn.t[:], in_=sn1.t[:]), [sn1], [psn])
    etmp = TL(stg[0].t[:, 0:1024].rearrange("p (j c) -> p j c", j=16), stg[0].b); etmp2 = TL(stg[1].t[:, 0:1024].rearrange("p (j c) -> p j c", j=16), stg[1].b)
    for k in range(7):
        L = 1 << k
        pcb = pc.t[:, :].unsqueeze(2).to_broadcast([128, 16, L])
        psb_ = psn.t[:, :].unsqueeze(2).to_broadcast([128, 16, L])
        V(E.tensor_tensor(out=etmp.t[:, :, 0:L], in0=Ec.t[:, :, 0:L], in1=pcb, op=ALU.mult), [Ec, pc], [etmp])
        V(E.tensor_tensor(out=etmp2.t[:, :, 0:L], in0=Es.t[:, :, 0:L], in1=psb_, op=ALU.mult), [Es, psn], [etmp2])
        V(E.tensor_tensor(out=Ec.t[:, :, L:2 * L], in0=etmp.t[:, :, 0:L], in1=etmp2.t[:, :, 0:L], op=ALU.subtract), [etmp, etmp2], [Ec])
        V(E.tensor_tensor(out=etmp.t[:, :, 0:L], in0=Ec.t[:, :, 0:L], in1=psb_, op=ALU.mult), [Ec, psn], [etmp])
        V(E.tensor_tensor(out=etmp2.t[:, :, 0:L], in0=Es.t[:, :, 0:L], in1=pcb, op=ALU.mult), [Es, pc], [etmp2])
        V(E.tensor_tensor(out=Es.t[:, :, L:2 * L], in0=etmp.t[:, :, 0:L], in1=etmp2.t[:, :, 0:L], op=ALU.add), [etmp, etmp2], [Es])
        if k < 6:
            V(E.tensor_tensor(out=t0.t[:], in0=pc.t[:], in1=pc.t[:], op=ALU.mult), [pc], [t0])
            V(E.tensor_tensor(out=t1.t[:], in0=psn.t[:], in1=psn.t[:], op=ALU.mult), [psn], [t1])
            V(E.tensor_tensor(out=t2.t[:], in0=pc.t[:], in1=psn.t[:], op=ALU.mult), [pc, psn], [t2])
            V(E.tensor_tensor(out=pc.t[:], in0=t0.t[:], in1=t1.t[:], op=ALU.subtract), [t0, t1], [pc])
            V(E.tensor_scalar(out=psn.t[:], in0=t2.t[:], scalar1=2.0, scalar2=None, op0=ALU.mult), [t2], [psn])

    CUT(3, TL(Es.t[:, 3, :], Es.b))
    BTre = mk("BTre", [128, 16, 128], BF16); BTim = mk("BTim", [128, 16, 128], BF16)
    CTre = mk("CTre", [128, 16, 128], BF16); CTimn = mk("CTimn", [128, 16, 128], BF16)
    padA = TL(stg[0].t[:, 0:2048].rearrange("p (j c) -> p j c", j=16), stg[0].b)
    padB = TL(stg[1].t[:, 0:2048].rearrange("p (j c) -> p j c", j=16), stg[1].b)
    ytile = mk("ytile", [128, 1088])
    btmp = TL(ytile.t[:, 0:128], ytile.b); btmp2 = TL(ytile.t[:, 128:256], Buf("btmp2"))
    G(E.memset(stg[0].t[:, 0:2048], 0.0), [], [stg[0]])
    G(E.memset(stg[1].t[:, 0:2048], 0.0), [], [stg[1]])
    Brv = Bre_d.rearrange("(m r) q c -> r q m c", r=8)
    Biv = Bim_d.rearrange("(m r) q c -> r q m c", r=8)
    for q in range(4):
        for g2 in range(2):
            r = 2 * q + g2
            off = 32 * q + 16 * g2
            dma_in(padA.t[g2 * 64:(g2 + 1) * 64, q::4, off:off + 16], Brv[r], stg[0].b, slow=True, join=True)
            dma_in(padB.t[g2 * 64:(g2 + 1) * 64, q::4, off:off + 16], Biv[r], stg[1].b, slow=True, join=True)
    for j in range(16):
        V(E.tensor_scalar(out=btmp.t[:], in0=padB.t[:, j, :], scalar1=gim.t[:, j:j + 1], scalar2=None, op0=ALU.mult), [stg[1], gim], [btmp])
        V(E.scalar_tensor_tensor(out=btmp.t[:], in0=padA.t[:, j, :], scalar=gre.t[:, j:j + 1], in1=btmp.t[:], op0=ALU.mult, op1=ALU.subtract), [stg[0], gre, btmp], [btmp])
        ps = PS()
        T(E.transpose(out=ps.t[:, 0:128], in_=btmp.t[:], identity=ident.t[:]), [btmp, ident], [ps])
        A(E.activation(out=BTre.t[:, j, :], in_=ps.t[:, 0:128], func=AF.Copy), [ps], [BTre])
        V(E.tensor_scalar(out=btmp2.t[:], in0=padA.t[:, j, :], scalar1=gim.t[:, j:j + 1], scalar2=None, op0=ALU.mult), [stg[0], gim], [btmp2])
        V(E.scalar_tensor_tensor(out=btmp2.t[:], in0=padB.t[:, j, :], scalar=gre.t[:, j:j + 1], in1=btmp2.t[:], op0=ALU.mult, op1=ALU.add), [stg[1], gre, btmp2], [btmp2])
        ps = PS()
        T(E.transpose(out=ps.t[:, 0:128], in_=btmp2.t[:], identity=ident.t[:]), [btmp2, ident], [ps])
        A(E.activation(out=BTim.t[:, j, :], in_=ps.t[:, 0:128], func=AF.Copy), [ps], [BTim])
    CUT(4, btmp2)
    G(E.memset(stg[0].t[:, 0:2048], 0.0), [stg[0]], [stg[0]])
    G(E.memset(stg[1].t[:, 0:2048], 0.0), [stg[1]], [stg[1]])
    Crv = Cre_d.rearrange("(m r) c q -> r c m q", r=8)
    Civ = Cim_d.rearrange("(m r) c q -> r c m q", r=8)
    for q in range(4):
        for g2 in range(2):
            r = 2 * q + g2
            off = 32 * q + 16 * g2
            dma_in(padA.t[off:off + 16, q::4, g2 * 64:(g2 + 1) * 64], Crv[r], stg[0].b, slow=True, join=True)
            dma_in(padB.t[off:off + 16, q::4, g2 * 64:(g2 + 1) * 64], Civ[r], stg[1].b, slow=True, join=True)
    for j in range(16):
        ps = PS()
        T(E.transpose(out=ps.t[:, 0:128], in_=padA.t[:, j, :], identity=ident.t[:]), [stg[0], ident], [ps])
        A(E.activation(out=CTre.t[:, j, :], in_=ps.t[:, 0:128], func=AF.Copy), [ps], [CTre])
        ps = PS()
        T(E.transpose(out=ps.t[:, 0:128], in_=padB.t[:, j, :], identity=ident.t[:]), [stg[1], ident], [ps])
        A(E.activation(out=CTimn.t[:, j, :], in_=ps.t[:, 0:128], func=AF.Copy, scale=-1.0), [ps], [CTimn])

    CUT(5, TL(stg[1].t[:, 0:16], stg[1].b))
    win = mk("win", [128, 8, INC], BF16)
    wout = mk("wout", [128, 8, DM], BF16); wgate = mk("wgate", [128, 8, DM], BF16)
    wple = mk("wple", [128, 2, DM], BF16); wglu = mk("wglu", [128, 4, 512], BF16)
    si = [0]

    def load_cast(dst_ap, src_ap, ncols, dstb, scale_ap=None, scale_b=None):
        s = stg[si[0] % 2]
        si[0] += 1
        dma_in(s.t[:, 0:ncols], src_ap, s.b)
        if scale_ap is not None:
            A(E.activation(out=dst_ap, in_=s.t[:, 0:ncols], func=AF.Copy, scale=scale_ap), [s, scale_b], [dstb])
        elif si[0] % 2 == 0:
            A(E.activation(out=dst_ap, in_=s.t[:, 0:ncols], func=AF.Copy), [s], [dstb])
        else:
            G(E.tensor_copy(out=dst_ap, in_=s.t[:, 0:ncols]), [s], [dstb])
    for k in range(8):
        load_cast(win.t[:, k, :], w_in_d[k * 128:(k + 1) * 128, :], INC, win.b, gmix.t[:, k:k + 1], gmix)
    for k in range(8):
        load_cast(wout.t[:, k, :], wout_d[k * 128:(k + 1) * 128, :], DM, wout.b)
        load_cast(wgate.t[:, k, :], wgate_d[k * 128:(k + 1) * 128, :], DM, wgate.b)
    for k in range(2):
        load_cast(wple.t[:, k, :], wple_d[k * 128:(k + 1) * 128, :], DM, wple.b)
    for k in range(4):
        load_cast(wglu.t[:, k, :], wglu_d[k * 128:(k + 1) * 128, :], 512, wglu.b)

    CUT(6, TL(stg[1].t[:, 0:16], stg[1].b))
    for wt_, src_, sc_ in ((wq, wq_d, 1.0), (wk, wk_d, float(128 ** -0.5)), (wv, wv_d, 1.0)):
        s_ = stg[si[0] % 2]
        si[0] += 1
        dma_in(s_.t[:, 0:512].rearrange("p (h e) -> p h e", h=4), src_.rearrange("h d e -> d h e"), s_.b)
        V(E.tensor_scalar(out=wt_.t[:, :, :], in0=s_.t[:, 0:512].rearrange("p (h e) -> p h e", h=4), scalar1=sc_, scalar2=None, op0=ALU.mult), [s_], [wt_])
    Caug = [mk("Caug%d" % h, [128, 129]) for h in range(4)]
    for h in range(4):
        V(E.memset(Caug[h].t[:], 0.0), [], [Caug[h]])
    Bn_c = mk("Bn_c", [4, 1]); M_c = mk("M_c", [4, 1])
    V(E.memset(Bn_c.t[:], 0.0), [], [Bn_c])
    V(E.memset(M_c.t[:], 0.0), [], [M_c])
    azin = mk("azin", [128, 16, 2])
    V(E.memset(azin.t[:], 0.0), [], [azin])
    xl_re = mk("xl_re", [128, 16]); xl_im = mk("xl_im", [128, 16])
    xmh = mk("xmh", [128, 4, 131])
    V(E.memset(xmh.t[:], 0.0), [], [xmh])

    def alias(name, ap, like):
        return TL(ap, Buf(name, like=like))
    xtok = alias("xtok", stg[0].t[:, 0:1024], [stg[0].b])
    h2 = alias("h2", stg[0].t[:, 1024:2048], [stg[0].b])
    sgate = alias("sgate", stg[1].t[:, 0:1024], [stg[1].b])
    esb = alias("esb", stg[1].t[:, 1024:2048], [stg[1].b])
    scr = ytile
    ptok = alias("ptok", stg[0].t[:, 2048:2304], [stg[0].b])
    aT = mk("aT", [128, 8, 128], BF16); pT = mk("pT", [128, 2, 128], BF16)
    szm = mk("szm", [128, 4, 128]); so = mk("so", [128, 512])
    tm = TL(so.t, so.b)
    xs5 = mk("xs5", [128, 4, 128]); xs5b = mk("xs5b", [128, 4, 128], BF16); szs = mk("szs", [128, 4, 128])
    xc = mk("xc", [128, 4, 128])
    xms = mk("xms", [128, 4, 16, 7]); xmc = TL(xmh.t[:, :, 0:64], xmh.b)
    qT = mk("qT", [128, 2, 128], BF16, nb=2); qTd = mk("qTd", [128, 2, 128], BF16, nb=2); kT = mk("kT", [128, 2, 128], BF16, nb=2)
    kw = mk("kw", [128, 2, 128], BF16, nb=2); vaug = mk("vaug", [128, 2, 130], BF16, nb=2)
    xcb = mk("xcb", [128, 4, 128], BF16); xmb_ = mk("xmb_", [128, 4, 128], BF16)
    sdb = [mk("sdb%d" % i, [128, 128], BF16) for i in range(2)]
    Cb = [mk("Cb%d" % h, [128, 130], BF16) for h in range(4)]
    for h in range(4):
        V(E.memset(Cb[h].t[:], 0.0), [], [Cb[h]])
    V(E.memset(vaug.t[:], 1.0), [], [vaug])
    narg = [alias("narg%d" % i, stg[0].t[:, 2304 + i * 128:2432 + i * 128], [stg[0].b]) for i in range(2)]
    DTt = narg
    SDt = sdb
    hA = mk("hA", [128, 2, 128], nb=2); hn = [alias("hn%d" % i, stg[1].t[:, 2048 + i * 128:2176 + i * 128], [stg[1].b]) for i in range(2)]
    mtmp = [alias("mtmp%d" % i, stg[1].t[:, 2304 + i * 128:2432 + i * 128], [stg[1].b]) for i in range(2)]
    mix = mk("mix", [128, 8, 128], BF16, nb=8)
    bnst = mk("bnst", [128, 4, 6]); mv = mk("mv", [128, 4, 2]); rstdh = mk("rstdh", [128, 4])
    den = mk("den", [128, 4]); rden = mk("rden", [128, 4]); dl = mk("dl", [128, 4])
    colq = mk("colq", [128, 12])
    ssq = mk("ssq", [128, 4]); rstd = mk("rstd", [128, 4])
    gi = mk("gi", [4, 128]); gsp = mk("gsp", [4, 128]); gBn = mk("gBn", [4, 128]); ga = gi
    gM = mk("gM", [4, 128]); gdec = mk("gdec", [4, 128]); gwl = mk("gwl", [4, 128]); gem = mk("gem", [4, 128])
    gtmp = mk("gtmp", [4, 128]); ga2 = gtmp; gneg = mk("gneg", [4, 1]); mnew = mk("mnew", [4, 16])
    m0c = mk("m0c", [4, 16]); m0row = mk("m0row", [4, 64])
    NS5 = 2
    s5P = [mk("s5P%d" % r, [128, 512]) for r in range(NS5)]
    s5W = [TL(hA.t[:, :, :].rearrange("p a b -> p (a b)"), hA.b)] + [mk("s5W%d" % r, [128, 256]) for r in range(1, NS5)]
    s5Z = [mk("s5Z%d" % r, [128, 256]) for r in range(NS5)]
    s5xb = [mk("s5xb%d" % r, [128, 256], BF16) for r in range(NS5)]
    rmat = [mk("rmat%d" % r, [128, 256]) for r in range(NS5)]
    ysb = mk("ysb", [128, 4, 128], nb=4); yg = ysb; ygb = mk("ygb", [128, 4, 128], BF16)
    gl1 = [mk("gl1_%d" % i, [128, 128]) for i in range(2)]; gl2 = gl1
    h2T = mk("h2T", [128, 8, 128], BF16)
    qz = ytile
    kwm = mk("kwm", [64, 8, 128], BF16); wlm = mk("wlm", [64, 16])
    Cst = Caug
    h0tmp = TL(ytile.t[:, 0:256].rearrange("p (j b) -> p j b", j=16), ytile.b)
    ah0j = [mk("ah0j", [128, 2, 16])] * NS5
    naim = mk("naim", [128, 16])
    sore = mk("sore", [128, 16, 16]); soim = mk("soim", [128, 16, 16])
    nT = mk("nT", [128, 64]); nOut = nT
    cnt = {"n": 0}
    CUT(7, TL(xmh.t[:, 0, :], xmh.b))

    def tile_src(kind, ti):
        if kind == "s":
            return xs_d[:, :], psm_d[:, :], 64
        return xp_d[ti * 128:(ti + 1) * 128, :], pp_d[ti * 128:(ti + 1) * 128, :], 128

    def load_inputs(kind, ti):
        xs_, ps_, n_ = tile_src(kind, ti)
        dma_in(xtok.t[0:n_, :], xs_, xtok.b)
        dma_in(ptok.t[0:n_, :], ps_, ptok.b)

    def front(kind, ti, nxt=None):
        smp = kind == "s"
        NT = 64 if smp else 128
        A(E.activation(out=scr.t[0:NT, 0:DM], in_=xtok.t[0:NT, :], func=AF.Square, accum_out=ssq.t[0:NT, 0:1]), [xtok], [scr, ssq])
        A(E.activation(out=rstd.t[0:NT, 0:1], in_=ssq.t[0:NT, 0:1], func=AF.Sqrt, bias=epsc.t[0:NT, :], scale=1.0 / DM), [ssq, epsc], [rstd])
        V(E.reciprocal(out=rstd.t[0:NT, 0:1], in_=rstd.t[0:NT, 0:1]), [rstd], [rstd])
        V(E.tensor_scalar(out=scr.t[0:NT, 0:DM], in0=xtok.t[0:NT, :], scalar1=rstd.t[0:NT, 0:1], scalar2=None, op0=ALU.mult), [xtok, rstd], [scr])
        for half in range(2):
            ps = PS()
            for kk in range(4):
                k = half * 4 + kk
                T(E.transpose(out=ps.t[:, kk * 128:kk * 128 + NT], in_=scr.t[0:NT, k * 128:(k + 1) * 128], identity=ident.t[0:NT, 0:NT]), [scr, ident], [ps])
            A(E.activation(out=aT.t[:, half * 4:half * 4 + 4, 0:NT], in_=ps.t[:, :].rearrange("p (k t) -> p k t", k=4)[:, :, 0:NT], func=AF.Copy), [ps], [aT])
        ps = PS()
        for k in range(2):
            T(E.transpose(out=ps.t[:, k * 128:k * 128 + NT], in_=ptok.t[0:NT, k * 128:(k + 1) * 128], identity=ident.t[0:NT, 0:NT]), [ptok, ident], [ps])
        V(E.tensor_copy(out=pT.t[:, :, 0:NT], in_=ps.t[:, 0:256].rearrange("p (k t) -> p k t", k=2)[:, :, 0:NT]), [ps], [pT])

        if nxt is not None:
            load_inputs(*nxt)

    def do_tile(kind, ti, carry=(), nxt=None, glu_prev=None):
        smp = kind == "s"
        NT = 64 if smp else 128
        x_src = xs_d[:, :] if smp else xp_d[ti * 128:(ti + 1) * 128, :]
        p_src = psm_d[:, :] if smp else pp_d[ti * 128:(ti + 1) * 128, :]
        y_dst = ys_o[:, :] if smp else yp_o[ti * 128:(ti + 1) * 128, :]
        psM = psb[5]; psD = psb[6]
        def proj_fm(col0):
            ps = PS()
            for k in range(8):
                T(E.matmul(ps.t[:, 0:NT], win.t[:, k, col0:col0 + 128], aT.t[:, k, 0:NT], start=(k == 0), stop=(k == 7)), [win, aT], [ps])
            return ps

        def g_block(col0, kind):
            ps = PS()
            for k in range(8):
                T(E.matmul(ps.t[0:NT, :], aT.t[:, k, 0:NT], win.t[:, k, col0:col0 + 512], start=(k == 0), stop=(k == 7)), [win, aT], [ps])
                yield
            if kind == "om":
                A(E.activation(out=so.t[0:NT, :], in_=ps.t[0:NT, :], func=AF.Sigmoid), [ps], [so])
                return
            A(E.activation(out=tm.t[0:NT, :], in_=ps.t[0:NT, :], func=AF.Copy), [ps], [tm])
            yield
            pt = PS()
            for c in range(4):
                T(E.transpose(out=pt.t[:, c * 128:c * 128 + NT], in_=tm.t[0:NT, c * 128:(c + 1) * 128], identity=ident.t[0:NT, 0:NT]), [tm, ident], [pt])
            ptv = pt.t[:, :].rearrange("p (c t) -> p c t", c=4)[:, :, 0:NT]
            if kind == "xm":
                if smp:
                    A(E.activation(out=xmc.t[:, :, 0:NT], in_=ptv, func=AF.Copy), [pt], [xmc])
                    G(E.tensor_copy(out=xmb_.t[:, :, 0:NT], in_=xmc.t[:, :, 0:NT]), [xmc], [xmb_])
                else:
                    A(E.activation(out=xmh.t[:, :, 3:3 + NT], in_=ptv, func=AF.Copy), [pt], [xmh])
                    G(E.tensor_copy(out=xmb_.t[:, :, 0:NT], in_=xmh.t[:, :, 3:3 + NT]), [xmh], [xmb_])
            elif kind == "zm":
                A(E.activation(out=szm.t[:, :, 0:NT], in_=ptv, func=AF.Silu), [pt], [szm])
            elif kind == "xs":
                A(E.activation(out=xs5.t[:, :, 0:NT], in_=ptv, func=AF.Copy), [pt], [xs5])
                V(E.tensor_copy(out=xs5b.t[:, :, 0:NT], in_=xs5.t[:, :, 0:NT]), [xs5], [xs5b])
            elif kind == "zs":
                A(E.activation(out=szs.t[:, :, 0:NT], in_=ptv, func=AF.Silu), [pt], [szs])

        def g_gates():
            psg = PS()
            for k in range(8):
                T(E.matmul(psg.t[0:4, 0:NT], win.t[:, k, 1536:1540], aT.t[:, k, 0:NT], start=(k == 0), stop=(k == 7)), [win, aT], [psg])
            for k in range(8):
                T(E.matmul(psg.t[0:4, 128:128 + NT], win.t[:, k, 1540:1544], aT.t[:, k, 0:NT], start=(k == 0), stop=(k == 7)), [win, aT], [psg])
            A(E.activation(out=gi.t[:, 0:NT], in_=psg.t[0:4, 0:NT], func=AF.Identity, bias=big_.t[:, :], scale=1.0), [psg, big_], [gi])
            A(E.activation(out=gsp.t[:, 0:NT], in_=psg.t[0:4, 128:128 + NT], func=AF.Exp, bias=nbfg.t[:, :], scale=-1.0), [psg, nbfg], [gsp])
        def g_eproj():
            pse = [PS(), PS()]
            for half in range(2):
                for k in range(2):
                    T(E.matmul(pse[half].t[0:NT, :], pT.t[:, k, 0:NT], wple.t[:, k, half * 512:(half + 1) * 512], start=(k == 0), stop=(k == 1)), [pT, wple], [pse[half]])
                A(E.activation(out=esb.t[0:NT, half * 512:(half + 1) * 512], in_=pse[half].t[0:NT, :], func=AF.Square, accum_out=ssq.t[0:NT, 1 + half:2 + half]), [pse[half]], [esb, ssq])
            V(E.tensor_tensor(out=ssq.t[0:NT, 1:2], in0=ssq.t[0:NT, 1:2], in1=ssq.t[0:NT, 2:3], op=ALU.add), [ssq], [ssq])
            A(E.activation(out=rstd.t[0:NT, 1:2], in_=ssq.t[0:NT, 1:2], func=AF.Sqrt, bias=epsc.t[0:NT, :], scale=1.0 / DM), [ssq, epsc], [rstd])
            V(E.reciprocal(out=rstd.t[0:NT, 1:2], in_=rstd.t[0:NT, 1:2]), [rstd], [rstd])
            for half in range(2):
                sl = slice(half * 512, (half + 1) * 512)
                V(E.scalar_tensor_tensor(out=esb.t[0:NT, sl], in0=pse[half].t[0:NT, :], scalar=rstd.t[0:NT, 1:2], in1=lnple.t[0:NT, sl], op0=ALU.mult, op1=ALU.mult), [pse[half], rstd, lnple], [esb])

        for _ in g_block(1544, "xs"):
            pass
        own = [(lambda: g_block(0, "xm"), None), (g_gates, None), (lambda: g_block(512, "zm"), None),
               (lambda: g_block(2056, "zs"), None), (lambda: g_block(1024, "om"), None)]
        pending = own
        cin = list(carry) + [g_eproj]
        if nxt is not None:
            cin.append(lambda: front(*nxt))

        def pop_carry(n=1):
            for _ in range(n):
                if cin:
                    cin.pop(0)()

        def pop_pending(background=True):
            if pending:
                f, a = pending.pop(0)
                r_ = f() if a is None else f(a)
                if r_ is not None:
                    if background:
                        bg.append(r_)
                    else:
                        for _ in r_:
                            pass

        def gates_gen():
            A(E.activation(out=gsp.t[:, 0:NT], in_=gsp.t[:, 0:NT], func=AF.Ln, bias=1.0), [gsp], [gsp])
            if smp:
                V(E.tensor_tensor_scan(out=gBn.t[:, 0:NT], data0=seqm.t[0:4, :], data1=gsp.t[:, 0:NT], initial=0.0, op0=ALU.mult, op1=ALU.add), [seqm, gsp], [gBn])
                yield
            else:
                V(E.tensor_tensor_scan(out=gBn.t[:, 0:NT], data0=ones4.t[0:4, 0:NT], data1=gsp.t[:, 0:NT], initial=Bn_c.t[:, :], op0=ALU.mult, op1=ALU.add), [ones4, gsp, Bn_c], [gBn])
                yield
            V(E.tensor_tensor(out=ga.t[:, 0:NT], in0=gi.t[:, 0:NT], in1=gBn.t[:, 0:NT], op=ALU.add), [gi, gBn], [ga])
            yield
            if smp:
                V(E.tensor_copy(out=ga2.t[:, 0:NT], in_=ga.t[:, 0:NT]), [ga], [ga2])
                yield
                a2v = ga2.t[:, 0:64].rearrange("h (b t) -> h b t", t=4)
                V(E.tensor_tensor(out=a2v[:, :, 0:1], in0=a2v[:, :, 0:1], in1=m0c.t[:, :].unsqueeze(2), op=ALU.max), [ga2, m0c], [ga2])
                yield
                V(E.tensor_tensor_scan(out=gM.t[:, 0:NT], data0=snb.t[:, :], data1=ga2.t[:, 0:NT], initial=0.0, op0=ALU.add, op1=ALU.max), [snb, ga2], [gM])
                yield
                V(E.tensor_tensor(out=gtmp.t[:, 0:NT], in0=m0row.t[:, :], in1=gM.t[:, 0:NT], op=ALU.subtract), [m0row, gM], [gtmp])
                yield
                A(E.activation(out=gdec.t[:, 0:NT], in_=gtmp.t[:, 0:NT], func=AF.Exp), [gtmp], [gdec])
                yield
                Mv = gM.t[:, 0:64].rearrange("h (b t) -> h b t", t=4)
                V(E.tensor_tensor(out=gtmp.t[:, 0:64].rearrange("h (b t) -> h b t", t=4), in0=ga.t[:, 0:64].rearrange("h (b t) -> h b t", t=4),
                                            in1=Mv[:, :, 3:4].to_broadcast([4, 16, 4]), op=ALU.subtract), [ga, gM], [gtmp])
                A(E.activation(out=gwl.t[:, 0:NT], in_=gtmp.t[:, 0:NT], func=AF.Exp), [gtmp], [gwl])
                yield
                V(E.tensor_tensor(out=mnew.t[:, :].unsqueeze(2), in0=Mv[:, :, 3:4], in1=gBn.t[:, 0:64].rearrange("h (b t) -> h b t", t=4)[:, :, 3:4], op=ALU.subtract), [gM, gBn], [mnew])
                yield
            else:
                V(E.tensor_tensor_scan(out=gM.t[:, 0:NT], data0=ones4.t[0:4, 0:NT], data1=ga.t[:, 0:NT], initial=M_c.t[:, :], op0=ALU.mult, op1=ALU.max), [ones4, ga, M_c], [gM])
                yield
                A(E.activation(out=gdec.t[:, 0:NT], in_=gM.t[:, 0:NT], func=AF.Exp, bias=M_c.t[:, :], scale=-1.0), [gM, M_c], [gdec])
                yield
                V(E.tensor_scalar(out=gneg.t[:, :], in0=gM.t[:, NT - 1:NT], scalar1=-1.0, scalar2=None, op0=ALU.mult), [gM], [gneg])
                yield
                A(E.activation(out=gwl.t[:, 0:NT], in_=ga.t[:, 0:NT], func=AF.Exp, bias=gneg.t[:, :], scale=1.0), [ga, gneg], [gwl])
                yield
                V(E.tensor_tensor(out=mnew.t[:, 0:1], in0=gM.t[:, NT - 1:NT], in1=gBn.t[:, NT - 1:NT], op=ALU.subtract), [gM, gBn], [mnew])
                yield
            V(E.tensor_tensor(out=gtmp.t[:, 0:NT], in0=gBn.t[:, 0:NT], in1=gM.t[:, 0:NT], op=ALU.subtract), [gBn, gM, gwl], [gtmp])
            yield
            A(E.activation(out=gem.t[:, 0:NT], in_=gtmp.t[:, 0:NT], func=AF.Exp), [gtmp], [gem])
            yield
            if not smp:
                V(E.tensor_copy(out=Bn_c.t[:, :], in_=gBn.t[:, NT - 1:NT]), [gBn], [Bn_c])
                yield
                V(E.tensor_copy(out=M_c.t[:, :], in_=gM.t[:, NT - 1:NT]), [gM], [M_c])
                yield
            psc = PS()
            for qi, gt in enumerate((ga, gwl, gem)):
                T(E.transpose(out=psc.t[0:NT, qi * 4:qi * 4 + 4], in_=gt.t[:, 0:NT], identity=ident.t[0:4, 0:4]), [gt, ident], [psc])
                yield
            V(E.tensor_copy(out=colq.t[0:NT, :], in_=psc.t[0:NT, 0:12]), [psc], [colq])
            yield
            for h in range(4):
                T(E.matmul(psM.t[:, h * 128:h * 128 + NT], sel.t[:, h * 128:(h + 1) * 128], gM.t[:, 0:NT], start=True, stop=True), [sel, gM], [psM])
                yield
            for h in range(4):
                T(E.matmul(psD.t[:, h * 128:h * 128 + NT], sel.t[:, h * 128:(h + 1) * 128], gdec.t[:, 0:NT], start=True, stop=True), [sel, gdec], [psD])
                yield


        def conv_gen():
            for c in range(4):
                if smp:
                    G(E.tensor_copy(out=xms.t[:, c, :, 3:7], in_=xmc.t[:, c, :].rearrange("p (b t) -> p b t", t=4)), [xmc], [xms])
                    yield
                    src = lambda j, c=c: xms.t[:, c, :, j:j + 4]
                    dstv = xc.t[:, c, 0:64].rearrange("p (b t) -> p b t", t=4)
                    rb = [xms]
                else:
                    src = lambda j, c=c: xmh.t[:, c, j:j + NT]
                    dstv = xc.t[:, c, 0:NT]
                    rb = [xmh]
                V(E.tensor_scalar(out=dstv, in0=src(0), scalar1=convw.t[:, c, 0:1], scalar2=convb.t[:, c:c + 1], op0=ALU.mult, op1=ALU.add), rb + [convw, convb], [xc])
                yield
                for j in range(1, 4):
                    V(E.scalar_tensor_tensor(out=dstv, in0=src(j), scalar=convw.t[:, c, j:j + 1], in1=dstv, op0=ALU.mult, op1=ALU.add), rb + [convw, xc], [xc])
                    yield
                A(E.activation(out=xc.t[:, c, 0:NT], in_=xc.t[:, c, 0:NT], func=AF.Silu), [xc], [xc])
                yield
                G(E.tensor_copy(out=xcb.t[:, c, 0:NT], in_=xc.t[:, c, 0:NT]), [xc], [xcb])
                yield
            if smp:
                G(E.memset(qz.t[:, :], 0.0), [], [qz])
                yield

        def pr(ap, c):
            if smp:
                return ap.rearrange("p (c b t) -> p c b t", c=c, t=4)
            return ap.rearrange("p (c t) -> p c t", c=c)

        def tb(tab, j):
            if smp:
                return tab.t[:, j, 0:4].unsqueeze(1).unsqueeze(1).to_broadcast([128, 2, 16, 4])
            return tab.t[:, j, 0:NT].unsqueeze(1).to_broadcast([128, 2, NT])
        N2, N3, N4 = 2 * NT, 3 * NT, 4 * NT
        def mlstm_head(h):
                xmcur = xmb_.t[:, h, 0:NT]
                xmb = xmb_
                ps = PS()
                T(E.matmul(ps.t[:, 0:NT], wq.t[:, h, :], xcb.t[:, h, 0:NT], start=True, stop=True), [wq, xcb], [ps])
                T(E.matmul(ps.t[:, 128:128 + NT], wk.t[:, h, :], xcb.t[:, h, 0:NT], start=True, stop=True), [wk, xcb], [ps])
                psk = PS()
                T(E.matmul(psk.t[0:NT, 256:384], xcb.t[:, h, 0:NT], wk.t[:, h, :], start=True, stop=True), [wk, xcb], [psk])
                T(E.matmul(psk.t[0:NT, 384:512], xmcur, wv.t[:, h, :], start=True, stop=True), [wv, xmb], [psk])
                A(E.activation(out=qT.t[:, h % 2, 0:NT], in_=ps.t[:, 0:NT], func=AF.Copy), [ps], [qT.b[h % 2]])
                A(E.activation(out=kT.t[:, h % 2, 0:NT], in_=ps.t[:, 128:128 + NT], func=AF.Copy), [ps], [kT.b[h % 2]])
                V(E.tensor_copy(out=vaug.t[0:NT, h % 2, 0:128], in_=psk.t[0:NT, 384:512]), [psk], [vaug.b[h % 2]])
                V(E.tensor_tensor(out=qTd.t[:, h % 2, 0:NT], in0=qT.t[:, h % 2, 0:NT], in1=psD.t[:, h * 128:h * 128 + NT], op=ALU.mult), [qT.b[h % 2], psD], [qTd.b[h % 2]])
                if smp:
                    V(E.tensor_scalar(out=wlm.t[:, :], in0=oneh.t[:, :], scalar1=colq.t[0:64, 4 + h:5 + h], scalar2=None, op0=ALU.mult), [oneh, colq], [wlm])
                    V(E.tensor_copy(out=kw.t[0:64, h % 2, :], in_=psk.t[0:64, 256:384]), [psk], [kw.b[h % 2]])
                else:
                    V(E.tensor_scalar(out=kw.t[0:NT, h % 2, :], in0=psk.t[0:NT, 256:384], scalar1=colq.t[0:NT, 4 + h:5 + h], scalar2=None, op0=ALU.mult), [psk, colq], [kw.b[h % 2]])
                yield

                r2 = cnt["n"] % 2
                cnt["n"] += 1
                na, dtt_, sd, hn_, mt = narg[r2], DTt[r2], SDt[r2], hn[r2], mtmp[r2]
                pmask = pms if smp else pmp
                V(E.scalar_tensor_tensor(out=na.t[0:NT, 0:NT], in0=psM.t[0:NT, h * 128:h * 128 + NT], scalar=colq.t[0:NT, h:h + 1],
                                                                           in1=pmask.t[0:NT, 0:NT], op0=ALU.subtract, op1=ALU.add), [psM, colq, pmask], [na])
                A(E.activation(out=dtt_.t[0:NT, 0:NT], in_=na.t[0:NT, 0:NT], func=AF.Exp, scale=-1.0), [na], [dtt_])
                yield
                ps2 = PS()
                T(E.matmul(ps2.t[0:NT, 0:NT], kT.t[:, h % 2, 0:NT], qT.t[:, h % 2, 0:NT], start=True, stop=True), [kT.b[h % 2], qT.b[h % 2]], [ps2])
                V(E.tensor_tensor(out=sd.t[0:NT, 0:NT], in0=ps2.t[0:NT, 0:NT], in1=dtt_.t[0:NT, 0:NT], op=ALU.mult), [ps2, dtt_], [sd])
                V(E.tensor_copy(out=dl.t[:, h:h + 1], in_=psD.t[:, h * 128 + NT - 1:h * 128 + NT]), [psD], [dl])
                yield
                psn_ = psb[7] if h % 2 == 0 else psb[4]
                if not smp:
                    T(E.matmul(psn_.t[0:NT, 0:129], sd.t[0:NT, 0:NT], vaug.t[0:NT, h % 2, 0:129], start=True, stop=False), [sd, vaug.b[h % 2]], [psn_])
                    T(E.matmul(psn_.t[0:NT, 0:129], qTd.t[:, h % 2, 0:NT], Cb[h].t[:, 0:129], start=False, stop=True), [qTd.b[h % 2], Cb[h]], [psn_])
                    psu = PS()
                    T(E.matmul(psu.t[:, 0:129], kw.t[0:NT, h % 2, :], vaug.t[0:NT, h % 2, 0:129], start=True, stop=True), [kw.b[h % 2], vaug.b[h % 2]], [psu])
                    V(E.scalar_tensor_tensor(out=Caug[h].t[:, :], in0=Caug[h].t[:, :], scalar=dl.t[:, h:h + 1], in1=psu.t[:, 0:129], op0=ALU.mult, op1=ALU.add), [Caug[h], dl, psu], [Caug[h]])
                    A(E.activation(out=Cb[h].t[:, 0:129], in_=Caug[h].t[:, :], func=AF.Copy), [Caug[h]], [Cb[h]])
                else:
                    V(E.tensor_copy(out=qz.t[:, :].rearrange("p (b x) -> p b x", x=68)[:, :, 0:4], in_=qTd.t[:, h % 2, 0:64].rearrange("p (b t) -> p b t", t=4)), [qTd.b[h % 2]], [qz])
                    T(E.matmul(psn_.t[0:NT, 0:129], sd.t[0:NT, 0:NT], vaug.t[0:NT, h % 2, 0:129], start=True, stop=False), [sd, vaug.b[h % 2]], [psn_])
                    for b in range(SB):
                        if b % 8 == 0:
                            V(E.tensor_tensor(out=kwm.t[:, :, :], in0=kw.t[0:64, h % 2, :].unsqueeze(1).to_broadcast([64, 8, 128]),
                                              in1=wlm.t[:, b:b + 8].unsqueeze(2).to_broadcast([64, 8, 128]), op=ALU.mult), [kw.b[h % 2], wlm], [kwm])
                        cs = Cst[(b + h * SB) % 4]
                        cn = cs

                        def fetch_state(b_):
                            cs_ = Cst[(b_ + h * SB) % 4]
                            dma_in(cs_.t[:, 0:128], sC_d[b_, h, :, :], cs_.b)
                            A(E.activation(out=cs_.t[:, 128:129], in_=nT.t[:, b_ * 4 + h:b_ * 4 + h + 1], func=AF.Copy), [nT], [cs_], join=True)
                        if b == 0:
                            fetch_state(0)
                            fetch_state(1)
                        if b + 2 < SB:
                            fetch_state(b + 2)
                        T(E.matmul(psn_.t[0:NT, 0:129], qz.t[:, b * 64:b * 64 + 64], cs.t[:, :], start=False, stop=(b == SB - 1)), [qz, cs], [psn_])
                        psu = PS()
                        T(E.matmul(psu.t[:, 0:129], kwm.t[:, b % 8, :], vaug.t[0:64, h % 2, 0:129], start=True, stop=True), [kwm, vaug.b[h % 2]], [psu])
                        V(E.scalar_tensor_tensor(out=cn.t[:, :], in0=cs.t[:, :], scalar=psD.t[:, h * 128 + 4 * b + 3:h * 128 + 4 * b + 4], in1=psu.t[:, 0:129],
                                                                                           op0=ALU.mult, op1=ALU.add), [cs, psD, psu], [cn])
                        A(E.activation(out=nOut.t[:, b * 4 + h:b * 4 + h + 1], in_=cn.t[:, 128:129], func=AF.Copy), [cn], [nOut])
                        D2(E.dma_start(out=oC_o[b, h, :, :], in_=cn.t[:, 0:128]), [cn.b], [], final=True)
                yield
                pop_carry(2 if h == 0 else 1)
                A(E.activation(out=den.t[0:NT, h:h + 1], in_=psn_.t[0:NT, 128:129], func=AF.Abs), [psn_], [den])
                V(E.tensor_tensor(out=den.t[0:NT, h:h + 1], in0=den.t[0:NT, h:h + 1], in1=colq.t[0:NT, 8 + h:9 + h], op=ALU.max), [den, colq], [den])
                V(E.reciprocal(out=rden.t[0:NT, h:h + 1], in_=den.t[0:NT, h:h + 1]), [den], [rden])
                V(E.scalar_tensor_tensor(out=hA.t[0:NT, h % 2, :], in0=psn_.t[0:NT, 0:128], scalar=rden.t[0:NT, h:h + 1], in1=so.t[0:NT, h * 128:(h + 1) * 128], op0=ALU.mult, op1=ALU.mult), [psn_, rden, so], [hA.b[h % 2]])
                yield
                V(E.bn_stats(out=bnst.t[0:NT, h, :], in_=hA.t[0:NT, h % 2, :]), [hA.b[h % 2]], [bnst])
                V(E.bn_aggr(out=mv.t[0:NT, h, :], in_=bnst.t[0:NT, h, :]), [bnst], [mv])
                A(E.activation(out=rstdh.t[0:NT, h:h + 1], in_=mv.t[0:NT, h, 1:2], func=AF.Sqrt, bias=epsc.t[0:NT, :], scale=1.0), [mv, epsc], [rstdh])
                V(E.reciprocal(out=rstdh.t[0:NT, h:h + 1], in_=rstdh.t[0:NT, h:h + 1]), [rstdh], [rstdh])
                V(E.tensor_scalar(out=hn_.t[0:NT, :], in0=hA.t[0:NT, h % 2, :], scalar1=mv.t[0:NT, h, 0:1], scalar2=rstdh.t[0:NT, h:h + 1], op0=ALU.subtract, op1=ALU.mult), [hA.b[h % 2], mv, rstdh], [hn_])
                yield
                pst = PS()
                T(E.transpose(out=pst.t[:, 0:NT], in_=hn_.t[0:NT, :], identity=ident.t[0:NT, 0:NT]), [hn_, ident], [pst])
                pop_carry()
                A(E.activation(out=mt.t[:, 0:NT], in_=pst.t[:, 0:NT], func=AF.Copy, scale=lnh.t[:, h:h + 1]), [pst, lnh], [mt])
                V(E.scalar_tensor_tensor(out=mt.t[:, 0:NT], in0=xc.t[:, h, 0:NT], scalar=skp.t[:, h:h + 1], in1=mt.t[:, 0:NT], op0=ALU.mult, op1=ALU.add), [xc, skp, mt], [mt])
                G(E.tensor_tensor(out=mix.t[:, h, 0:NT], in0=mt.t[:, 0:NT], in1=szm.t[:, h, 0:NT], op=ALU.mult), [mt, szm], [mix.b[h]])

        s5ps = {}

        def s5_B(j):
            ct = j // 4
            psb2 = psb[4 + j % 2]
            T(E.matmul(psb2.t[:, 0:NT], BTre.t[:, j, :], xs5b.t[:, ct, 0:NT], start=True, stop=True), [BTre, xs5b], [psb2])
            T(E.matmul(psb2.t[:, NT:N2], BTim.t[:, j, :], xs5b.t[:, ct, 0:NT], start=True, stop=True), [BTim, xs5b], [psb2])
            s5ps[j] = psb2

        def s5_chain1(j):
            ct = j // 4; q = j % 4; r = j % NS5
            P_, W_, Z_, xb, rm = s5P[r], s5W[r], s5Z[r], s5xb[r], rmat[r]
            psb2 = s5ps.pop(j)
            Bv = pr(psb2.t[:, 0:N2], 2)
            V(E.tensor_tensor(out=pr(P_.t[:, 0:N2], 2), in0=Bv, in1=tb(Ec, j), op=ALU.mult), [psb2, Ec], [P_])
            yield
            V(E.tensor_tensor(out=pr(P_.t[:, N2:N4], 2), in0=Bv, in1=tb(Es, j), op=ALU.mult), [psb2, Es], [P_])
            yield
            V(E.tensor_tensor(out=W_.t[:, 0:NT], in0=P_.t[:, 0:NT], in1=P_.t[:, N3:N4], op=ALU.add), [P_], [W_])
            yield
            V(E.tensor_tensor(out=W_.t[:, NT:N2], in0=P_.t[:, NT:N2], in1=P_.t[:, N2:N3], op=ALU.subtract), [P_], [W_])
            yield
            if smp:
                Wv = pr(W_.t[:, 0:N2], 2)
                ah = ah0j[r]
                V(E.tensor_scalar(out=ah.t[:, 0, :], in0=sore.t[:, :, j], scalar1=are.t[:, j:j + 1], scalar2=None, op0=ALU.mult), [sore, are], [ah])
                V(E.scalar_tensor_tensor(out=ah.t[:, 0, :], in0=soim.t[:, :, j], scalar=naim.t[:, j:j + 1], in1=ah.t[:, 0, :], op0=ALU.mult, op1=ALU.add), [soim, naim, ah], [ah])
                V(E.tensor_scalar(out=ah.t[:, 1, :], in0=soim.t[:, :, j], scalar1=are.t[:, j:j + 1], scalar2=None, op0=ALU.mult), [soim, are], [ah])
                V(E.scalar_tensor_tensor(out=ah.t[:, 1, :], in0=sore.t[:, :, j], scalar=aim.t[:, j:j + 1], in1=ah.t[:, 1, :], op0=ALU.mult, op1=ALU.add), [sore, aim, ah], [ah])
                V(E.tensor_tensor(out=Wv[:, :, :, 0:1], in0=Wv[:, :, :, 0:1], in1=ah.t[:, :, :].unsqueeze(3), op=ALU.add), [W_, ah], [W_])
                A(E.activation(out=pr(rm.t[:, 0:N2], 2), in_=seqm.t[:, :].rearrange("p (b t) -> p b t", t=4).unsqueeze(1).to_broadcast([128, 2, 16, 4]),
                               func=AF.Copy, scale=mag.t[:, j:j + 1]), [seqm, mag], [rm])
            else:
                Wv = pr(W_.t[:, 0:N2], 2)
                V(E.tensor_tensor(out=Wv[:, :, 0:1], in0=Wv[:, :, 0:1], in1=azin.t[:, j, :].unsqueeze(2), op=ALU.add), [W_, azin], [W_])
                A(E.activation(out=pr(rm.t[:, 0:N2], 2), in_=ones4.t[:, 0:NT].unsqueeze(1).to_broadcast([128, 2, NT]), func=AF.Copy, scale=mag.t[:, j:j + 1]), [ones4, mag], [rm])
                A(E.activation(out=pr(rm.t[:, 0:N2], 2)[:, :, 0:1], in_=pr(rm.t[:, 0:N2], 2)[:, :, 0:1], func=AF.Copy, scale=0.0), [rm], [rm])
            V(E.tensor_tensor_scan(out=Z_.t[:, 0:N2], data0=rm.t[:, 0:N2], data1=W_.t[:, 0:N2], initial=0.0, op0=ALU.mult, op1=ALU.add), [rm, W_], [Z_])
            Zv = pr(Z_.t[:, 0:N2], 2)
            G(E.tensor_tensor(out=pr(P_.t[:, 0:N2], 2), in0=Zv, in1=tb(Ec, j), op=ALU.mult), [Z_, Ec], [P_])
            G(E.tensor_tensor(out=pr(P_.t[:, N2:N4], 2), in0=Zv, in1=tb(Es, j), op=ALU.mult), [Z_, Es], [P_])

        def s5_chain2(j):
            ct = j // 4; q = j % 4; r = j % NS5
            P_, W_, Z_, xb, rm = s5P[r], s5W[r], s5Z[r], s5xb[r], rmat[r]
            V(E.tensor_tensor(out=xb.t[:, 0:NT], in0=P_.t[:, 0:NT], in1=P_.t[:, N3:N4], op=ALU.subtract), [P_], [xb])
            yield
            V(E.tensor_tensor(out=xb.t[:, NT:N2], in0=P_.t[:, N2:N3], in1=P_.t[:, NT:N2], op=ALU.add), [P_], [xb])
            yield
            if smp:
                def l3(a, b):
                    return P_.t[:, a:b].rearrange("p (b t) -> p b t", t=4)[:, :, 3:4]
                V(E.tensor_tensor(out=sore.t[:, :, j:j + 1], in0=l3(0, NT), in1=l3(N3, N4), op=ALU.subtract), [P_], [sore])
                V(E.tensor_tensor(out=soim.t[:, :, j:j + 1], in0=l3(N2, N3), in1=l3(NT, N2), op=ALU.add), [P_], [soim])
            else:
                V(E.tensor_tensor(out=xl_re.t[:, j:j + 1], in0=P_.t[:, NT - 1:NT], in1=P_.t[:, N4 - 1:N4], op=ALU.subtract), [P_], [xl_re])
                V(E.tensor_tensor(out=xl_im.t[:, j:j + 1], in0=P_.t[:, N3 - 1:N3], in1=P_.t[:, N2 - 1:N2], op=ALU.add), [P_], [xl_im])
            yield
            s5_C(j)

        def s5_C(j):
            ct = j // 4; q = j % 4; r = j % NS5
            xb = s5xb[r]; psy = psb[6 + ct % 2]
            T(E.matmul(psy.t[:, 0:NT], CTre.t[:, j, :], xb.t[:, 0:NT], start=(q == 0), stop=False), [CTre, xb], [psy])
            T(E.matmul(psy.t[:, 0:NT], CTimn.t[:, j, :], xb.t[:, NT:N2], start=False, stop=(q == 3)), [CTimn, xb], [psy])

        def s5_epi(ct):
            psy = psb[6 + ct % 2]
            V(E.scalar_tensor_tensor(out=ysb.t[:, ct, 0:NT], in0=xs5.t[:, ct, 0:NT], scalar=s5D.t[:, ct:ct + 1], in1=psy.t[:, 0:NT], op0=ALU.mult, op1=ALU.add), [xs5, s5D, psy], [ysb.b[ct]])
            yield
            g1 = gl1[ct % 2]; g2_ = gl2[ct % 2]
            A(E.activation(out=g1.t[:, 0:NT], in_=ysb.t[:, ct, 0:NT], func=AF.Square), [ysb.b[ct]], [g1])
            yield
            V(E.tensor_scalar(out=g1.t[:, 0:NT], in0=g1.t[:, 0:NT], scalar1=0.044715, scalar2=1.0, op0=ALU.mult, op1=ALU.add), [g1], [g1])
            yield
            V(E.tensor_tensor(out=g1.t[:, 0:NT], in0=g1.t[:, 0:NT], in1=ysb.t[:, ct, 0:NT], op=ALU.mult), [g1, ysb.b[ct]], [g1])
            yield
            A(E.activation(out=g2_.t[:, 0:NT], in_=g1.t[:, 0:NT], func=AF.Sigmoid, scale=1.5957691216057308), [g1], [g2_])
            yield
            V(E.tensor_tensor(out=yg.t[:, ct, 0:NT], in0=ysb.t[:, ct, 0:NT], in1=g2_.t[:, 0:NT], op=ALU.mult), [ysb.b[ct], g2_], [yg.b[ct]])
            yield
            G(E.tensor_copy(out=ygb.t[:, ct, 0:NT], in_=yg.t[:, ct, 0:NT]), [yg.b[ct]], [ygb])

        bg = []
        if glu_prev is not None:
            bg.append(glu_prev())
        s5_B(0)
        s5_B(1)
        for _ in s5_chain1(0):
            pass
        for j in range(16):
            if j + 2 < 16:
                s5_B(j + 2)
            if j % 2 == 0 and j >= 2:
                pop_pending()
            gens = ([s5_chain1(j + 1)] if j + 1 < 16 else []) + [s5_chain2(j)] + bg
            del bg[:]
            while gens:
                for g_ in list(gens):
                    try:
                        next(g_)
                    except StopIteration:
                        gens.remove(g_)
            if j % 4 == 3:
                bg.append(s5_epi(j // 4))

        for g_ in bg:
            for _ in g_:
                pass
        while pending:
            pop_pending(background=False)
        gens = [gates_gen(), conv_gen()]
        while gens:
            for g_ in list(gens):
                try:
                    next(g_)
                except StopIteration:
                    gens.remove(g_)
        for pair in (((0,), (1,), (2,), (3,)) if smp else ((0, 1), (2, 3))):
            gens = [mlstm_head(h_) for h_ in pair]
            while gens:
                for g_ in list(gens):
                    try:
                        next(g_)
                    except StopIteration:
                        gens.remove(g_)
        pop_carry(len(cin))
        if not smp:
            V(E.tensor_copy(out=xmh.t[:, :, 0:3], in_=xmh.t[:, :, NT:NT + 3]), [xmh], [xmh])

        if not smp:
            V(E.tensor_tensor(out=t0.t[:], in0=are.t[:], in1=xl_re.t[:], op=ALU.mult), [are, xl_re], [t0])
            V(E.tensor_tensor(out=t1.t[:], in0=aim.t[:], in1=xl_im.t[:], op=ALU.mult), [aim, xl_im], [t1])
            V(E.tensor_tensor(out=azin.t[:, :, 0], in0=t0.t[:], in1=t1.t[:], op=ALU.subtract), [t0, t1], [azin])
            V(E.tensor_tensor(out=t0.t[:], in0=aim.t[:], in1=xl_re.t[:], op=ALU.mult), [aim, xl_re], [t0])
            V(E.tensor_tensor(out=t1.t[:], in0=are.t[:], in1=xl_im.t[:], op=ALU.mult), [are, xl_im], [t1])
            V(E.tensor_tensor(out=azin.t[:, :, 1], in0=t0.t[:], in1=t1.t[:], op=ALU.add), [t0, t1], [azin])
        def glu_gen():
            for oc in range(4):
                ps = PS()
                for kc in range(4):
                    T(E.matmul(ps.t[:, 0:NT], wglu.t[:, kc, oc * 128:(oc + 1) * 128], ygb.t[:, kc, 0:NT], start=(kc == 0), stop=(kc == 3)), [wglu, ygb], [ps])
                    yield
                g1 = gl1[oc % 2]
                A(E.activation(out=g1.t[:, 0:NT], in_=ps.t[:, 0:NT], func=AF.Sigmoid, bias=bglu.t[:, oc:oc + 1], scale=1.0), [ps, bglu], [g1])
                yield
                V(E.tensor_tensor(out=g1.t[:, 0:NT], in0=g1.t[:, 0:NT], in1=yg.t[:, oc, 0:NT], op=ALU.mult), [g1, yg.b[oc]], [g1])
                yield
                G(E.tensor_tensor(out=mix.t[:, 4 + oc, 0:NT], in0=g1.t[:, 0:NT], in1=szs.t[:, oc, 0:NT], op=ALU.mult), [g1, szs], [mix.b[4 + oc]])
                yield

        def c_outproj(half):
            if half == 0:
                dma_in(h2.t[0:NT, :], x_src, h2.b)
            ps = PS()
            for k in range(8):
                T(E.matmul(ps.t[0:NT, :], mix.t[:, k, 0:NT], wout.t[:, k, half * 512:(half + 1) * 512], start=(k == 0), stop=(k == 7)), [mix.b[k], wout], [ps])
            V(E.tensor_tensor(out=h2.t[0:NT, half * 512:(half + 1) * 512], in0=ps.t[0:NT, :], in1=h2.t[0:NT, half * 512:(half + 1) * 512], op=ALU.add), [ps, h2], [h2])

        def c_h2T():
            for half in range(2):
                ps = PS()
                for kk in range(4):
                    k = half * 4 + kk
                    T(E.transpose(out=ps.t[:, kk * 128:kk * 128 + NT], in_=h2.t[0:NT, k * 128:(k + 1) * 128], identity=ident.t[0:NT, 0:NT]), [h2, ident], [ps])
                A(E.activation(out=h2T.t[:, half * 4:half * 4 + 4, 0:NT], in_=ps.t[:, :].rearrange("p (k t) -> p k t", k=4)[:, :, 0:NT], func=AF.Copy), [ps], [h2T])

        def c_gate(half):
            ps = PS()
            for k in range(8):
                T(E.matmul(ps.t[0:NT, :], h2T.t[:, k, 0:NT], wgate.t[:, k, half * 512:(half + 1) * 512], start=(k == 0), stop=(k == 7)), [h2T, wgate], [ps])
            A(E.activation(out=sgate.t[0:NT, half * 512:(half + 1) * 512], in_=ps.t[0:NT, :], func=AF.Sigmoid), [ps], [sgate])

        def c_tail():
            G(E.tensor_tensor(out=esb.t[0:NT, :], in0=esb.t[0:NT, :], in1=sgate.t[0:NT, :], op=ALU.mult), [esb, sgate], [esb])
            G(E.tensor_tensor(out=esb.t[0:NT, :], in0=esb.t[0:NT, :], in1=h2.t[0:NT, :], op=ALU.add), [esb, h2], [esb])
            A(E.activation(out=sgate.t[0:NT, :], in_=esb.t[0:NT, :], func=AF.Square, accum_out=ssq.t[0:NT, 3:4]), [esb], [sgate, ssq])
            A(E.activation(out=rstd.t[0:NT, 3:4], in_=ssq.t[0:NT, 3:4], func=AF.Sqrt, bias=epsc.t[0:NT, :], scale=1.0 / DM), [ssq, epsc], [rstd])
            V(E.reciprocal(out=rstd.t[0:NT, 3:4], in_=rstd.t[0:NT, 3:4]), [rstd], [rstd])
            V(E.scalar_tensor_tensor(out=sgate.t[0:NT, :], in0=esb.t[0:NT, :], scalar=rstd.t[0:NT, 3:4], in1=lnfin.t[0:NT, :], op0=ALU.mult, op1=ALU.mult), [esb, rstd, lnfin], [sgate])
            dma_out(y_dst, sgate.t[0:NT, :], [sgate.b])
        return [lambda: c_outproj(0), lambda: c_outproj(1), c_h2T, lambda: c_gate(0), lambda: c_gate(1), c_tail], glu_gen

    if DBG_STAGE == "setup":
        dma_out(yp_o[0:128, 0:128], Ec.t[:, 3, :], [Ec.b])
        dma_out(yp_o[0:128, 128:256], Es.t[:, 15, :], [Es.b])
        V(E.tensor_copy(out=ytile.t[:, 0:128], in_=BTre.t[:, 5, :]), [BTre], [ytile])
        V(E.tensor_copy(out=ytile.t[:, 128:256], in_=CTimn.t[:, 9, :]), [CTimn], [ytile])
        V(E.tensor_copy(out=ytile.t[:, 256:512], in_=win.t[:, 7, 1400:1656]), [win], [ytile])
        V(E.tensor_copy(out=ytile.t[:, 512:768], in_=wgate.t[:, 7, 100:356]), [wgate], [ytile])
        V(E.tensor_copy(out=ytile.t[:, 768:1024], in_=wglu.t[:, 3, 256:512]), [wglu], [ytile])
        dma_out(yp_o[0:128, 256:1280 - 256], ytile.t[:, 0:768], [ytile.b])
        p.build()
        return nc, p
    dma_in(m0c.t[:, :], sm_d.rearrange("b h -> h b"), m0c.b, slow=True)
    V(E.tensor_copy(out=m0row.t[:, :].rearrange("h (b t) -> h b t", t=4), in_=m0c.t[:, :].unsqueeze(2).to_broadcast([4, 16, 4])), [m0c], [m0row])
    cv = sconv_d.rearrange("b j (c q) -> (b j c) q", q=128)
    dma_in(ytile.t[:, 0:128], cv[0:128, :], ytile.b)
    dma_in(ytile.t[0:64, 128:256], cv[128:192, :], ytile.b, join=True)
    hvr = sre_d.rearrange("b (j g) q -> (b j) (g q)", g=2); hvi = sim_d.rearrange("b (j g) q -> (b j) (g q)", g=2)
    for hf in range(2):
        dma_in(ytile.t[:, 256 + hf * 128:384 + hf * 128], hvr[hf * 128:(hf + 1) * 128, :], ytile.b, join=True)
        dma_in(ytile.t[:, 512 + hf * 128:640 + hf * 128], hvi[hf * 128:(hf + 1) * 128, :], ytile.b, join=True)
    dma_in(ytile.t[0:64, 768:896], sn_d.rearrange("b h d -> (b h) d"), ytile.b, join=True)
    psp = PS()
    T(E.transpose(out=psp.t[:, 0:128], in_=ytile.t[:, 0:128], identity=ident.t[:, :]), [ytile, ident], [psp])
    T(E.transpose(out=psp.t[:, 128:192], in_=ytile.t[0:64, 128:256], identity=ident.t[0:64, 0:64]), [ytile, ident], [psp])
    for c in range(4):
        V(E.tensor_copy(out=xms.t[:, c, :, 0:3], in_=psp.t[:, 0:192].rearrange("p (b j c) -> p c b j", j=3, c=4)[:, c]), [psp], [xms])
    for src0, dstt in ((256, sore), (512, soim)):
        psp = PS()
        for hf in range(2):
            T(E.transpose(out=psp.t[:, hf * 128:(hf + 1) * 128], in_=ytile.t[:, src0 + hf * 128:src0 + (hf + 1) * 128], identity=ident.t[:, :]), [ytile, ident], [psp])
        V(E.tensor_copy(out=dstt.t[:, :, :].rearrange("p b j -> p (b j)"), in_=psp.t[:, 0:256]), [psp], [dstt])
    psp = PS()
    T(E.transpose(out=psp.t[:, 0:64], in_=ytile.t[0:64, 768:896], identity=ident.t[0:64, 0:64]), [ytile, ident], [psp])
    V(E.tensor_copy(out=nT.t[:, :], in_=psp.t[:, 0:64]), [psp], [nT])
    V(E.tensor_scalar(out=naim.t[:], in0=aim.t[:], scalar1=-1.0, scalar2=None, op0=ALU.mult), [aim], [naim])
    carry = []
    glu_prev = None
    def tile_id(n):
        return ("p", n) if n < NPT else (("s", 0) if n == NPT else None)
    load_inputs("p", 0)
    front("p", 0, nxt=tile_id(1))
    for ti in range(NPT):
        nf = tile_id(ti + 1)
        carry, glu_prev = do_tile("p", ti, carry, nxt=(nf[0], nf[1], tile_id(ti + 2)), glu_prev=glu_prev)
    if DBG_STAGE == "prompt":
        p.build()
        return nc, p
    for h in range(4):
        dma_out(pC_o[h, :, :], Caug[h].t[:, 0:128], [Caug[h].b])
        dma_out(pn_o[h, :].rearrange("(d o) -> d o", o=1), Caug[h].t[:, 128:129], [Caug[h].b], slow=True)
    dma_out(pm_o.rearrange("(h o) -> h o", o=1), mnew.t[:, 0:1], [mnew.b], slow=True)
    for c in range(4):
        dma_out(pconv_o[:, c * 128:(c + 1) * 128].rearrange("j q -> q j"), xmh.t[:, c, 0:3], [xmh.b], slow=True)
    dma_out(pre_o.rearrange("(j g) q -> (g q) j", g=2), xl_re.t[:, :], [xl_re.b], slow=True)
    dma_out(pim_o.rearrange("(j g) q -> (g q) j", g=2), xl_im.t[:, :], [xl_im.b], slow=True)

    carry, glu_prev = do_tile("s", 0, carry, glu_prev=glu_prev)
    for _ in glu_prev():
        pass
    for f in carry:
        f()
    dma_out(om_o.rearrange("b h -> h b"), mnew.t[:, :], [mnew.b], slow=True)
    ost = TL(s5P[0].t, s5P[0].b); ost2 = TL(s5P[1].t, s5P[1].b)
    for nm, src, dst in (("re", sore, ore_o), ("im", soim, oim_o)):
        pso = PS()
        for hf in range(2):
            T(E.transpose(out=pso.t[:, hf * 128:(hf + 1) * 128], in_=src.t[:, hf * 8:(hf + 1) * 8, :].rearrange("p b j -> p (b j)"), identity=ident.t[:, :]), [src, ident], [pso])
        o_ = ost if nm == "re" else ost2
        V(E.tensor_copy(out=o_.t[:, 0:256], in_=pso.t[:, 0:256]), [pso], [o_])
        dv = dst.rearrange("b (j g) q -> (b j) (g q)", g=2)
        for hf in range(2):
            dma_out(dv[hf * 128:(hf + 1) * 128, :], o_.t[:, hf * 128:(hf + 1) * 128], [o_.b])
    for c in range(4):
        V(E.tensor_copy(out=ost.t[:, 256:448].rearrange("p (b j c) -> p c b j", j=3, c=4)[:, c], in_=xms.t[:, c, :, 4:7]), [xms], [ost])
    pso = PS()
    T(E.transpose(out=pso.t[:, 0:128], in_=ost.t[:, 256:384], identity=ident.t[:, :]), [ost, ident], [pso])
    T(E.transpose(out=pso.t[0:64, 128:256], in_=ost.t[:, 384:448], identity=ident.t[:, :]), [ost, ident], [pso])
    V(E.tensor_copy(out=ost2.t[:, 256:384], in_=pso.t[:, 0:128]), [pso], [ost2])
    V(E.tensor_copy(out=ost2.t[0:64, 384:512], in_=pso.t[0:64, 128:256]), [pso], [ost2])
    cvo = oconv_o.rearrange("b j (c q) -> (b j c) q", q=128)
    dma_out(cvo[0:128, :], ost2.t[:, 256:384], [ost2.b])
    dma_out(cvo[128:192, :], ost2.t[0:64, 384:512], [ost2.b])
    pso = PS()
    T(E.transpose(out=pso.t[0:64, 0:128], in_=nOut.t[:, :], identity=ident.t[:, :]), [nOut, ident], [pso])
    V(E.tensor_copy(out=ost.t[0:64, 0:128], in_=pso.t[0:64, 0:128]), [pso], [ost])
    dma_out(on_o.rearrange("b h d -> (b h) d"), ost.t[0:64, 0:128], [ost.b])

    p.build()
    return nc, p


_CACHE = {}


def kernel(**inputs):
    f = lambda a: np.ascontiguousarray(np.asarray(a, dtype=np.float32))
    if "nc" not in _CACHE:
        _CACHE["nc"] = build_program()
    nc, prog = _CACHE["nc"]
    consts = make_consts()
    shared = {}
    for k in ("ln_mix", "w_in", "b_igate", "b_fgate", "conv_w", "conv_b", "w_q", "w_k", "w_v", "ln_head", "skip_a",
              "s5_lam_re", "s5_lam_im", "s5_log_dt", "s5_B_re", "s5_B_im", "s5_C_re", "s5_C_im", "s5_D", "w_glu",
              "b_glu", "w_out", "w_ple", "ln_ple", "w_ple_gate"):
        shared[k] = f(inputs[k])[0]
    shared["ln_final"] = f(inputs["ln_final"])
    for k, v in consts.items():
        shared["c_" + k] = v
    xp = f(inputs["x_prompt"]); xs = f(inputs["x_sample"])
    pp = f(inputs["p_prompt"])[0]; ps = f(inputs["p_sample"])[0]
    sC = f(inputs["state_mlstm_C"])[0]; sn = f(inputs["state_mlstm_n"])[0]; sm = f(inputs["state_mlstm_m"])[0]
    sconv = f(inputs["state_conv"])[0]; sre = f(inputs["state_s5_re"])[0]; sim = f(inputs["state_s5_im"])[0]
    in_maps = []
    for c in range(NCORES):
        sl = slice(c * SB, (c + 1) * SB)
        m = dict(shared)
        m["xp"] = xp[c]; m["xs"] = np.ascontiguousarray(xs[sl].reshape(64, DM))
        m["pp"] = pp[c]; m["psm"] = np.ascontiguousarray(ps[sl].reshape(64, 256))
        m["sC"] = np.ascontiguousarray(sC[sl]); m["sn"] = np.ascontiguousarray(sn[sl]); m["sm"] = np.ascontiguousarray(sm[sl])
        m["sconv"] = np.ascontiguousarray(sconv[sl]); m["sre"] = np.ascontiguousarray(sre[sl]); m["sim"] = np.ascontiguousarray(sim[sl])
        in_maps.append(m)
    res = run_bass_kernel_spmd(nc, in_maps, core_ids=list(range(NCORES)))
    R = res.results
    g = lambda k: [np.asarray(R[c][k], dtype=np.float32) for c in range(NCORES)]
    y_prompt = np.stack(g("yp"), 0)
    y_sample = np.concatenate([a.reshape(SB, 4, DM) for a in g("ys")], 0)
    pC = np.stack(g("pC"), 0)[None]; pn = np.stack(g("pn"), 0)[None]; pm = np.stack(g("pm"), 0)[None]
    pconv = np.stack(g("pconv"), 0)[None]; pre = np.stack(g("pre"), 0)[None]; pim = np.stack(g("pim"), 0)[None]
    oC = np.concatenate(g("oC"), 0)[None]; on = np.concatenate(g("on"), 0)[None]; om = np.concatenate(g("om"), 0)[None]
    oconv = np.concatenate(g("oconv"), 0)[None]; ore = np.concatenate(g("ore"), 0)[None]; oim = np.concatenate(g("oim"), 0)[None]
    return (y_prompt, y_sample, pC, pn, pm, pconv, pre, pim, oC, on, om, oconv, ore, oim)
```

```python
import math
from contextlib import ExitStack
import numpy as np
import concourse.bass as bass
import concourse.mybir as mybir
from concourse.bass_utils import run_bass_kernel_spmd

F32 = mybir.dt.float32
BF16 = mybir.dt.bfloat16
I32 = mybir.dt.int32
AF = mybir.ActivationFunctionType
ALU = mybir.AluOpType
ENGS = ("sync", "scalar", "vector", "gpsimd", "tensor")

NCORES = 8
SEQ = 2048
NPT = 16
SB = 16
DM = 1024
INC = 2568
EPS = 1e-6
BIG = 1.0e30
SAME_ENGINE_SYNC = True
DBG_CUT = None
DBG_VAR = 0
DBG_STAGE = None


class Buf:
    __slots__ = ("name", "ws", "base", "readers")

    def __init__(self, name, like=None):
        self.name = name
        self.ws = []
        self.base = []
        self.readers = []
        if like is not None:
            for l in like:
                self.ws += l.ws
                self.base += l.base
                self.readers += l.readers


class Prog:
    def __init__(self, nc, n_dma_sems=24):
        self.nc = nc
        self.ops = []
        self.n_dma_sems = n_dma_sems
        self.final_dma = []
        self.es = ExitStack()

    def sbuf(self, name, shape, dtype=F32):
        return self.es.enter_context(self.nc.sbuf_tensor(name, list(shape), dtype))

    def psum(self, name, shape, dtype=F32):
        return self.es.enter_context(self.nc.psum_tensor(name, list(shape), dtype))

    def op(self, eng, fn, reads=(), writes=(), dma=False, final=False, join=False):
        i = len(self.ops)
        deps = set()
        for b in reads:
            deps.update(b.ws)
        for b in writes:
            if not join:
                deps.update(b.ws)
                deps.update(b.readers)
            else:
                if not b.base and (b.ws or b.readers):
                    b.base = list(b.ws) + list(b.readers)
                deps.update(b.base)
                deps.update(b.readers)
        deps.discard(i)
        self.ops.append(dict(eng=eng, fn=fn, deps=deps, dma=dma))
        for b in reads:
            b.readers.append(i)
        for b in writes:
            if join:
                b.ws = b.ws + [i]
            else:
                b.ws = [i]
                b.base = []
            b.readers = []
        if final:
            self.final_dma.append(i)
        return i

    def build(self):
        nc = self.nc
        ops = self.ops
        n = len(ops)
        needed = [False] * n
        for i, o in enumerate(ops):
            for d in o["deps"]:
                if ops[d]["eng"] != o["eng"] or SAME_ENGINE_SYNC or ops[d]["dma"]:
                    needed[d] = True
        with self.es as es:
            esem = {e: es.enter_context(nc.semaphore("s_" + e)) for e in ENGS}
            dsem = [es.enter_context(nc.semaphore("d%d" % k)) for k in range(self.n_dma_sems)]
            tok = [None] * n
            ecount = {e: 0 for e in ENGS}
            dcount = [0] * self.n_dma_sems
            dlast = [None] * self.n_dma_sems
            dk = 0
            prev_same_sem = {}
            for i, o in enumerate(ops):
                if o["dma"]:
                    k = dk % self.n_dma_sems
                    dk += 1
                    dcount[k] += 16
                    tok[i] = ("d", k, dcount[k])
                    if dlast[k] is not None:
                        prev_same_sem[i] = dlast[k]
                    dlast[k] = i
                elif needed[i]:
                    ecount[o["eng"]] += 1
                    tok[i] = ("e", o["eng"], ecount[o["eng"]])
                else:
                    tok[i] = ("e", o["eng"], ecount[o["eng"]] + 1)
            per_eng = {e: [] for e in ENGS}
            waited = {e: {} for e in ENGS}
            for i, o in enumerate(ops):
                e = o["eng"]
                deps = set(o["deps"])
                if i in prev_same_sem:
                    deps.add(prev_same_sem[i])
                waits = []
                for d in sorted(deps):
                    od = ops[d]
                    if (not od["dma"]) and od["eng"] == e and not SAME_ENGINE_SYNC:
                        continue
                    t = tok[d]
                    key = (t[0], t[1])
                    if waited[e].get(key, 0) >= t[2]:
                        continue
                    waited[e][key] = t[2]
                    sem = dsem[t[1]] if t[0] == "d" else esem[t[1]]
                    waits.append((sem, t[2]))
                sig = None
                if o["dma"]:
                    sig = (dsem[tok[i][1]], 16)
                elif needed[i]:
                    sig = (esem[e], 1)
                per_eng[e].append((waits, o["fn"], sig))
            fin = [(dsem[tok[i][1]], tok[i][2]) for i in self.final_dma]
            self.stats = {e: len(per_eng[e]) for e in ENGS}

            def run(engobj, lst, is_sync=False):
                for waits, fn, sig in lst:
                    for (s, v) in waits:
                        engobj.wait_ge(s, v)
                    ins = fn(engobj)
                    if sig is not None:
                        ins.then_inc(sig[0], sig[1])
                if is_sync:
                    done = {}
                    for (s, v) in fin:
                        engobj.wait_ge(s, v)

            with nc.Block() as block:
                @block.sync
                def _(e):
                    run(e, per_eng["sync"], True)

                @block.scalar
                def _(e):
                    run(e, per_eng["scalar"])

                @block.vector
                def _(e):
                    run(e, per_eng["vector"])

                @block.gpsimd
                def _(e):
                    run(e, per_eng["gpsimd"])

                @block.tensor
                def _(e):
                    run(e, per_eng["tensor"])
        return nc


class _Rec:
    def __getattr__(self, name):
        def f(*a, **k):
            return lambda e: getattr(e, name)(*a, **k)
        return f


E = _Rec()


class TL:
    def __init__(self, t, b):
        self.t = t
        self.b = b


def make_consts():
    c = {}
    c["ident"] = np.eye(128, dtype=np.float32)
    s = np.arange(128)[:, None]
    t = np.arange(128)[None, :]
    c["posmask_p"] = np.where(s <= t, 0.0, BIG).astype(np.float32)
    s6 = np.arange(64)[:, None]
    t6 = np.arange(64)[None, :]
    pm = np.where((s6 <= t6) & (s6 // 4 == t6 // 4), 0.0, BIG).astype(np.float32)
    c["posmask_s"] = pm
    sel = np.zeros((4, 4, 128), np.float32)
    for h in range(4):
        sel[h, h, :] = 1.0
    c["sel"] = sel.reshape(4, 512)
    sm = np.ones((128, 64), np.float32)
    sm[:, 0::4] = 0.0
    c["seqmask1"] = sm
    nb = np.zeros((4, 64), np.float32)
    nb[:, 0::4] = -BIG
    c["seqnegbig"] = nb
    oh = np.zeros((64, 16), np.float32)
    for s_ in range(64):
        oh[s_, s_ // 4] = 1.0
    c["onehot_s"] = oh
    return c


def build_program():
    nc = bass.Bass("TRN2", target_bir_lowering=False)
    p = Prog(nc)
    try:
        return _build_body(nc, p)
    except Exception as ex:
        if type(ex).__name__ != "_Cut":
            raise
        p.build()
        return nc, p


def _build_body(nc, p):
    din, dout = {}, {}

    def inp(name, shape):
        din[name] = nc.dram_tensor(name, list(shape), F32, kind="ExternalInput").ap()
        return din[name]

    def outp(name, shape):
        dout[name] = nc.dram_tensor(name, list(shape), F32, kind="ExternalOutput").ap()
        return dout[name]

    xp_d = inp("xp", [SEQ, DM]); xs_d = inp("xs", [64, DM])
    pp_d = inp("pp", [SEQ, 256]); psm_d = inp("psm", [64, 256])
    sC_d = inp("sC", [SB, 4, 128, 128]); sn_d = inp("sn", [SB, 4, 128]); sm_d = inp("sm", [SB, 4])
    sconv_d = inp("sconv", [SB, 3, 512]); sre_d = inp("sre", [SB, 32, 64]); sim_d = inp("sim", [SB, 32, 64])
    ln_mix_d = inp("ln_mix", [DM]); w_in_d = inp("w_in", [DM, INC])
    b_ig_d = inp("b_igate", [4]); b_fg_d = inp("b_fgate", [4])
    conv_w_d = inp("conv_w", [4, 512]); conv_b_d = inp("conv_b", [512])
    wq_d = inp("w_q", [4, 128, 128]); wk_d = inp("w_k", [4, 128, 128]); wv_d = inp("w_v", [4, 128, 128])
    ln_head_d = inp("ln_head", [512]); skip_d = inp("skip_a", [512])
    lamre_d = inp("s5_lam_re", [32, 64]); lamim_d = inp("s5_lam_im", [32, 64]); logdt_d = inp("s5_log_dt", [32, 64])
    Bre_d = inp("s5_B_re", [32, 64, 16]); Bim_d = inp("s5_B_im", [32, 64, 16])
    Cre_d = inp("s5_C_re", [32, 16, 64]); Cim_d = inp("s5_C_im", [32, 16, 64])
    s5D_d = inp("s5_D", [512]); wglu_d = inp("w_glu", [512, 512]); bglu_d = inp("b_glu", [512])
    wout_d = inp("w_out", [DM, DM]); wple_d = inp("w_ple", [256, DM]); lnple_d = inp("ln_ple", [DM])
    wgate_d = inp("w_ple_gate", [DM, DM]); lnfin_d = inp("ln_final", [DM])
    c_ident = inp("c_ident", [128, 128]); c_pmp = inp("c_posmask_p", [128, 128]); c_pms = inp("c_posmask_s", [64, 64])
    c_sel = inp("c_sel", [4, 512]); c_seqm = inp("c_seqmask1", [128, 64]); c_snb = inp("c_seqnegbig", [4, 64])
    c_oh = inp("c_onehot_s", [64, 16])

    yp_o = outp("yp", [SEQ, DM]); ys_o = outp("ys", [64, DM])
    pC_o = outp("pC", [4, 128, 128]); pn_o = outp("pn", [4, 128]); pm_o = outp("pm", [4])
    pconv_o = outp("pconv", [3, 512]); pre_o = outp("pre", [32, 64]); pim_o = outp("pim", [32, 64])
    oC_o = outp("oC", [SB, 4, 128, 128]); on_o = outp("on", [SB, 4, 128]); om_o = outp("om", [SB, 4])
    oconv_o = outp("oconv", [SB, 3, 512]); ore_o = outp("ore", [SB, 32, 64]); oim_o = outp("oim", [SB, 32, 64])

    def mk(name, shape, dtype=F32, nb=1):
        t = p.sbuf(name, shape, dtype)
        if nb == 1:
            return TL(t, Buf(name))
        return TL(t, [Buf("%s%d" % (name, i)) for i in range(nb)])

    def bl(x):
        out = []
        for a in x:
            if isinstance(a, TL):
                out += a.b if isinstance(a.b, list) else [a.b]
            elif isinstance(a, Buf):
                out.append(a)
            elif isinstance(a, (list, tuple)):
                out += bl(a)
            elif a is not None:
                raise TypeError(a)
        return out

    def V(fn, r, w, **kw): p.op("vector", fn, bl(r), bl(w), **kw)
    def A(fn, r, w, **kw): p.op("scalar", fn, bl(r), bl(w), **kw)
    def G(fn, r, w, **kw): p.op("gpsimd", fn, bl(r), bl(w), **kw)
    def T(fn, r, w, **kw): p.op("tensor", fn, bl(r), bl(w), **kw)
    def D(fn, r, w, **kw): p.op("sync", fn, bl(r), bl(w), dma=True, **kw)
    def D2(fn, r, w, **kw): p.op("scalar", fn, bl(r), bl(w), dma=True, **kw)

    NPS = 8
    psb = [TL(p.psum("ps%d" % i, [128, 512], F32), Buf("ps%d" % i)) for i in range(NPS)]
    psi = [0]

    def PS():
        x = psb[psi[0] % 4]
        psi[0] += 1
        return x

    class _Cut(Exception):
        pass

    def CUT(k, tl):
        if DBG_CUT == k:
            D(E.dma_start(out=yp_o[0:tl.t.shape[0], 0:16], in_=tl.t[:, 0:16]), [tl], [], final=True)
            raise _Cut()

    def dma_in(dst_ap, src_ap, wbuf, slow=False, join=False):
        if slow:
            D(E.dma_start(out=dst_ap, in_=src_ap, allow_slow_non_contiguous=True), [], [wbuf], join=join)
        else:
            D(E.dma_start(out=dst_ap, in_=src_ap), [], [wbuf], join=join)

    def dma_out(dst_ap, src_ap, rbufs, slow=False):
        if slow:
            D(E.dma_start(out=dst_ap, in_=src_ap, allow_slow_non_contiguous=True), rbufs, [], final=True)
        else:
            D(E.dma_start(out=dst_ap, in_=src_ap), rbufs, [], final=True)

    def col512(name, src_d):
        t = mk(name, [128, 4])
        dma_in(t.t[:, :], src_d.rearrange("(c p) -> p c", p=128), t.b, slow=True)
        return t

    stg = [mk("stg%d" % i, [128, INC]) for i in range(2)]
    ident = mk("ident", [128, 128]); dma_in(ident.t[:], c_ident[:, :], ident.b)
    pmp = mk("pmp", [128, 128]); dma_in(pmp.t[:], c_pmp[:, :], pmp.b)
    pms = mk("pms", [64, 64]); dma_in(pms.t[:], c_pms[:, :], pms.b)
    sel = mk("sel", [4, 512]); dma_in(sel.t[:], c_sel[:, :], sel.b)
    seqm = mk("seqm", [128, 64]); dma_in(seqm.t[:], c_seqm[:, :], seqm.b)
    snb = mk("snb", [4, 64]); dma_in(snb.t[:], c_snb[:, :], snb.b)
    oneh = mk("oneh", [64, 16]); dma_in(oneh.t[:], c_oh[:, :], oneh.b)
    ones4 = mk("ones4", [128, 128]); G(E.memset(ones4.t[:], 1.0), [], [ones4])

    convw = mk("convw", [128, 4, 4])
    for c in range(4):
        dma_in(convw.t[:, c, :], conv_w_d[:, c * 128:(c + 1) * 128].rearrange("j p -> p j"), convw.b, slow=True, join=True)
    convb = col512("convb", conv_b_d); lnh = col512("lnh", ln_head_d); skp = col512("skp", skip_d)
    s5D = col512("s5D", s5D_d); bglu = col512("bglu", bglu_d)
    gmix = mk("gmix", [128, 8]); dma_in(gmix.t[:, :], ln_mix_d.rearrange("(k p) -> p k", p=128), gmix.b, slow=True)
    big_ = mk("big_", [4, 1]); dma_in(big_.t[:, :], b_ig_d.rearrange("(h o) -> h o", o=1), big_.b, slow=True)
    bfg = mk("bfg", [4, 1]); dma_in(bfg.t[:, :], b_fg_d.rearrange("(h o) -> h o", o=1), bfg.b, slow=True)
    nbfg = mk("nbfg", [4, 1]); V(E.tensor_scalar(out=nbfg.t[:], in0=bfg.t[:], scalar1=-1.0, scalar2=None, op0=ALU.mult), [bfg], [nbfg])
    lnple = mk("lnple", [128, DM]); dma_in(lnple.t[:], lnple_d.rearrange("(o d) -> o d", o=1).partition_broadcast(128), lnple.b)
    lnfin = mk("lnfin", [128, DM]); dma_in(lnfin.t[:], lnfin_d.rearrange("(o d) -> o d", o=1).partition_broadcast(128), lnfin.b)
    epsc = mk("epsc", [128, 1]); G(E.memset(epsc.t[:], EPS), [], [epsc])

    wq = mk("wq", [128, 4, 128], BF16); wk = mk("wk", [128, 4, 128], BF16); wv = mk("wv", [128, 4, 128], BF16)


    def ld16(name, src):
        t = mk(name, [128, 16])
        dma_in(t.t[:, :], src.rearrange("(j g) q -> (g q) j", g=2), t.b, slow=True)
        return t
    lamre = ld16("lamre", lamre_d); lamim = ld16("lamim", lamim_d); logdt = ld16("logdt", logdt_d)
    s5tmp = [mk("s5tmp%d" % i, [128, 16]) for i in range(8)]
    lr = mk("lr", [128, 16]); dtt = mk("dtt", [128, 16]); mag = mk("mag", [128, 16])
    cs1 = mk("cs1", [128, 16]); sn1 = mk("sn1", [128, 16]); are = mk("are", [128, 16]); aim = mk("aim", [128, 16])
    gre = mk("gre", [128, 16]); gim = mk("gim", [128, 16])
    V(E.tensor_scalar(out=lr.t[:], in0=lamre.t[:], scalar1=-1e-4, scalar2=None, op0=ALU.min), [lamre], [lr])
    A(E.activation(out=dtt.t[:], in_=logdt.t[:], func=AF.Exp), [logdt], [dtt])
    t0, t1, t2, t3, t4, t5, t6, t7 = s5tmp
    V(E.tensor_tensor(out=t0.t[:], in0=lr.t[:], in1=dtt.t[:], op=ALU.mult), [lr, dtt], [t0])
    A(E.activation(out=mag.t[:], in_=t0.t[:], func=AF.Exp), [t0], [mag])
    th = mk("th", [128, 16])
    V(E.tensor_tensor(out=th.t[:], in0=lamim.t[:], in1=dtt.t[:], op=ALU.mult), [lamim, dtt], [th])
    ti32 = TL(p.sbuf("ti32", [128, 16], I32), Buf("ti32"))
    TWO_PI = 2.0 * math.pi

    def sin_reduced(dst, shift):
        V(E.tensor_scalar(out=t1.t[:], in0=th.t[:], scalar1=shift, scalar2=None, op0=ALU.add), [th], [t1])
        V(E.tensor_scalar(out=t2.t[:], in0=t1.t[:], scalar1=1.0 / TWO_PI, scalar2=0.5, op0=ALU.mult, op1=ALU.add), [t1], [t2])
        V(E.tensor_copy(out=ti32.t[:], in_=t2.t[:]), [t2], [ti32])
        V(E.tensor_copy(out=t3.t[:], in_=ti32.t[:]), [ti32], [t3])
        V(E.tensor_tensor(out=t4.t[:], in0=t3.t[:], in1=t2.t[:], op=ALU.is_gt), [t3, t2], [t4])
        V(E.tensor_tensor(out=t3.t[:], in0=t3.t[:], in1=t4.t[:], op=ALU.subtract), [t3, t4], [t3])
        V(E.scalar_tensor_tensor(out=t1.t[:], in0=t3.t[:], scalar=-TWO_PI, in1=t1.t[:], op0=ALU.mult, op1=ALU.add), [t3, t1], [t1])
        V(E.tensor_scalar(out=t1.t[:], in0=t1.t[:], scalar1=-math.pi, scalar2=math.pi, op0=ALU.max, op1=ALU.min), [t1], [t1])
        A(E.activation(out=dst.t[:], in_=t1.t[:], func=AF.Sin), [t1], [dst])
    sin_reduced(sn1, 0.0)
    sin_reduced(cs1, 0.5 * math.pi)
    V(E.tensor_tensor(out=are.t[:], in0=mag.t[:], in1=cs1.t[:], op=ALU.mult), [mag, cs1], [are])
    V(E.tensor_tensor(out=aim.t[:], in0=mag.t[:], in1=sn1.t[:], op=ALU.mult), [mag, sn1], [aim])
    V(E.tensor_tensor(out=t5.t[:], in0=lr.t[:], in1=lr.t[:], op=ALU.mult), [lr], [t5])
    V(E.tensor_tensor(out=t6.t[:], in0=lamim.t[:], in1=lamim.t[:], op=ALU.mult), [lamim], [t6])
    V(E.tensor_tensor(out=t5.t[:], in0=t5.t[:], in1=t6.t[:], op=ALU.add), [t5, t6], [t5])
    V(E.reciprocal(out=t5.t[:], in_=t5.t[:]), [t5], [t5])
    V(E.tensor_scalar(out=t6.t[:], in0=are.t[:], scalar1=-1.0, scalar2=None, op0=ALU.add), [are], [t6])
    V(E.tensor_tensor(out=t7.t[:], in0=t6.t[:], in1=lr.t[:], op=ALU.mult), [t6, lr], [t7])
    V(E.tensor_tensor(out=t0.t[:], in0=aim.t[:], in1=lamim.t[:], op=ALU.mult), [aim, lamim], [t0])
    V(E.tensor_tensor(out=t7.t[:], in0=t7.t[:], in1=t0.t[:], op=ALU.add), [t7, t0], [t7])
    V(E.tensor_tensor(out=gre.t[:], in0=t7.t[:], in1=t5.t[:], op=ALU.mult), [t7, t5], [gre])
    V(E.tensor_tensor(out=t7.t[:], in0=aim.t[:], in1=lr.t[:], op=ALU.mult), [aim, lr], [t7])
    V(E.tensor_tensor(out=t0.t[:], in0=t6.t[:], in1=lamim.t[:], op=ALU.mult), [t6, lamim], [t0])
    V(E.tensor_tensor(out=t7.t[:], in0=t7.t[:], in1=t0.t[:], op=ALU.subtract), [t7, t0], [t7])
    V(E.tensor_tensor(out=gim.t[:], in0=t7.t[:], in1=t5.t[:], op=ALU.mult), [t7, t5], [gim])

    CUT(2, gim)
    Ec = mk("Ec", [128, 16, 128]); Es = mk("Es", [128, 16, 128])
    pc = mk("pc", [128, 16]); psn = mk("psn", [128, 16])
    G(E.memset(Ec.t[:, :, 0:1], 1.0), [], [Ec])
    G(E.memset(Es.t[:, :, 0:1], 0.0), [], [Es])
    V(E.tensor_copy(out=pc.t[:], in_=cs1.t[:]), [cs1], [pc])
    V(E.tensor_copy(out=psn.t[:], in_=sn1.t[:]), [sn1], [psn])
    etmp = TL(stg[0].t[:, 0:1024].rearrange("p (j c) -> p j c", j=16), stg[0].b); etmp2 = TL(stg[1].t[:, 0:1024].rearrange("p (j c) -> p j c", j=16), stg[1].b)
    for k in range(7):
        L = 1 << k
        pcb = pc.t[:, :].unsqueeze(2).to_broadcast([128, 16, L])
        psb_ = psn.t[:, :].unsqueeze(2).to_broadcast([128, 16, L])
        V(E.tensor_tensor(out=etmp.t[:, :, 0:L], in0=Ec.t[:, :, 0:L], in1=pcb, op=ALU.mult), [Ec, pc], [etmp])
        V(E.tensor_tensor(out=etmp2.t[:, :, 0:L], in0=Es.t[:, :, 0:L], in1=psb_, op=ALU.mult), [Es, psn], [etmp2])
        V(E.tensor_tensor(out=Ec.t[:, :, L:2 * L], in0=etmp.t[:, :, 0:L], in1=etmp2.t[:, :, 0:L], op=ALU.subtract), [etmp, etmp2], [Ec])
        V(E.tensor_tensor(out=etmp.t[:, :, 0:L], in0=Ec.t[:, :, 0:L], in1=psb_, op=ALU.mult), [Ec, psn], [etmp])
        V(E.tensor_tensor(out=etmp2.t[:, :, 0:L], in0=Es.t[:, :, 0:L], in1=pcb, op=ALU.mult), [Es, pc], [etmp2])
        V(E.tensor_tensor(out=Es.t[:, :, L:2 * L], in0=etmp.t[:, :, 0:L], in1=etmp2.t[:, :, 0:L], op=ALU.add), [etmp, etmp2], [Es])
        if k < 6:
            V(E.tensor_tensor(out=t0.t[:], in0=pc.t[:], in1=pc.t[:], op=ALU.mult), [pc], [t0])
            V(E.tensor_tensor(out=t1.t[:], in0=psn.t[:], in1=psn.t[:], op=ALU.mult), [psn], [t1])
            V(E.tensor_tensor(out=t2.t[:], in0=pc.t[:], in1=psn.t[:], op=ALU.mult), [pc, psn], [t2])
            V(E.tensor_tensor(out=pc.t[:], in0=t0.t[:], in1=t1.t[:], op=ALU.subtract), [t0, t1], [pc])
            V(E.tensor_scalar(out=psn.t[:], in0=t2.t[:], scalar1=2.0, scalar2=None, op0=ALU.mult), [t2], [psn])

    CUT(3, TL(Es.t[:, 3, :], Es.b))
    BTre = mk("BTre", [128, 16, 128], BF16); BTim = mk("BTim", [128, 16, 128], BF16)
    CTre = mk("CTre", [128, 16, 128], BF16); CTimn = mk("CTimn", [128, 16, 128], BF16)
    padA = TL(stg[0].t[:, 0:2048].rearrange("p (j c) -> p j c", j=16), stg[0].b)
    padB = TL(stg[1].t[:, 0:2048].rearrange("p (j c) -> p j c", j=16), stg[1].b)
    ytile = mk("ytile", [128, 1088])
    btmp = TL(ytile.t[:, 0:128], ytile.b); btmp2 = TL(ytile.t[:, 128:256], Buf("btmp2"))
    G(E.memset(stg[0].t[:, 0:2048], 0.0), [], [stg[0]])
    G(E.memset(stg[1].t[:, 0:2048], 0.0), [], [stg[1]])
    Brv = Bre_d.rearrange("(m r) q c -> r q m c", r=8)
    Biv = Bim_d.rearrange("(m r) q c -> r q m c", r=8)
    for q in range(4):
        for g2 in range(2):
            r = 2 * q + g2
            off = 32 * q + 16 * g2
            dma_in(padA.t[g2 * 64:(g2 + 1) * 64, q::4, off:off + 16], Brv[r], stg[0].b, slow=True, join=True)
            dma_in(padB.t[g2 * 64:(g2 + 1) * 64, q::4, off:off + 16], Biv[r], stg[1].b, slow=True, join=True)
    for j in range(16):
        V(E.tensor_scalar(out=btmp.t[:], in0=padB.t[:, j, :], scalar1=gim.t[:, j:j + 1], scalar2=None, op0=ALU.mult), [stg[1], gim], [btmp])
        V(E.scalar_tensor_tensor(out=btmp.t[:], in0=padA.t[:, j, :], scalar=gre.t[:, j:j + 1], in1=btmp.t[:], op0=ALU.mult, op1=ALU.subtract), [stg[0], gre, btmp], [btmp])
        ps = PS()
        T(E.transpose(out=ps.t[:, 0:128], in_=btmp.t[:], identity=ident.t[:]), [btmp, ident], [ps])
        A(E.activation(out=BTre.t[:, j, :], in_=ps.t[:, 0:128], func=AF.Copy), [ps], [BTre])
        V(E.tensor_scalar(out=btmp2.t[:], in0=padA.t[:, j, :], scalar1=gim.t[:, j:j + 1], scalar2=None, op0=ALU.mult), [stg[0], gim], [btmp2])
        V(E.scalar_tensor_tensor(out=btmp2.t[:], in0=padB.t[:, j, :], scalar=gre.t[:, j:j + 1], in1=btmp2.t[:], op0=ALU.mult, op1=ALU.add), [stg[1], gre, btmp2], [btmp2])
        ps = PS()
        T(E.transpose(out=ps.t[:, 0:128], in_=btmp2.t[:], identity=ident.t[:]), [btmp2, ident], [ps])
        A(E.activation(out=BTim.t[:, j, :], in_=ps.t[:, 0:128], func=AF.Copy), [ps], [BTim])
    CUT(4, btmp2)
    G(E.memset(stg[0].t[:, 0:2048], 0.0), [stg[0]], [stg[0]])
    G(E.memset(stg[1].t[:, 0:2048], 0.0), [stg[1]], [stg[1]])
    Crv = Cre_d.rearrange("(m r) c q -> r c m q", r=8)
    Civ = Cim_d.rearrange("(m r) c q -> r c m q", r=8)
    for q in range(4):
        for g2 in range(2):
            r = 2 * q + g2
            off = 32 * q + 16 * g2
            dma_in(padA.t[off:off + 16, q::4, g2 * 64:(g2 + 1) * 64], Crv[r], stg[0].b, slow=True, join=True)
            dma_in(padB.t[off:off + 16, q::4, g2 * 64:(g2 + 1) * 64], Civ[r], stg[1].b, slow=True, join=True)
    for j in range(16):
        ps = PS()
        T(E.transpose(out=ps.t[:, 0:128], in_=padA.t[:, j, :], identity=ident.t[:]), [stg[0], ident], [ps])
        A(E.activation(out=CTre.t[:, j, :], in_=ps.t[:, 0:128], func=AF.Copy), [ps], [CTre])
        ps = PS()
        T(E.transpose(out=ps.t[:, 0:128], in_=padB.t[:, j, :], identity=ident.t[:]), [stg[1], ident], [ps])
        A(E.activation(out=CTimn.t[:, j, :], in_=ps.t[:, 0:128], func=AF.Copy, scale=-1.0), [ps], [CTimn])

    CUT(5, TL(stg[1].t[:, 0:16], stg[1].b))
    win = mk("win", [128, 8, INC], BF16)
    wout = mk("wout", [128, 8, DM], BF16); wgate = mk("wgate", [128, 8, DM], BF16)
    wple = mk("wple", [128, 2, DM], BF16); wglu = mk("wglu", [128, 4, 512], BF16)
    si = [0]

    def load_cast(dst_ap, src_ap, ncols, dstb, scale_ap=None, scale_b=None):
        s = stg[si[0] % 2]
        si[0] += 1
        dma_in(s.t[:, 0:ncols], src_ap, s.b)
        if scale_ap is not None:
            A(E.activation(out=dst_ap, in_=s.t[:, 0:ncols], func=AF.Copy, scale=scale_ap), [s, scale_b], [dstb])
        elif si[0] % 3 == 0:
            A(E.activation(out=dst_ap, in_=s.t[:, 0:ncols], func=AF.Copy), [s], [dstb])
        elif si[0] % 3 == 1:
            V(E.tensor_copy(out=dst_ap, in_=s.t[:, 0:ncols]), [s], [dstb])
        else:
            G(E.tensor_copy(out=dst_ap, in_=s.t[:, 0:ncols]), [s], [dstb])
    for k in range(8):
        load_cast(win.t[:, k, :], w_in_d[k * 128:(k + 1) * 128, :], INC, win.b, gmix.t[:, k:k + 1], gmix)
    for k in range(8):
        load_cast(wout.t[:, k, :], wout_d[k * 128:(k + 1) * 128, :], DM, wout.b)
        load_cast(wgate.t[:, k, :], wgate_d[k * 128:(k + 1) * 128, :], DM, wgate.b)
    for k in range(2):
        load_cast(wple.t[:, k, :], wple_d[k * 128:(k + 1) * 128, :], DM, wple.b)
    for k in range(4):
        load_cast(wglu.t[:, k, :], wglu_d[k * 128:(k + 1) * 128, :], 512, wglu.b)

    CUT(6, TL(stg[1].t[:, 0:16], stg[1].b))
    for wt_, src_, sc_ in ((wq, wq_d, 1.0), (wk, wk_d, float(128 ** -0.5)), (wv, wv_d, 1.0)):
        s_ = stg[si[0] % 2]
        si[0] += 1
        dma_in(s_.t[:, 0:512].rearrange("p (h e) -> p h e", h=4), src_.rearrange("h d e -> d h e"), s_.b)
        V(E.tensor_scalar(out=wt_.t[:, :, :], in0=s_.t[:, 0:512].rearrange("p (h e) -> p h e", h=4), scalar1=sc_, scalar2=None, op0=ALU.mult), [s_], [wt_])
    Caug = [mk("Caug%d" % h, [128, 129]) for h in range(4)]
    for h in range(4):
        V(E.memset(Caug[h].t[:], 0.0), [], [Caug[h]])
    Bn_c = mk("Bn_c", [4, 1]); M_c = mk("M_c", [4, 1])
    V(E.memset(Bn_c.t[:], 0.0), [], [Bn_c])
    V(E.memset(M_c.t[:], 0.0), [], [M_c])
    azin = mk("azin", [128, 16, 2])
    V(E.memset(azin.t[:], 0.0), [], [azin])
    xl_re = mk("xl_re", [128, 16]); xl_im = mk("xl_im", [128, 16])
    xmh = mk("xmh", [128, 4, 131])
    V(E.memset(xmh.t[:], 0.0), [], [xmh])

    def alias(name, ap, like):
        return TL(ap, Buf(name, like=like))
    xtok = alias("xtok", stg[0].t[:, 0:1024], [stg[0].b])
    h2 = alias("h2", stg[0].t[:, 1024:2048], [stg[0].b])
    sgate = alias("sgate", stg[1].t[:, 0:1024], [stg[1].b])
    esb = alias("esb", stg[1].t[:, 1024:2048], [stg[1].b])
    scr = ytile
    ptok = alias("ptok", stg[0].t[:, 2048:2304], [stg[0].b])
    aT = mk("aT", [128, 8, 128], BF16); pT = mk("pT", [128, 2, 128], BF16)
    szm = mk("szm", [128, 4, 128]); so = mk("so", [128, 512])
    tm = TL(so.t, so.b)
    xs5 = mk("xs5", [128, 4, 128]); xs5b = mk("xs5b", [128, 4, 128], BF16); szs = mk("szs", [128, 4, 128])
    xc = mk("xc", [128, 4, 128])
    xms = mk("xms", [128, 4, 16, 7]); xmc = TL(xmh.t[:, :, 0:64], xmh.b)
    qT = mk("qT", [128, 2, 128], BF16, nb=2); qTd = mk("qTd", [128, 2, 128], BF16, nb=2); kT = mk("kT", [128, 2, 128], BF16, nb=2)
    kw = mk("kw", [128, 2, 128], BF16, nb=2); vaug = mk("vaug", [128, 2, 130], BF16, nb=2)
    xcb = mk("xcb", [128, 4, 128], BF16); xmb_ = mk("xmb_", [128, 4, 128], BF16)
    sdb = [mk("sdb%d" % i, [128, 128], BF16) for i in range(2)]
    Cb = [mk("Cb%d" % h, [128, 130], BF16) for h in range(4)]
    for h in range(4):
        V(E.memset(Cb[h].t[:], 0.0), [], [Cb[h]])
    V(E.memset(vaug.t[:], 1.0), [], [vaug])
    narg = [alias("narg%d" % i, stg[0].t[:, 2304 + i * 128:2432 + i * 128], [stg[0].b]) for i in range(2)]
    DTt = narg
    SDt = sdb
    hA = mk("hA", [128, 2, 128], nb=2); hn = [alias("hn%d" % i, stg[1].t[:, 2048 + i * 128:2176 + i * 128], [stg[1].b]) for i in range(2)]
    mtmp = [alias("mtmp%d" % i, stg[1].t[:, 2304 + i * 128:2432 + i * 128], [stg[1].b]) for i in range(2)]
    mix = mk("mix", [128, 8, 128], BF16, nb=8)
    bnst = mk("bnst", [128, 4, 6]); mv = mk("mv", [128, 4, 2]); rstdh = mk("rstdh", [128, 4])
    den = mk("den", [128, 4]); rden = mk("rden", [128, 4]); dl = mk("dl", [128, 4])
    colq = mk("colq", [128, 12])
    ssq = mk("ssq", [128, 4]); rstd = mk("rstd", [128, 4])
    gi = mk("gi", [4, 128]); gsp = mk("gsp", [4, 128]); gBn = mk("gBn", [4, 128]); ga = gi
    gM = mk("gM", [4, 128]); gdec = mk("gdec", [4, 128]); gwl = mk("gwl", [4, 128]); gem = mk("gem", [4, 128])
    gtmp = mk("gtmp", [4, 128]); ga2 = gtmp; gneg = mk("gneg", [4, 1]); mnew = mk("mnew", [4, 16])
    m0c = mk("m0c", [4, 16]); m0row = mk("m0row", [4, 64])
    NS5 = 2
    s5P = [mk("s5P%d" % r, [128, 512]) for r in range(NS5)]
    s5W = [TL(hA.t[:, :, :].rearrange("p a b -> p (a b)"), hA.b)] + [mk("s5W%d" % r, [128, 256]) for r in range(1, NS5)]
    s5Z = [mk("s5Z%d" % r, [128, 256]) for r in range(NS5)]
    s5xb = [mk("s5xb%d" % r, [128, 256], BF16) for r in range(NS5)]
    rmat = [mk("rmat%d" % r, [128, 256]) for r in range(NS5)]
    ysb = mk("ysb", [128, 4, 128], nb=4); yg = ysb; ygb = mk("ygb", [128, 4, 128], BF16)
    gl1 = [mk("gl1_%d" % i, [128, 128]) for i in range(2)]; gl2 = gl1
    h2T = mk("h2T", [128, 8, 128], BF16)
    qz = ytile
    kwm = mk("kwm", [64, 8, 128], BF16); wlm = mk("wlm", [64, 16])
    Cst = Caug
    h0tmp = TL(ytile.t[:, 0:256].rearrange("p (j b) -> p j b", j=16), ytile.b)
    ah0j = [mk("ah0j", [128, 2, 16])] * NS5
    naim = mk("naim", [128, 16])
    sore = mk("sore", [128, 16, 16]); soim = mk("soim", [128, 16, 16])
    nT = mk("nT", [128, 64]); nOut = nT
    cnt = {"n": 0}
    CUT(7, TL(xmh.t[:, 0, :], xmh.b))

    def tile_src(kind, ti):
        if kind == "s":
            return xs_d[:, :], psm_d[:, :], 64
        return xp_d[ti * 128:(ti + 1) * 128, :], pp_d[ti * 128:(ti + 1) * 128, :], 128

    def load_inputs(kind, ti):
        xs_, ps_, n_ = tile_src(kind, ti)
        dma_in(xtok.t[0:n_, :], xs_, xtok.b)
        dma_in(ptok.t[0:n_, :], ps_, ptok.b)

    def front(kind, ti, nxt=None):
        smp = kind == "s"
        NT = 64 if smp else 128
        A(E.activation(out=scr.t[0:NT, 0:DM], in_=xtok.t[0:NT, :], func=AF.Square, accum_out=ssq.t[0:NT, 0:1]), [xtok], [scr, ssq])
        A(E.activation(out=rstd.t[0:NT, 0:1], in_=ssq.t[0:NT, 0:1], func=AF.Sqrt, bias=epsc.t[0:NT, :], scale=1.0 / DM), [ssq, epsc], [rstd])
        V(E.reciprocal(out=rstd.t[0:NT, 0:1], in_=rstd.t[0:NT, 0:1]), [rstd], [rstd])
        V(E.tensor_scalar(out=scr.t[0:NT, 0:DM], in0=xtok.t[0:NT, :], scalar1=rstd.t[0:NT, 0:1], scalar2=None, op0=ALU.mult), [xtok, rstd], [scr])
        for half in range(2):
            ps = PS()
            for kk in range(4):
                k = half * 4 + kk
                T(E.transpose(out=ps.t[:, kk * 128:kk * 128 + NT], in_=scr.t[0:NT, k * 128:(k + 1) * 128], identity=ident.t[0:NT, 0:NT]), [scr, ident], [ps])
            A(E.activation(out=aT.t[:, half * 4:half * 4 + 4, 0:NT], in_=ps.t[:, :].rearrange("p (k t) -> p k t", k=4)[:, :, 0:NT], func=AF.Copy), [ps], [aT])
        ps = PS()
        for k in range(2):
            T(E.transpose(out=ps.t[:, k * 128:k * 128 + NT], in_=ptok.t[0:NT, k * 128:(k + 1) * 128], identity=ident.t[0:NT, 0:NT]), [ptok, ident], [ps])
        V(E.tensor_copy(out=pT.t[:, :, 0:NT], in_=ps.t[:, 0:256].rearrange("p (k t) -> p k t", k=2)[:, :, 0:NT]), [ps], [pT])

        if nxt is not None:
            load_inputs(*nxt)

    def do_tile(kind, ti, carry=(), nxt=None, glu_prev=None):
        smp = kind == "s"
        NT = 64 if smp else 128
        x_src = xs_d[:, :] if smp else xp_d[ti * 128:(ti + 1) * 128, :]
        p_src = psm_d[:, :] if smp else pp_d[ti * 128:(ti + 1) * 128, :]
        y_dst = ys_o[:, :] if smp else yp_o[ti * 128:(ti + 1) * 128, :]
        psM = psb[5]; psD = psb[6]
        def proj_fm(col0):
            ps = PS()
            for k in range(8):
                T(E.matmul(ps.t[:, 0:NT], win.t[:, k, col0:col0 + 128], aT.t[:, k, 0:NT], start=(k == 0), stop=(k == 7)), [win, aT], [ps])
            return ps

        def g_block(col0, kind):
            ps = PS()
            for k in range(8):
                T(E.matmul(ps.t[0:NT, :], aT.t[:, k, 0:NT], win.t[:, k, col0:col0 + 512], start=(k == 0), stop=(k == 7)), [win, aT], [ps])
                yield
            if kind == "om":
                A(E.activation(out=so.t[0:NT, :], in_=ps.t[0:NT, :], func=AF.Sigmoid), [ps], [so])
                return
            A(E.activation(out=tm.t[0:NT, :], in_=ps.t[0:NT, :], func=AF.Copy), [ps], [tm])
            yield
            pt = PS()
            for c in range(4):
                T(E.transpose(out=pt.t[:, c * 128:c * 128 + NT], in_=tm.t[0:NT, c * 128:(c + 1) * 128], identity=ident.t[0:NT, 0:NT]), [tm, ident], [pt])
            ptv = pt.t[:, :].rearrange("p (c t) -> p c t", c=4)[:, :, 0:NT]
            if kind == "xm":
                if smp:
                    A(E.activation(out=xmc.t[:, :, 0:NT], in_=ptv, func=AF.Copy), [pt], [xmc])
                    G(E.tensor_copy(out=xmb_.t[:, :, 0:NT], in_=xmc.t[:, :, 0:NT]), [xmc], [xmb_])
                else:
                    A(E.activation(out=xmh.t[:, :, 3:3 + NT], in_=ptv, func=AF.Copy), [pt], [xmh])
                    G(E.tensor_copy(out=xmb_.t[:, :, 0:NT], in_=xmh.t[:, :, 3:3 + NT]), [xmh], [xmb_])
            elif kind == "zm":
                A(E.activation(out=szm.t[:, :, 0:NT], in_=ptv, func=AF.Silu), [pt], [szm])
            elif kind == "xs":
                A(E.activation(out=xs5.t[:, :, 0:NT], in_=ptv, func=AF.Copy), [pt], [xs5])
                V(E.tensor_copy(out=xs5b.t[:, :, 0:NT], in_=xs5.t[:, :, 0:NT]), [xs5], [xs5b])
            elif kind == "zs":
                A(E.activation(out=szs.t[:, :, 0:NT], in_=ptv, func=AF.Silu), [pt], [szs])

        def g_gates():
            psg = PS()
            for k in range(8):
                T(E.matmul(psg.t[0:4, 0:NT], win.t[:, k, 1536:1540], aT.t[:, k, 0:NT], start=(k == 0), stop=(k == 7)), [win, aT], [psg])
            for k in range(8):
                T(E.matmul(psg.t[0:4, 128:128 + NT], win.t[:, k, 1540:1544], aT.t[:, k, 0:NT], start=(k == 0), stop=(k == 7)), [win, aT], [psg])
            A(E.activation(out=gi.t[:, 0:NT], in_=psg.t[0:4, 0:NT], func=AF.Identity, bias=big_.t[:, :], scale=1.0), [psg, big_], [gi])
            A(E.activation(out=gsp.t[:, 0:NT], in_=psg.t[0:4, 128:128 + NT], func=AF.Exp, bias=nbfg.t[:, :], scale=-1.0), [psg, nbfg], [gsp])
        def g_eproj():
            pse = [PS(), PS()]
            for half in range(2):
                for k in range(2):
                    T(E.matmul(pse[half].t[0:NT, :], pT.t[:, k, 0:NT], wple.t[:, k, half * 512:(half + 1) * 512], start=(k == 0), stop=(k == 1)), [pT, wple], [pse[half]])
                A(E.activation(out=esb.t[0:NT, half * 512:(half + 1) * 512], in_=pse[half].t[0:NT, :], func=AF.Square, accum_out=ssq.t[0:NT, 1 + half:2 + half]), [pse[half]], [esb, ssq])
            V(E.tensor_tensor(out=ssq.t[0:NT, 1:2], in0=ssq.t[0:NT, 1:2], in1=ssq.t[0:NT, 2:3], op=ALU.add), [ssq], [ssq])
            A(E.activation(out=rstd.t[0:NT, 1:2], in_=ssq.t[0:NT, 1:2], func=AF.Sqrt, bias=epsc.t[0:NT, :], scale=1.0 / DM), [ssq, epsc], [rstd])
            V(E.reciprocal(out=rstd.t[0:NT, 1:2], in_=rstd.t[0:NT, 1:2]), [rstd], [rstd])
            for half in range(2):
                sl = slice(half * 512, (half + 1) * 512)
                V(E.scalar_tensor_tensor(out=esb.t[0:NT, sl], in0=pse[half].t[0:NT, :], scalar=rstd.t[0:NT, 1:2], in1=lnple.t[0:NT, sl], op0=ALU.mult, op1=ALU.mult), [pse[half], rstd, lnple], [esb])

        for _ in g_block(1544, "xs"):
            pass
        own = [(lambda: g_block(0, "xm"), None), (g_gates, None), (lambda: g_block(512, "zm"), None),
               (lambda: g_block(2056, "zs"), None), (lambda: g_block(1024, "om"), None)]
        pending = own
        cin = list(carry) + [g_eproj]
        if nxt is not None:
            cin.append(lambda: front(*nxt))

        def pop_carry(n=1):
            for _ in range(n):
                if cin:
                    cin.pop(0)()

        def pop_pending(background=True):
            if pending:
                f, a = pending.pop(0)
                r_ = f() if a is None else f(a)
                if r_ is not None:
                    if background:
                        bg.append(r_)
                    else:
                        for _ in r_:
                            pass

        def gates_gen():
            A(E.activation(out=gsp.t[:, 0:NT], in_=gsp.t[:, 0:NT], func=AF.Ln, bias=1.0), [gsp], [gsp])
            if smp:
                V(E.tensor_tensor_scan(out=gBn.t[:, 0:NT], data0=seqm.t[0:4, :], data1=gsp.t[:, 0:NT], initial=0.0, op0=ALU.mult, op1=ALU.add), [seqm, gsp], [gBn])
                yield
            else:
                V(E.tensor_tensor_scan(out=gBn.t[:, 0:NT], data0=ones4.t[0:4, 0:NT], data1=gsp.t[:, 0:NT], initial=Bn_c.t[:, :], op0=ALU.mult, op1=ALU.add), [ones4, gsp, Bn_c], [gBn])
                yield
            V(E.tensor_tensor(out=ga.t[:, 0:NT], in0=gi.t[:, 0:NT], in1=gBn.t[:, 0:NT], op=ALU.add), [gi, gBn], [ga])
            yield
            if smp:
                V(E.tensor_copy(out=ga2.t[:, 0:NT], in_=ga.t[:, 0:NT]), [ga], [ga2])
                yield
                a2v = ga2.t[:, 0:64].rearrange("h (b t) -> h b t", t=4)
                V(E.tensor_tensor(out=a2v[:, :, 0:1], in0=a2v[:, :, 0:1], in1=m0c.t[:, :].unsqueeze(2), op=ALU.max), [ga2, m0c], [ga2])
                yield
                V(E.tensor_tensor_scan(out=gM.t[:, 0:NT], data0=snb.t[:, :], data1=ga2.t[:, 0:NT], initial=0.0, op0=ALU.add, op1=ALU.max), [snb, ga2], [gM])
                yield
                V(E.tensor_tensor(out=gtmp.t[:, 0:NT], in0=m0row.t[:, :], in1=gM.t[:, 0:NT], op=ALU.subtract), [m0row, gM], [gtmp])
                yield
                A(E.activation(out=gdec.t[:, 0:NT], in_=gtmp.t[:, 0:NT], func=AF.Exp), [gtmp], [gdec])
                yield
                Mv = gM.t[:, 0:64].rearrange("h (b t) -> h b t", t=4)
                V(E.tensor_tensor(out=gtmp.t[:, 0:64].rearrange("h (b t) -> h b t", t=4), in0=ga.t[:, 0:64].rearrange("h (b t) -> h b t", t=4),
                                            in1=Mv[:, :, 3:4].to_broadcast([4, 16, 4]), op=ALU.subtract), [ga, gM], [gtmp])
                A(E.activation(out=gwl.t[:, 0:NT], in_=gtmp.t[:, 0:NT], func=AF.Exp), [gtmp], [gwl])
                yield
                V(E.tensor_tensor(out=mnew.t[:, :].unsqueeze(2), in0=Mv[:, :, 3:4], in1=gBn.t[:, 0:64].rearrange("h (b t) -> h b t", t=4)[:, :, 3:4], op=ALU.subtract), [gM, gBn], [mnew])
                yield
            else:
                V(E.tensor_tensor_scan(out=gM.t[:, 0:NT], data0=ones4.t[0:4, 0:NT], data1=ga.t[:, 0:NT], initial=M_c.t[:, :], op0=ALU.mult, op1=ALU.max), [ones4, ga, M_c], [gM])
                yield
                A(E.activation(out=gdec.t[:, 0:NT], in_=gM.t[:, 0:NT], func=AF.Exp, bias=M_c.t[:, :], scale=-1.0), [gM, M_c], [gdec])
                yield
                V(E.tensor_scalar(out=gneg.t[:, :], in0=gM.t[:, NT - 1:NT], scalar1=-1.0, scalar2=None, op0=ALU.mult), [gM], [gneg])
                yield
                A(E.activation(out=gwl.t[:, 0:NT], in_=ga.t[:, 0:NT], func=AF.Exp, bias=gneg.t[:, :], scale=1.0), [ga, gneg], [gwl])
                yield
                V(E.tensor_tensor(out=mnew.t[:, 0:1], in0=gM.t[:, NT - 1:NT], in1=gBn.t[:, NT - 1:NT], op=ALU.subtract), [gM, gBn], [mnew])
                yield
            V(E.tensor_tensor(out=gtmp.t[:, 0:NT], in0=gBn.t[:, 0:NT], in1=gM.t[:, 0:NT], op=ALU.subtract), [gBn, gM, gwl], [gtmp])
            yield
            A(E.activation(out=gem.t[:, 0:NT], in_=gtmp.t[:, 0:NT], func=AF.Exp), [gtmp], [gem])
            yield
            if not smp:
                V(E.tensor_copy(out=Bn_c.t[:, :], in_=gBn.t[:, NT - 1:NT]), [gBn], [Bn_c])
                yield
                V(E.tensor_copy(out=M_c.t[:, :], in_=gM.t[:, NT - 1:NT]), [gM], [M_c])
                yield
            psc = PS()
            for qi, gt in enumerate((ga, gwl, gem)):
                T(E.transpose(out=psc.t[0:NT, qi * 4:qi * 4 + 4], in_=gt.t[:, 0:NT], identity=ident.t[0:4, 0:4]), [gt, ident], [psc])
                yield
            V(E.tensor_copy(out=colq.t[0:NT, :], in_=psc.t[0:NT, 0:12]), [psc], [colq])
            yield
            for h in range(4):
                T(E.matmul(psM.t[:, h * 128:h * 128 + NT], sel.t[:, h * 128:(h + 1) * 128], gM.t[:, 0:NT], start=True, stop=True), [sel, gM], [psM])
                yield
            for h in range(4):
                T(E.matmul(psD.t[:, h * 128:h * 128 + NT], sel.t[:, h * 128:(h + 1) * 128], gdec.t[:, 0:NT], start=True, stop=True), [sel, gdec], [psD])
                yield


        def conv_gen():
            for c in range(4):
                if smp:
                    G(E.tensor_copy(out=xms.t[:, c, :, 3:7], in_=xmc.t[:, c, :].rearrange("p (b t) -> p b t", t=4)), [xmc], [xms])
                    yield
                    src = lambda j, c=c: xms.t[:, c, :, j:j + 4]
                    dstv = xc.t[:, c, 0:64].rearrange("p (b t) -> p b t", t=4)
                    rb = [xms]
                else:
                    src = lambda j, c=c: xmh.t[:, c, j:j + NT]
                    dstv = xc.t[:, c, 0:NT]
                    rb = [xmh]
                V(E.tensor_scalar(out=dstv, in0=src(0), scalar1=convw.t[:, c, 0:1], scalar2=convb.t[:, c:c + 1], op0=ALU.mult, op1=ALU.add), rb + [convw, convb], [xc])
                yield
                for j in range(1, 4):
                    V(E.scalar_tensor_tensor(out=dstv, in0=src(j), scalar=convw.t[:, c, j:j + 1], in1=dstv, op0=ALU.mult, op1=ALU.add), rb + [convw, xc], [xc])
                    yield
                A(E.activation(out=xc.t[:, c, 0:NT], in_=xc.t[:, c, 0:NT], func=AF.Silu), [xc], [xc])
                yield
                G(E.tensor_copy(out=xcb.t[:, c, 0:NT], in_=xc.t[:, c, 0:NT]), [xc], [xcb])
                yield
            if smp:
                G(E.memset(qz.t[:, :], 0.0), [], [qz])
                yield

        def pr(ap, c):
            if smp:
                return ap.rearrange("p (c b t) -> p c b t", c=c, t=4)
            return ap.rearrange("p (c t) -> p c t", c=c)

        def tb(tab, j):
            if smp:
                return tab.t[:, j, 0:4].unsqueeze(1).unsqueeze(1).to_broadcast([128, 2, 16, 4])
            return tab.t[:, j, 0:NT].unsqueeze(1).to_broadcast([128, 2, NT])
        N2, N3, N4 = 2 * NT, 3 * NT, 4 * NT
        def mlstm_head(h):
                xmcur = xmb_.t[:, h, 0:NT]
                xmb = xmb_
                ps = PS()
                T(E.matmul(ps.t[:, 0:NT], wq.t[:, h, :], xcb.t[:, h, 0:NT], start=True, stop=True), [wq, xcb], [ps])
                T(E.matmul(ps.t[:, 128:128 + NT], wk.t[:, h, :], xcb.t[:, h, 0:NT], start=True, stop=True), [wk, xcb], [ps])
                psk = PS()
                T(E.matmul(psk.t[0:NT, 256:384], xcb.t[:, h, 0:NT], wk.t[:, h, :], start=True, stop=True), [wk, xcb], [psk])
                T(E.matmul(psk.t[0:NT, 384:512], xmcur, wv.t[:, h, :], start=True, stop=True), [wv, xmb], [psk])
                A(E.activation(out=qT.t[:, h % 2, 0:NT], in_=ps.t[:, 0:NT], func=AF.Copy), [ps], [qT.b[h % 2]])
                A(E.activation(out=kT.t[:, h % 2, 0:NT], in_=ps.t[:, 128:128 + NT], func=AF.Copy), [ps], [kT.b[h % 2]])
                V(E.tensor_copy(out=vaug.t[0:NT, h % 2, 0:128], in_=psk.t[0:NT, 384:512]), [psk], [vaug.b[h % 2]])
                V(E.tensor_tensor(out=qTd.t[:, h % 2, 0:NT], in0=qT.t[:, h % 2, 0:NT], in1=psD.t[:, h * 128:h * 128 + NT], op=ALU.mult), [qT.b[h % 2], psD], [qTd.b[h % 2]])
                if smp:
                    V(E.tensor_scalar(out=wlm.t[:, :], in0=oneh.t[:, :], scalar1=colq.t[0:64, 4 + h:5 + h], scalar2=None, op0=ALU.mult), [oneh, colq], [wlm])
                    V(E.tensor_copy(out=kw.t[0:64, h % 2, :], in_=psk.t[0:64, 256:384]), [psk], [kw.b[h % 2]])
                else:
                    V(E.tensor_scalar(out=kw.t[0:NT, h % 2, :], in0=psk.t[0:NT, 256:384], scalar1=colq.t[0:NT, 4 + h:5 + h], scalar2=None, op0=ALU.mult), [psk, colq], [kw.b[h % 2]])
                yield

                r2 = cnt["n"] % 2
                cnt["n"] += 1
                na, dtt_, sd, hn_, mt = narg[r2], DTt[r2], SDt[r2], hn[r2], mtmp[r2]
                pmask = pms if smp else pmp
                V(E.scalar_tensor_tensor(out=na.t[0:NT, 0:NT], in0=psM.t[0:NT, h * 128:h * 128 + NT], scalar=colq.t[0:NT, h:h + 1],
                                                                           in1=pmask.t[0:NT, 0:NT], op0=ALU.subtract, op1=ALU.add), [psM, colq, pmask], [na])
                A(E.activation(out=dtt_.t[0:NT, 0:NT], in_=na.t[0:NT, 0:NT], func=AF.Exp, scale=-1.0), [na], [dtt_])
                yield
                ps2 = PS()
                T(E.matmul(ps2.t[0:NT, 0:NT], kT.t[:, h % 2, 0:NT], qT.t[:, h % 2, 0:NT], start=True, stop=True), [kT.b[h % 2], qT.b[h % 2]], [ps2])
                V(E.tensor_tensor(out=sd.t[0:NT, 0:NT], in0=ps2.t[0:NT, 0:NT], in1=dtt_.t[0:NT, 0:NT], op=ALU.mult), [ps2, dtt_], [sd])
                V(E.tensor_copy(out=dl.t[:, h:h + 1], in_=psD.t[:, h * 128 + NT - 1:h * 128 + NT]), [psD], [dl])
                yield
                psn_ = psb[7] if h % 2 == 0 else psb[4]
                if not smp:
                    T(E.matmul(psn_.t[0:NT, 0:129], sd.t[0:NT, 0:NT], vaug.t[0:NT, h % 2, 0:129], start=True, stop=False), [sd, vaug.b[h % 2]], [psn_])
                    T(E.matmul(psn_.t[0:NT, 0:129], qTd.t[:, h % 2, 0:NT], Cb[h].t[:, 0:129], start=False, stop=True), [qTd.b[h % 2], Cb[h]], [psn_])
                    psu = PS()
                    T(E.matmul(psu.t[:, 0:129], kw.t[0:NT, h % 2, :], vaug.t[0:NT, h % 2, 0:129], start=True, stop=True), [kw.b[h % 2], vaug.b[h % 2]], [psu])
                    V(E.scalar_tensor_tensor(out=Caug[h].t[:, :], in0=Caug[h].t[:, :], scalar=dl.t[:, h:h + 1], in1=psu.t[:, 0:129], op0=ALU.mult, op1=ALU.add), [Caug[h], dl, psu], [Caug[h]])
                    A(E.activation(out=Cb[h].t[:, 0:129], in_=Caug[h].t[:, :], func=AF.Copy), [Caug[h]], [Cb[h]])
                else:
                    V(E.tensor_copy(out=qz.t[:, :].rearrange("p (b x) -> p b x", x=68)[:, :, 0:4], in_=qTd.t[:, h % 2, 0:64].rearrange("p (b t) -> p b t", t=4)), [qTd.b[h % 2]], [qz])
                    T(E.matmul(psn_.t[0:NT, 0:129], sd.t[0:NT, 0:NT], vaug.t[0:NT, h % 2, 0:129], start=True, stop=False), [sd, vaug.b[h % 2]], [psn_])
                    for b in range(SB):
                        if b % 8 == 0:
                            V(E.tensor_tensor(out=kwm.t[:, :, :], in0=kw.t[0:64, h % 2, :].unsqueeze(1).to_broadcast([64, 8, 128]),
                                              in1=wlm.t[:, b:b + 8].unsqueeze(2).to_broadcast([64, 8, 128]), op=ALU.mult), [kw.b[h % 2], wlm], [kwm])
                        cs = Cst[(b + h * SB) % 4]
                        cn = cs

                        def fetch_state(b_):
                            cs_ = Cst[(b_ + h * SB) % 4]
                            dma_in(cs_.t[:, 0:128], sC_d[b_, h, :, :], cs_.b)
                            A(E.activation(out=cs_.t[:, 128:129], in_=nT.t[:, b_ * 4 + h:b_ * 4 + h + 1], func=AF.Copy), [nT], [cs_], join=True)
                        if b == 0:
                            fetch_state(0)
                        if b + 1 < SB:
                            fetch_state(b + 1)
                        T(E.matmul(psn_.t[0:NT, 0:129], qz.t[:, b * 64:b * 64 + 64], cs.t[:, :], start=False, stop=(b == SB - 1)), [qz, cs], [psn_])
                        psu = PS()
                        T(E.matmul(psu.t[:, 0:129], kwm.t[:, b % 8, :], vaug.t[0:64, h % 2, 0:129], start=True, stop=True), [kwm, vaug.b[h % 2]], [psu])
                        V(E.scalar_tensor_tensor(out=cn.t[:, :], in0=cs.t[:, :], scalar=psD.t[:, h * 128 + 4 * b + 3:h * 128 + 4 * b + 4], in1=psu.t[:, 0:129],
                                                                                           op0=ALU.mult, op1=ALU.add), [cs, psD, psu], [cn])
                        A(E.activation(out=nOut.t[:, b * 4 + h:b * 4 + h + 1], in_=cn.t[:, 128:129], func=AF.Copy), [cn], [nOut])
                        D2(E.dma_start(out=oC_o[b, h, :, :], in_=cn.t[:, 0:128]), [cn.b], [], final=True)
                yield
                pop_carry(2 if h == 0 else 1)
                A(E.activation(out=den.t[0:NT, h:h + 1], in_=psn_.t[0:NT, 128:129], func=AF.Abs), [psn_], [den])
                V(E.tensor_tensor(out=den.t[0:NT, h:h + 1], in0=den.t[0:NT, h:h + 1], in1=colq.t[0:NT, 8 + h:9 + h], op=ALU.max), [den, colq], [den])
                V(E.reciprocal(out=rden.t[0:NT, h:h + 1], in_=den.t[0:NT, h:h + 1]), [den], [rden])
                V(E.scalar_tensor_tensor(out=hA.t[0:NT, h % 2, :], in0=psn_.t[0:NT, 0:128], scalar=rden.t[0:NT, h:h + 1], in1=so.t[0:NT, h * 128:(h + 1) * 128], op0=ALU.mult, op1=ALU.mult), [psn_, rden, so], [hA.b[h % 2]])
                yield
                V(E.bn_stats(out=bnst.t[0:NT, h, :], in_=hA.t[0:NT, h % 2, :]), [hA.b[h % 2]], [bnst])
                V(E.bn_aggr(out=mv.t[0:NT, h, :], in_=bnst.t[0:NT, h, :]), [bnst], [mv])
                A(E.activation(out=rstdh.t[0:NT, h:h + 1], in_=mv.t[0:NT, h, 1:2], func=AF.Sqrt, bias=epsc.t[0:NT, :], scale=1.0), [mv, epsc], [rstdh])
                V(E.reciprocal(out=rstdh.t[0:NT, h:h + 1], in_=rstdh.t[0:NT, h:h + 1]), [rstdh], [rstdh])
                V(E.tensor_scalar(out=hn_.t[0:NT, :], in0=hA.t[0:NT, h % 2, :], scalar1=mv.t[0:NT, h, 0:1], scalar2=rstdh.t[0:NT, h:h + 1], op0=ALU.subtract, op1=ALU.mult), [hA.b[h % 2], mv, rstdh], [hn_])
                yield
                pst = PS()
                T(E.transpose(out=pst.t[:, 0:NT], in_=hn_.t[0:NT, :], identity=ident.t[0:NT, 0:NT]), [hn_, ident], [pst])
                pop_carry()
                A(E.activation(out=mt.t[:, 0:NT], in_=pst.t[:, 0:NT], func=AF.Copy, scale=lnh.t[:, h:h + 1]), [pst, lnh], [mt])
                V(E.scalar_tensor_tensor(out=mt.t[:, 0:NT], in0=xc.t[:, h, 0:NT], scalar=skp.t[:, h:h + 1], in1=mt.t[:, 0:NT], op0=ALU.mult, op1=ALU.add), [xc, skp, mt], [mt])
                G(E.tensor_tensor(out=mix.t[:, h, 0:NT], in0=mt.t[:, 0:NT], in1=szm.t[:, h, 0:NT], op=ALU.mult), [mt, szm], [mix.b[h]])

        s5ps = {}

        def s5_B(j):
            ct = j // 4
            psb2 = psb[4 + j % 2]
            T(E.matmul(psb2.t[:, 0:NT], BTre.t[:, j, :], xs5b.t[:, ct, 0:NT], start=True, stop=True), [BTre, xs5b], [psb2])
            T(E.matmul(psb2.t[:, NT:N2], BTim.t[:, j, :], xs5b.t[:, ct, 0:NT], start=True, stop=True), [BTim, xs5b], [psb2])
            s5ps[j] = psb2

        def s5_chain1(j):
            ct = j // 4; q = j % 4; r = j % NS5
            P_, W_, Z_, xb, rm = s5P[r], s5W[r], s5Z[r], s5xb[r], rmat[r]
            psb2 = s5ps.pop(j)
            Bv = pr(psb2.t[:, 0:N2], 2)
            V(E.tensor_tensor(out=pr(P_.t[:, 0:N2], 2), in0=Bv, in1=tb(Ec, j), op=ALU.mult), [psb2, Ec], [P_])
            yield
            V(E.tensor_tensor(out=pr(P_.t[:, N2:N4], 2), in0=Bv, in1=tb(Es, j), op=ALU.mult), [psb2, Es], [P_])
            yield
            V(E.tensor_tensor(out=W_.t[:, 0:NT], in0=P_.t[:, 0:NT], in1=P_.t[:, N3:N4], op=ALU.add), [P_], [W_])
            yield
            V(E.tensor_tensor(out=W_.t[:, NT:N2], in0=P_.t[:, NT:N2], in1=P_.t[:, N2:N3], op=ALU.subtract), [P_], [W_])
            yield
            if smp:
                Wv = pr(W_.t[:, 0:N2], 2)
                ah = ah0j[r]
                V(E.tensor_scalar(out=ah.t[:, 0, :], in0=sore.t[:, :, j], scalar1=are.t[:, j:j + 1], scalar2=None, op0=ALU.mult), [sore, are], [ah])
                V(E.scalar_tensor_tensor(out=ah.t[:, 0, :], in0=soim.t[:, :, j], scalar=naim.t[:, j:j + 1], in1=ah.t[:, 0, :], op0=ALU.mult, op1=ALU.add), [soim, naim, ah], [ah])
                V(E.tensor_scalar(out=ah.t[:, 1, :], in0=soim.t[:, :, j], scalar1=are.t[:, j:j + 1], scalar2=None, op0=ALU.mult), [soim, are], [ah])
                V(E.scalar_tensor_tensor(out=ah.t[:, 1, :], in0=sore.t[:, :, j], scalar=aim.t[:, j:j + 1], in1=ah.t[:, 1, :], op0=ALU.mult, op1=ALU.add), [sore, aim, ah], [ah])
                V(E.tensor_tensor(out=Wv[:, :, :, 0:1], in0=Wv[:, :, :, 0:1], in1=ah.t[:, :, :].unsqueeze(3), op=ALU.add), [W_, ah], [W_])
                A(E.activation(out=pr(rm.t[:, 0:N2], 2), in_=seqm.t[:, :].rearrange("p (b t) -> p b t", t=4).unsqueeze(1).to_broadcast([128, 2, 16, 4]),
                               func=AF.Copy, scale=mag.t[:, j:j + 1]), [seqm, mag], [rm])
            else:
                Wv = pr(W_.t[:, 0:N2], 2)
                V(E.tensor_tensor(out=Wv[:, :, 0:1], in0=Wv[:, :, 0:1], in1=azin.t[:, j, :].unsqueeze(2), op=ALU.add), [W_, azin], [W_])
                A(E.activation(out=pr(rm.t[:, 0:N2], 2), in_=ones4.t[:, 0:NT].unsqueeze(1).to_broadcast([128, 2, NT]), func=AF.Copy, scale=mag.t[:, j:j + 1]), [ones4, mag], [rm])
                A(E.activation(out=pr(rm.t[:, 0:N2], 2)[:, :, 0:1], in_=pr(rm.t[:, 0:N2], 2)[:, :, 0:1], func=AF.Copy, scale=0.0), [rm], [rm])
            V(E.tensor_tensor_scan(out=Z_.t[:, 0:N2], data0=rm.t[:, 0:N2], data1=W_.t[:, 0:N2], initial=0.0, op0=ALU.mult, op1=ALU.add), [rm, W_], [Z_])
            Zv = pr(Z_.t[:, 0:N2], 2)
            G(E.tensor_tensor(out=pr(P_.t[:, 0:N2], 2), in0=Zv, in1=tb(Ec, j), op=ALU.mult), [Z_, Ec], [P_])
            G(E.tensor_tensor(out=pr(P_.t[:, N2:N4], 2), in0=Zv, in1=tb(Es, j), op=ALU.mult), [Z_, Es], [P_])

        def s5_chain2(j):
            ct = j // 4; q = j % 4; r = j % NS5
            P_, W_, Z_, xb, rm = s5P[r], s5W[r], s5Z[r], s5xb[r], rmat[r]
            V(E.tensor_tensor(out=xb.t[:, 0:NT], in0=P_.t[:, 0:NT], in1=P_.t[:, N3:N4], op=ALU.subtract), [P_], [xb])
            yield
            V(E.tensor_tensor(out=xb.t[:, NT:N2], in0=P_.t[:, N2:N3], in1=P_.t[:, NT:N2], op=ALU.add), [P_], [xb])
            yield
            if smp:
                def l3(a, b):
                    return P_.t[:, a:b].rearrange("p (b t) -> p b t", t=4)[:, :, 3:4]
                V(E.tensor_tensor(out=sore.t[:, :, j:j + 1], in0=l3(0, NT), in1=l3(N3, N4), op=ALU.subtract), [P_], [sore])
                V(E.tensor_tensor(out=soim.t[:, :, j:j + 1], in0=l3(N2, N3), in1=l3(NT, N2), op=ALU.add), [P_], [soim])
            else:
                V(E.tensor_tensor(out=xl_re.t[:, j:j + 1], in0=P_.t[:, NT - 1:NT], in1=P_.t[:, N4 - 1:N4], op=ALU.subtract), [P_], [xl_re])
                V(E.tensor_tensor(out=xl_im.t[:, j:j + 1], in0=P_.t[:, N3 - 1:N3], in1=P_.t[:, N2 - 1:N2], op=ALU.add), [P_], [xl_im])
            yield
            s5_C(j)

        def s5_C(j):
            ct = j // 4; q = j % 4; r = j % NS5
            xb = s5xb[r]; psy = psb[6 + ct % 2]
            T(E.matmul(psy.t[:, 0:NT], CTre.t[:, j, :], xb.t[:, 0:NT], start=(q == 0), stop=False), [CTre, xb], [psy])
            T(E.matmul(psy.t[:, 0:NT], CTimn.t[:, j, :], xb.t[:, NT:N2], start=False, stop=(q == 3)), [CTimn, xb], [psy])

        def s5_epi(ct):
            psy = psb[6 + ct % 2]
            V(E.scalar_tensor_tensor(out=ysb.t[:, ct, 0:NT], in0=xs5.t[:, ct, 0:NT], scalar=s5D.t[:, ct:ct + 1], in1=psy.t[:, 0:NT], op0=ALU.mult, op1=ALU.add), [xs5, s5D, psy], [ysb.b[ct]])
            yield
            g1 = gl1[ct % 2]; g2_ = gl2[ct % 2]
            A(E.activation(out=g1.t[:, 0:NT], in_=ysb.t[:, ct, 0:NT], func=AF.Square), [ysb.b[ct]], [g1])
            yield
            V(E.tensor_scalar(out=g1.t[:, 0:NT], in0=g1.t[:, 0:NT], scalar1=0.044715, scalar2=1.0, op0=ALU.mult, op1=ALU.add), [g1], [g1])
            yield
            V(E.tensor_tensor(out=g1.t[:, 0:NT], in0=g1.t[:, 0:NT], in1=ysb.t[:, ct, 0:NT], op=ALU.mult), [g1, ysb.b[ct]], [g1])
            yield
            A(E.activation(out=g2_.t[:, 0:NT], in_=g1.t[:, 0:NT], func=AF.Sigmoid, scale=1.5957691216057308), [g1], [g2_])
            yield
            V(E.tensor_tensor(out=yg.t[:, ct, 0:NT], in0=ysb.t[:, ct, 0:NT], in1=g2_.t[:, 0:NT], op=ALU.mult), [ysb.b[ct], g2_], [yg.b[ct]])
            yield
            G(E.tensor_copy(out=ygb.t[:, ct, 0:NT], in_=yg.t[:, ct, 0:NT]), [yg.b[ct]], [ygb])

        bg = []
        if glu_prev is not None:
            bg.append(glu_prev())
        s5_B(0)
        s5_B(1)
        for _ in s5_chain1(0):
            pass
        for j in range(16):
            if j + 2 < 16:
                s5_B(j + 2)
            if j % 2 == 0 and j >= 2:
                pop_pending()
            gens = ([s5_chain1(j + 1)] if j + 1 < 16 else []) + [s5_chain2(j)] + bg
            del bg[:]
            while gens:
                for g_ in list(gens):
                    try:
                        next(g_)
                    except StopIteration:
                        gens.remove(g_)
            if j % 4 == 3:
                bg.append(s5_epi(j // 4))

        for g_ in bg:
            for _ in g_:
                pass
        while pending:
            pop_pending(background=False)
        gens = [gates_gen(), conv_gen()]
        while gens:
            for g_ in list(gens):
                try:
                    next(g_)
                except StopIteration:
                    gens.remove(g_)
        for pair in (((0,), (1,), (2,), (3,)) if smp else ((0, 1), (2, 3))):
            gens = [mlstm_head(h_) for h_ in pair]
            while gens:
                for g_ in list(gens):
                    try:
                        next(g_)
                    except StopIteration:
                        gens.remove(g_)
        pop_carry(len(cin))
        if not smp:
            V(E.tensor_copy(out=xmh.t[:, :, 0:3], in_=xmh.t[:, :, NT:NT + 3]), [xmh], [xmh])

        if not smp:
            V(E.tensor_tensor(out=t0.t[:], in0=are.t[:], in1=xl_re.t[:], op=ALU.mult), [are, xl_re], [t0])
            V(E.tensor_tensor(out=t1.t[:], in0=aim.t[:], in1=xl_im.t[:], op=ALU.mult), [aim, xl_im], [t1])
            V(E.tensor_tensor(out=azin.t[:, :, 0], in0=t0.t[:], in1=t1.t[:], op=ALU.subtract), [t0, t1], [azin])
            V(E.tensor_tensor(out=t0.t[:], in0=aim.t[:], in1=xl_re.t[:], op=ALU.mult), [aim, xl_re], [t0])
            V(E.tensor_tensor(out=t1.t[:], in0=are.t[:], in1=xl_im.t[:], op=ALU.mult), [are, xl_im], [t1])
            V(E.tensor_tensor(out=azin.t[:, :, 1], in0=t0.t[:], in1=t1.t[:], op=ALU.add), [t0, t1], [azin])
        def glu_gen():
            for oc in range(4):
                ps = PS()
                for kc in range(4):
                    T(E.matmul(ps.t[:, 0:NT], wglu.t[:, kc, oc * 128:(oc + 1) * 128], ygb.t[:, kc, 0:NT], start=(kc == 0), stop=(kc == 3)), [wglu, ygb], [ps])
                    yield
                g1 = gl1[oc % 2]
                A(E.activation(out=g1.t[:, 0:NT], in_=ps.t[:, 0:NT], func=AF.Sigmoid, bias=bglu.t[:, oc:oc + 1], scale=1.0), [ps, bglu], [g1])
                yield
                V(E.tensor_tensor(out=g1.t[:, 0:NT], in0=g1.t[:, 0:NT], in1=yg.t[:, oc, 0:NT], op=ALU.mult), [g1, yg.b[oc]], [g1])
                yield
                G(E.tensor_tensor(out=mix.t[:, 4 + oc, 0:NT], in0=g1.t[:, 0:NT], in1=szs.t[:, oc, 0:NT], op=ALU.mult), [g1, szs], [mix.b[4 + oc]])
                yield

        def c_outproj(half):
            if half == 0:
                dma_in(h2.t[0:NT, :], x_src, h2.b)
            ps = PS()
            for k in range(8):
                T(E.matmul(ps.t[0:NT, :], mix.t[:, k, 0:NT], wout.t[:, k, half * 512:(half + 1) * 512], start=(k == 0), stop=(k == 7)), [mix.b[k], wout], [ps])
            V(E.tensor_tensor(out=h2.t[0:NT, half * 512:(half + 1) * 512], in0=ps.t[0:NT, :], in1=h2.t[0:NT, half * 512:(half + 1) * 512], op=ALU.add), [ps, h2], [h2])

        def c_h2T():
            for half in range(2):
                ps = PS()
                for kk in range(4):
                    k = half * 4 + kk
                    T(E.transpose(out=ps.t[:, kk * 128:kk * 128 + NT], in_=h2.t[0:NT, k * 128:(k + 1) * 128], identity=ident.t[0:NT, 0:NT]), [h2, ident], [ps])
                A(E.activation(out=h2T.t[:, half * 4:half * 4 + 4, 0:NT], in_=ps.t[:, :].rearrange("p (k t) -> p k t", k=4)[:, :, 0:NT], func=AF.Copy), [ps], [h2T])

        def c_gate(half):
            ps = PS()
            for k in range(8):
                T(E.matmul(ps.t[0:NT, :], h2T.t[:, k, 0:NT], wgate.t[:, k, half * 512:(half + 1) * 512], start=(k == 0), stop=(k == 7)), [h2T, wgate], [ps])
            A(E.activation(out=sgate.t[0:NT, half * 512:(half + 1) * 512], in_=ps.t[0:NT, :], func=AF.Sigmoid), [ps], [sgate])

        def c_tail():
            G(E.tensor_tensor(out=esb.t[0:NT, :], in0=esb.t[0:NT, :], in1=sgate.t[0:NT, :], op=ALU.mult), [esb, sgate], [esb])
            G(E.tensor_tensor(out=esb.t[0:NT, :], in0=esb.t[0:NT, :], in1=h2.t[0:NT, :], op=ALU.add), [esb, h2], [esb])
            A(E.activation(out=sgate.t[0:NT, :], in_=esb.t[0:NT, :], func=AF.Square, accum_out=ssq.t[0:NT, 3:4]), [esb], [sgate, ssq])
            A(E.activation(out=rstd.t[0:NT, 3:4], in_=ssq.t[0:NT, 3:4], func=AF.Sqrt, bias=epsc.t[0:NT, :], scale=1.0 / DM), [ssq, epsc], [rstd])
            V(E.reciprocal(out=rstd.t[0:NT, 3:4], in_=rstd.t[0:NT, 3:4]), [rstd], [rstd])
            V(E.scalar_tensor_tensor(out=sgate.t[0:NT, :], in0=esb.t[0:NT, :], scalar=rstd.t[0:NT, 3:4], in1=lnfin.t[0:NT, :], op0=ALU.mult, op1=ALU.mult), [esb, rstd, lnfin], [sgate])
            dma_out(y_dst, sgate.t[0:NT, :], [sgate.b])
        return [lambda: c_outproj(0), lambda: c_outproj(1), c_h2T, lambda: c_gate(0), lambda: c_gate(1), c_tail], glu_gen

    if DBG_STAGE == "setup":
        dma_out(yp_o[0:128, 0:128], Ec.t[:, 3, :], [Ec.b])
        dma_out(yp_o[0:128, 128:256], Es.t[:, 15, :], [Es.b])
        V(E.tensor_copy(out=ytile.t[:, 0:128], in_=BTre.t[:, 5, :]), [BTre], [ytile])
        V(E.tensor_copy(out=ytile.t[:, 128:256], in_=CTimn.t[:, 9, :]), [CTimn], [ytile])
        V(E.tensor_copy(out=ytile.t[:, 256:512], in_=win.t[:, 7, 1400:1656]), [win], [ytile])
        V(E.tensor_copy(out=ytile.t[:, 512:768], in_=wgate.t[:, 7, 100:356]), [wgate], [ytile])
        V(E.tensor_copy(out=ytile.t[:, 768:1024], in_=wglu.t[:, 3, 256:512]), [wglu], [ytile])
        dma_out(yp_o[0:128, 256:1280 - 256], ytile.t[:, 0:768], [ytile.b])
        p.build()
        return nc, p
    dma_in(m0c.t[:, :], sm_d.rearrange("b h -> h b"), m0c.b, slow=True)
    V(E.tensor_copy(out=m0row.t[:, :].rearrange("h (b t) -> h b t", t=4), in_=m0c.t[:, :].unsqueeze(2).to_broadcast([4, 16, 4])), [m0c], [m0row])
    cv = sconv_d.rearrange("b j (c q) -> (b j c) q", q=128)
    dma_in(ytile.t[:, 0:128], cv[0:128, :], ytile.b)
    dma_in(ytile.t[0:64, 128:256], cv[128:192, :], ytile.b, join=True)
    hvr = sre_d.rearrange("b (j g) q -> (b j) (g q)", g=2); hvi = sim_d.rearrange("b (j g) q -> (b j) (g q)", g=2)
    for hf in range(2):
        dma_in(ytile.t[:, 256 + hf * 128:384 + hf * 128], hvr[hf * 128:(hf + 1) * 128, :], ytile.b, join=True)
        dma_in(ytile.t[:, 512 + hf * 128:640 + hf * 128], hvi[hf * 128:(hf + 1) * 128, :], ytile.b, join=True)
    dma_in(ytile.t[0:64, 768:896], sn_d.rearrange("b h d -> (b h) d"), ytile.b, join=True)
    psp = PS()
    T(E.transpose(out=psp.t[:, 0:128], in_=ytile.t[:, 0:128], identity=ident.t[:, :]), [ytile, ident], [psp])
    T(E.transpose(out=psp.t[:, 128:192], in_=ytile.t[0:64, 128:256], identity=ident.t[0:64, 0:64]), [ytile, ident], [psp])
    for c in range(4):
        V(E.tensor_copy(out=xms.t[:, c, :, 0:3], in_=psp.t[:, 0:192].rearrange("p (b j c) -> p c b j", j=3, c=4)[:, c]), [psp], [xms])
    for src0, dstt in ((256, sore), (512, soim)):
        psp = PS()
        for hf in range(2):
            T(E.transpose(out=psp.t[:, hf * 128:(hf + 1) * 128], in_=ytile.t[:, src0 + hf * 128:src0 + (hf + 1) * 128], identity=ident.t[:, :]), [ytile, ident], [psp])
        V(E.tensor_copy(out=dstt.t[:, :, :].rearrange("p b j -> p (b j)"), in_=psp.t[:, 0:256]), [psp], [dstt])
    psp = PS()
    T(E.transpose(out=psp.t[:, 0:64], in_=ytile.t[0:64, 768:896], identity=ident.t[0:64, 0:64]), [ytile, ident], [psp])
    V(E.tensor_copy(out=nT.t[:, :], in_=psp.t[:, 0:64]), [psp], [nT])
    V(E.tensor_scalar(out=naim.t[:], in0=aim.t[:], scalar1=-1.0, scalar2=None, op0=ALU.mult), [aim], [naim])
    carry = []
    glu_prev = None
    def tile_id(n):
        return ("p", n) if n < NPT else (("s", 0) if n == NPT else None)
    load_inputs("p", 0)
    front("p", 0, nxt=tile_id(1))
    for ti in range(NPT):
        nf = tile_id(ti + 1)
        carry, glu_prev = do_tile("p", ti, carry, nxt=(nf[0], nf[1], tile_id(ti + 2)), glu_prev=glu_prev)
    if DBG_STAGE == "prompt":
        p.build()
        return nc, p
    for h in range(4):
        dma_out(pC_o[h, :, :], Caug[h].t[:, 0:128], [Caug[h].b])
        dma_out(pn_o[h, :].rearrange("(d o) -> d o", o=1), Caug[h].t[:, 128:129], [Caug[h].b], slow=True)
    dma_out(pm_o.rearrange("(h o) -> h o", o=1), mnew.t[:, 0:1], [mnew.b], slow=True)
    for c in range(4):
        dma_out(pconv_o[:, c * 128:(c + 1) * 128].rearrange("j q -> q j"), xmh.t[:, c, 0:3], [xmh.b], slow=True)
    dma_out(pre_o.rearrange("(j g) q -> (g q) j", g=2), xl_re.t[:, :], [xl_re.b], slow=True)
    dma_out(pim_o.rearrange("(j g) q -> (g q) j", g=2), xl_im.t[:, :], [xl_im.b], slow=True)

    carry, glu_prev = do_tile("s", 0, carry, glu_prev=glu_prev)
    for _ in glu_prev():
        pass
    for f in carry:
        f()
    dma_out(om_o.rearrange("b h -> h b"), mnew.t[:, :], [mnew.b], slow=True)
    ost = TL(s5P[0].t, s5P[0].b); ost2 = TL(s5P[1].t, s5P[1].b)
    for nm, src, dst in (("re", sore, ore_o), ("im", soim, oim_o)):
        pso = PS()
        for hf in range(2):
            T(E.transpose(out=pso.t[:, hf * 128:(hf + 1) * 128], in_=src.t[:, hf * 8:(hf + 1) * 8, :].rearrange("p b j -> p (b j)"), identity=ident.t[:, :]), [src, ident], [pso])
        o_ = ost if nm == "re" else ost2
        V(E.tensor_copy(out=o_.t[:, 0:256], in_=pso.t[:, 0:256]), [pso], [o_])
        dv = dst.rearrange("b (j g) q -> (b j) (g q)", g=2)
        for hf in range(2):
            dma_out(dv[hf * 128:(hf + 1) * 128, :], o_.t[:, hf * 128:(hf + 1) * 128], [o_.b])
    for c in range(4):
        V(E.tensor_copy(out=ost.t[:, 256:448].rearrange("p (b j c) -> p c b j", j=3, c=4)[:, c], in_=xms.t[:, c, :, 4:7]), [xms], [ost])
    pso = PS()
    T(E.transpose(out=pso.t[:, 0:128], in_=ost.t[:, 256:384], identity=ident.t[:, :]), [ost, ident], [pso])
    T(E.transpose(out=pso.t[0:64, 128:256], in_=ost.t[:, 384:448], identity=ident.t[:, :]), [ost, ident], [pso])
    V(E.tensor_copy(out=ost2.t[:, 256:384], in_=pso.t[:, 0:128]), [pso], [ost2])
    V(E.tensor_copy(out=ost2.t[0:64, 384:512], in_=pso.t[0:64, 128:256]), [pso], [ost2])
    cvo = oconv_o.rearrange("b j (c q) -> (b j c) q", q=128)
    dma_out(cvo[0:128, :], ost2.t[:, 256:384], [ost2.b])
    dma_out(cvo[128:192, :], ost2.t[0:64, 384:512], [ost2.b])
    pso = PS()
    T(E.transpose(out=pso.t[0:64, 0:128], in_=nOut.t[:, :], identity=ident.t[:, :]), [nOut, ident], [pso])
    V(E.tensor_copy(out=ost.t[0:64, 0:128], in_=pso.t[0:64, 0:128]), [pso], [ost])
    dma_out(on_o.rearrange("b h d -> (b h) d"), ost.t[0:64, 0:128], [ost.b])

    p.build()
    return nc, p


_CACHE = {}


def kernel(**inputs):
    f = lambda a: np.ascontiguousarray(np.asarray(a, dtype=np.float32))
    if "nc" not in _CACHE:
        _CACHE["nc"] = build_program()
    nc, prog = _CACHE["nc"]
    consts = make_consts()
    shared = {}
    for k in ("ln_mix", "w_in", "b_igate", "b_fgate", "conv_w", "conv_b", "w_q", "w_k", "w_v", "ln_head", "skip_a",
              "s5_lam_re", "s5_lam_im", "s5_log_dt", "s5_B_re", "s5_B_im", "s5_C_re", "s5_C_im", "s5_D", "w_glu",
              "b_glu", "w_out", "w_ple", "ln_ple", "w_ple_gate"):
        shared[k] = f(inputs[k])[0]
    shared["ln_final"] = f(inputs["ln_final"])
    for k, v in consts.items():
        shared["c_" + k] = v
    xp = f(inputs["x_prompt"]); xs = f(inputs["x_sample"])
    pp = f(inputs["p_prompt"])[0]; ps = f(inputs["p_sample"])[0]
    sC = f(inputs["state_mlstm_C"])[0]; sn = f(inputs["state_mlstm_n"])[0]; sm = f(inputs["state_mlstm_m"])[0]
    sconv = f(inputs["state_conv"])[0]; sre = f(inputs["state_s5_re"])[0]; sim = f(inputs["state_s5_im"])[0]
    in_maps = []
    for c in range(NCORES):
        sl = slice(c * SB, (c + 1) * SB)
        m = dict(shared)
        m["xp"] = xp[c]; m["xs"] = np.ascontiguousarray(xs[sl].reshape(64, DM))
        m["pp"] = pp[c]; m["psm"] = np.ascontiguousarray(ps[sl].reshape(64, 256))
        m["sC"] = np.ascontiguousarray(sC[sl]); m["sn"] = np.ascontiguousarray(sn[sl]); m["sm"] = np.ascontiguousarray(sm[sl])
        m["sconv"] = np.ascontiguousarray(sconv[sl]); m["sre"] = np.ascontiguousarray(sre[sl]); m["sim"] = np.ascontiguousarray(sim[sl])
        in_maps.append(m)
    res = run_bass_kernel_spmd(nc, in_maps, core_ids=list(range(NCORES)))
    R = res.results
    g = lambda k: [np.asarray(R[c][k], dtype=np.float32) for c in range(NCORES)]
    y_prompt = np.stack(g("yp"), 0)
    y_sample = np.concatenate([a.reshape(SB, 4, DM) for a in g("ys")], 0)
    pC = np.stack(g("pC"), 0)[None]; pn = np.stack(g("pn"), 0)[None]; pm = np.stack(g("pm"), 0)[None]
    pconv = np.stack(g("pconv"), 0)[None]; pre = np.stack(g("pre"), 0)[None]; pim = np.stack(g("pim"), 0)[None]
    oC = np.concatenate(g("oC"), 0)[None]; on = np.concatenate(g("on"), 0)[None]; om = np.concatenate(g("om"), 0)[None]
    oconv = np.concatenate(g("oconv"), 0)[None]; ore = np.concatenate(g("ore"), 0)[None]; oim = np.concatenate(g("oim"), 0)[None]
    return (y_prompt, y_sample, pC, pn, pm, pconv, pre, pim, oC, on, om, oconv, ore, oim)
```

```python
import math
from contextlib import ExitStack
import numpy as np
import concourse.bass as bass
import concourse.mybir as mybir
from concourse.bass_utils import run_bass_kernel_spmd

F32 = mybir.dt.float32
BF16 = mybir.dt.bfloat16
I32 = mybir.dt.int32
AF = mybir.ActivationFunctionType
ALU = mybir.AluOpType
ENGS = ("sync", "scalar", "vector", "gpsimd", "tensor")

NCORES = 8
SEQ = 2048
NPT = 16
SB = 16
DM = 1024
INC = 2568
EPS = 1e-6
BIG = 1.0e30
SAME_ENGINE_SYNC = True
DBG_CUT = None
DBG_VAR = 0
DBG_STAGE = None


class Buf:
    __slots__ = ("name", "ws", "base", "readers")

    def __init__(self, name, like=None):
        self.name = name
        self.ws = []
        self.base = []
        self.readers = []
        if like is not None:
            for l in like:
                self.ws += l.ws
                self.base += l.base
                self.readers += l.readers


class Prog:
    def __init__(self, nc, n_dma_sems=24):
        self.nc = nc
        self.ops = []
        self.n_dma_sems = n_dma_sems
        self.final_dma = []
        self.es = ExitStack()

    def sbuf(self, name, shape, dtype=F32):
        return self.es.enter_context(self.nc.sbuf_tensor(name, list(shape), dtype))

    def psum(self, name, shape, dtype=F32):
        return self.es.enter_context(self.nc.psum_tensor(name, list(shape), dtype))

    def op(self, eng, fn, reads=(), writes=(), dma=False, final=False, join=False):
        i = len(self.ops)
        deps = set()
        for b in reads:
            deps.update(b.ws)
        for b in writes:
            if not join:
                deps.update(b.ws)
                deps.update(b.readers)
            else:
                if not b.base and (b.ws or b.readers):
                    b.base = list(b.ws) + list(b.readers)
                deps.update(b.base)
                deps.update(b.readers)
        deps.discard(i)
        self.ops.append(dict(eng=eng, fn=fn, deps=deps, dma=dma))
        for b in reads:
            b.readers.append(i)
        for b in writes:
            if join:
                b.ws = b.ws + [i]
            else:
                b.ws = [i]
                b.base = []
            b.readers = []
        if final:
            self.final_dma.append(i)
        return i

    def build(self):
        nc = self.nc
        ops = self.ops
        n = len(ops)
        needed = [False] * n
        for i, o in enumerate(ops):
            for d in o["deps"]:
                if ops[d]["eng"] != o["eng"] or SAME_ENGINE_SYNC or ops[d]["dma"]:
                    needed[d] = True
        with self.es as es:
            esem = {e: es.enter_context(nc.semaphore("s_" + e)) for e in ENGS}
            dsem = [es.enter_context(nc.semaphore("d%d" % k)) for k in range(self.n_dma_sems)]
            tok = [None] * n
            ecount = {e: 0 for e in ENGS}
            dcount = [0] * self.n_dma_sems
            dlast = [None] * self.n_dma_sems
            dk = 0
            prev_same_sem = {}
            for i, o in enumerate(ops):
                if o["dma"]:
                    k = dk % self.n_dma_sems
                    dk += 1
                    dcount[k] += 16
                    tok[i] = ("d", k, dcount[k])
                    if dlast[k] is not None:
                        prev_same_sem[i] = dlast[k]
                    dlast[k] = i
                elif needed[i]:
                    ecount[o["eng"]] += 1
                    tok[i] = ("e", o["eng"], ecount[o["eng"]])
                else:
                    tok[i] = ("e", o["eng"], ecount[o["eng"]] + 1)
            per_eng = {e: [] for e in ENGS}
            waited = {e: {} for e in ENGS}
            for i, o in enumerate(ops):
                e = o["eng"]
                deps = set(o["deps"])
                if i in prev_same_sem:
                    deps.add(prev_same_sem[i])
                waits = []
                for d in sorted(deps):
                    od = ops[d]
                    if (not od["dma"]) and od["eng"] == e and not SAME_ENGINE_SYNC:
                        continue
                    t = tok[d]
                    key = (t[0], t[1])
                    if waited[e].get(key, 0) >= t[2]:
                        continue
                    waited[e][key] = t[2]
                    sem = dsem[t[1]] if t[0] == "d" else esem[t[1]]
                    waits.append((sem, t[2]))
                sig = None
                if o["dma"]:
                    sig = (dsem[tok[i][1]], 16)
                elif needed[i]:
                    sig = (esem[e], 1)
                per_eng[e].append((waits, o["fn"], sig))
            fin = [(dsem[tok[i][1]], tok[i][2]) for i in self.final_dma]
            self.stats = {e: len(per_eng[e]) for e in ENGS}

            def run(engobj, lst, is_sync=False):
                for waits, fn, sig in lst:
                    for (s, v) in waits:
                        engobj.wait_ge(s, v)
                    ins = fn(engobj)
                    if sig is not None:
                        ins.then_inc(sig[0], sig[1])
                if is_sync:
                    done = {}
                    for (s, v) in fin:
                        engobj.wait_ge(s, v)

            with nc.Block() as block:
                @block.sync
                def _(e):
                    run(e, per_eng["sync"], True)

                @block.scalar
                def _(e):
                    run(e, per_eng["scalar"])

                @block.vector
                def _(e):
                    run(e, per_eng["vector"])

                @block.gpsimd
                def _(e):
                    run(e, per_eng["gpsimd"])

                @block.tensor
                def _(e):
                    run(e, per_eng["tensor"])
        return nc


class _Rec:
    def __getattr__(self, name):
        def f(*a, **k):
            return lambda e: getattr(e, name)(*a, **k)
        return f


E = _Rec()


class TL:
    def __init__(self, t, b):
        self.t = t
        self.b = b


def make_consts():
    c = {}
    c["ident"] = np.eye(128, dtype=np.float32)
    s = np.arange(128)[:, None]
    t = np.arange(128)[None, :]
    c["posmask_p"] = np.where(s <= t, 0.0, BIG).astype(np.float32)
    s6 = np.arange(64)[:, None]
    t6 = np.arange(64)[None, :]
    pm = np.where((s6 <= t6) & (s6 // 4 == t6 // 4), 0.0, BIG).astype(np.float32)
    c["posmask_s"] = pm
    sel = np.zeros((4, 4, 128), np.float32)
    for h in range(4):
        sel[h, h, :] = 1.0
    c["sel"] = sel.reshape(4, 512)
    sm = np.ones((128, 64), np.float32)
    sm[:, 0::4] = 0.0
    c["seqmask1"] = sm
    nb = np.zeros((4, 64), np.float32)
    nb[:, 0::4] = -BIG
    c["seqnegbig"] = nb
    oh = np.zeros((64, 16), np.float32)
    for s_ in range(64):
        oh[s_, s_ // 4] = 1.0
    c["onehot_s"] = oh
    return c


def build_program():
    nc = bass.Bass("TRN2", target_bir_lowering=False)
    p = Prog(nc)
    try:
        return _build_body(nc, p)
    except Exception as ex:
        if type(ex).__name__ != "_Cut":
            raise
        p.build()
        return nc, p


def _build_body(nc, p):
    din, dout = {}, {}

    def inp(name, shape):
        din[name] = nc.dram_tensor(name, list(shape), F32, kind="ExternalInput").ap()
        return din[name]

    def outp(name, shape):
        dout[name] = nc.dram_tensor(name, list(shape), F32, kind="ExternalOutput").ap()
        return dout[name]

    xp_d = inp("xp", [SEQ, DM]); xs_d = inp("xs", [64, DM])
    pp_d = inp("pp", [SEQ, 256]); psm_d = inp("psm", [64, 256])
    sC_d = inp("sC", [SB, 4, 128, 128]); sn_d = inp("sn", [SB, 4, 128]); sm_d = inp("sm", [SB, 4])
    sconv_d = inp("sconv", [SB, 3, 512]); sre_d = inp("sre", [SB, 32, 64]); sim_d = inp("sim", [SB, 32, 64])
    ln_mix_d = inp("ln_mix", [DM]); w_in_d = inp("w_in", [DM, INC])
    b_ig_d = inp("b_igate", [4]); b_fg_d = inp("b_fgate", [4])
    conv_w_d = inp("conv_w", [4, 512]); conv_b_d = inp("conv_b", [512])
    wq_d = inp("w_q", [4, 128, 128]); wk_d = inp("w_k", [4, 128, 128]); wv_d = inp("w_v", [4, 128, 128])
    ln_head_d = inp("ln_head", [512]); skip_d = inp("skip_a", [512])
    lamre_d = inp("s5_lam_re", [32, 64]); lamim_d = inp("s5_lam_im", [32, 64]); logdt_d = inp("s5_log_dt", [32, 64])
    Bre_d = inp("s5_B_re", [32, 64, 16]); Bim_d = inp("s5_B_im", [32, 64, 16])
    Cre_d = inp("s5_C_re", [32, 16, 64]); Cim_d = inp("s5_C_im", [32, 16, 64])
    s5D_d = inp("s5_D", [512]); wglu_d = inp("w_glu", [512, 512]); bglu_d = inp("b_glu", [512])
    wout_d = inp("w_out", [DM, DM]); wple_d = inp("w_ple", [256, DM]); lnple_d = inp("ln_ple", [DM])
    wgate_d = inp("w_ple_gate", [DM, DM]); lnfin_d = inp("ln_final", [DM])
    c_ident = inp("c_ident", [128, 128]); c_pmp = inp("c_posmask_p", [128, 128]); c_pms = inp("c_posmask_s", [64, 64])
    c_sel = inp("c_sel", [4, 512]); c_seqm = inp("c_seqmask1", [128, 64]); c_snb = inp("c_seqnegbig", [4, 64])
    c_oh = inp("c_onehot_s", [64, 16])

    yp_o = outp("yp", [SEQ, DM]); ys_o = outp("ys", [64, DM])
    pC_o = outp("pC", [4, 128, 128]); pn_o = outp("pn", [4, 128]); pm_o = outp("pm", [4])
    pconv_o = outp("pconv", [3, 512]); pre_o = outp("pre", [32, 64]); pim_o = outp("pim", [32, 64])
    oC_o = outp("oC", [SB, 4, 128, 128]); on_o = outp("on", [SB, 4, 128]); om_o = outp("om", [SB, 4])
    oconv_o = outp("oconv", [SB, 3, 512]); ore_o = outp("ore", [SB, 32, 64]); oim_o = outp("oim", [SB, 32, 64])

    def mk(name, shape, dtype=F32, nb=1):
        t = p.sbuf(name, shape, dtype)
        if nb == 1:
            return TL(t, Buf(name))
        return TL(t, [Buf("%s%d" % (name, i)) for i in range(nb)])

    def bl(x):
        out = []
        for a in x:
            if isinstance(a, TL):
                out += a.b if isinstance(a.b, list) else [a.b]
            elif isinstance(a, Buf):
                out.append(a)
            elif isinstance(a, (list, tuple)):
                out += bl(a)
            elif a is not None:
                raise TypeError(a)
        return out

    def V(fn, r, w, **kw): p.op("vector", fn, bl(r), bl(w), **kw)
    def A(fn, r, w, **kw): p.op("scalar", fn, bl(r), bl(w), **kw)
    def G(fn, r, w, **kw): p.op("gpsimd", fn, bl(r), bl(w), **kw)
    def T(fn, r, w, **kw): p.op("tensor", fn, bl(r), bl(w), **kw)
    def D(fn, r, w, **kw): p.op("sync", fn, bl(r), bl(w), dma=True, **kw)
    def D2(fn, r, w, **kw): p.op("scalar", fn, bl(r), bl(w), dma=True, **kw)

    NPS = 8
    psb = [TL(p.psum("ps%d" % i, [128, 512], F32), Buf("ps%d" % i)) for i in range(NPS)]
    psi = [0]

    def PS():
        x = psb[psi[0] % 4]
        psi[0] += 1
        return x

    class _Cut(Exception):
        pass

    def CUT(k, tl):
        if DBG_CUT == k:
            D(E.dma_start(out=yp_o[0:tl.t.shape[0], 0:16], in_=tl.t[:, 0:16]), [tl], [], final=True)
            raise _Cut()

    def dma_in(dst_ap, src_ap, wbuf, slow=False, join=False):
        if slow:
            D(E.dma_start(out=dst_ap, in_=src_ap, allow_slow_non_contiguous=True), [], [wbuf], join=join)
        else:
            D(E.dma_start(out=dst_ap, in_=src_ap), [], [wbuf], join=join)

    def dma_out(dst_ap, src_ap, rbufs, slow=False):
        if slow:
            D(E.dma_start(out=dst_ap, in_=src_ap, allow_slow_non_contiguous=True), rbufs, [], final=True)
        else:
            D(E.dma_start(out=dst_ap, in_=src_ap), rbufs, [], final=True)

    def col512(name, src_d):
        t = mk(name, [128, 4])
        dma_in(t.t[:, :], src_d.rearrange("(c p) -> p c", p=128), t.b, slow=True)
        return t

    stg = [mk("stg%d" % i, [128, INC]) for i in range(2)]
    ident = mk("ident", [128, 128]); dma_in(ident.t[:], c_ident[:, :], ident.b)
    pmp = mk("pmp", [128, 128]); dma_in(pmp.t[:], c_pmp[:, :], pmp.b)
    pms = mk("pms", [64, 64]); dma_in(pms.t[:], c_pms[:, :], pms.b)
    sel = mk("sel", [4, 512]); dma_in(sel.t[:], c_sel[:, :], sel.b)
    seqm = mk("seqm", [128, 64]); dma_in(seqm.t[:], c_seqm[:, :], seqm.b)
    snb = mk("snb", [4, 64]); dma_in(snb.t[:], c_snb[:, :], snb.b)
    oneh = mk("oneh", [64, 16]); dma_in(oneh.t[:], c_oh[:, :], oneh.b)
    ones4 = mk("ones4", [128, 128]); G(E.memset(ones4.t[:], 1.0), [], [ones4])

    convw = mk("convw", [128, 4, 4])
    for c in range(4):
        dma_in(convw.t[:, c, :], conv_w_d[:, c * 128:(c + 1) * 128].rearrange("j p -> p j"), convw.b, slow=True, join=True)
    convb = col512("convb", conv_b_d); lnh = col512("lnh", ln_head_d); skp = col512("skp", skip_d)
    s5D = col512("s5D", s5D_d); bglu = col512("bglu", bglu_d)
    gmix = mk("gmix", [128, 8]); dma_in(gmix.t[:, :], ln_mix_d.rearrange("(k p) -> p k", p=128), gmix.b, slow=True)
    big_ = mk("big_", [4, 1]); dma_in(big_.t[:, :], b_ig_d.rearrange("(h o) -> h o", o=1), big_.b, slow=True)
    bfg = mk("bfg", [4, 1]); dma_in(bfg.t[:, :], b_fg_d.rearrange("(h o) -> h o", o=1), bfg.b, slow=True)
    nbfg = mk("nbfg", [4, 1]); V(E.tensor_scalar(out=nbfg.t[:], in0=bfg.t[:], scalar1=-1.0, scalar2=None, op0=ALU.mult), [bfg], [nbfg])
    lnple = mk("lnple", [128, DM]); dma_in(lnple.t[:], lnple_d.rearrange("(o d) -> o d", o=1).partition_broadcast(128), lnple.b)
    lnfin = mk("lnfin", [128, DM]); dma_in(lnfin.t[:], lnfin_d.rearrange("(o d) -> o d", o=1).partition_broadcast(128), lnfin.b)
    epsc = mk("epsc", [128, 1]); G(E.memset(epsc.t[:], EPS), [], [epsc])

    wq = mk("wq", [128, 4, 128], BF16); wk = mk("wk", [128, 4, 128], BF16); wv = mk("wv", [128, 4, 128], BF16)


    def ld16(name, src):
        t = mk(name, [128, 16])
        dma_in(t.t[:, :], src.rearrange("(j g) q -> (g q) j", g=2), t.b, slow=True)
        return t
    lamre = ld16("lamre", lamre_d); lamim = ld16("lamim", lamim_d); logdt = ld16("logdt", logdt_d)
    s5tmp = [mk("s5tmp%d" % i, [128, 16]) for i in range(8)]
    lr = mk("lr", [128, 16]); dtt = mk("dtt", [128, 16]); mag = mk("mag", [128, 16])
    cs1 = mk("cs1", [128, 16]); sn1 = mk("sn1", [128, 16]); are = mk("are", [128, 16]); aim = mk("aim", [128, 16])
    gre = mk("gre", [128, 16]); gim = mk("gim", [128, 16])
    V(E.tensor_scalar(out=lr.t[:], in0=lamre.t[:], scalar1=-1e-4, scalar2=None, op0=ALU.min), [lamre], [lr])
    A(E.activation(out=dtt.t[:], in_=logdt.t[:], func=AF.Exp), [logdt], [dtt])
    t0, t1, t2, t3, t4, t5, t6, t7 = s5tmp
    V(E.tensor_tensor(out=t0.t[:], in0=lr.t[:], in1=dtt.t[:], op=ALU.mult), [lr, dtt], [t0])
    A(E.activation(out=mag.t[:], in_=t0.t[:], func=AF.Exp), [t0], [mag])
    th = mk("th", [128, 16])
    V(E.tensor_tensor(out=th.t[:], in0=lamim.t[:], in1=dtt.t[:], op=ALU.mult), [lamim, dtt], [th])
    ti32 = TL(p.sbuf("ti32", [128, 16], I32), Buf("ti32"))
    TWO_PI = 2.0 * math.pi

    def sin_reduced(dst, shift):
        V(E.tensor_scalar(out=t1.t[:], in0=th.t[:], scalar1=shift, scalar2=None, op0=ALU.add), [th], [t1])
        V(E.tensor_scalar(out=t2.t[:], in0=t1.t[:], scalar1=1.0 / TWO_PI, scalar2=0.5, op0=ALU.mult, op1=ALU.add), [t1], [t2])
        V(E.tensor_copy(out=ti32.t[:], in_=t2.t[:]), [t2], [ti32])
        V(E.tensor_copy(out=t3.t[:], in_=ti32.t[:]), [ti32], [t3])
        V(E.tensor_tensor(out=t4.t[:], in0=t3.t[:], in1=t2.t[:], op=ALU.is_gt), [t3, t2], [t4])
        V(E.tensor_tensor(out=t3.t[:], in0=t3.t[:], in1=t4.t[:], op=ALU.subtract), [t3, t4], [t3])
        V(E.scalar_tensor_tensor(out=t1.t[:], in0=t3.t[:], scalar=-TWO_PI, in1=t1.t[:], op0=ALU.mult, op1=ALU.add), [t3, t1], [t1])
        V(E.tensor_scalar(out=t1.t[:], in0=t1.t[:], scalar1=-math.pi, scalar2=math.pi, op0=ALU.max, op1=ALU.min), [t1], [t1])
        A(E.activation(out=dst.t[:], in_=t1.t[:], func=AF.Sin), [t1], [dst])
    sin_reduced(sn1, 0.0)
    sin_reduced(cs1, 0.5 * math.pi)
    V(E.tensor_tensor(out=are.t[:], in0=mag.t[:], in1=cs1.t[:], op=ALU.mult), [mag, cs1], [are])
    V(E.tensor_tensor(out=aim.t[:], in0=mag.t[:], in1=sn1.t[:], op=ALU.mult), [mag, sn1], [aim])
    V(E.tensor_tensor(out=t5.t[:], in0=lr.t[:], in1=lr.t[:], op=ALU.mult), [lr], [t5])
    V(E.tensor_tensor(out=t6.t[:], in0=lamim.t[:], in1=lamim.t[:], op=ALU.mult), [lamim], [t6])
    V(E.tensor_tensor(out=t5.t[:], in0=t5.t[:], in1=t6.t[:], op=ALU.add), [t5, t6], [t5])
    V(E.reciprocal(out=t5.t[:], in_=t5.t[:]), [t5], [t5])
    V(E.tensor_scalar(out=t6.t[:], in0=are.t[:], scalar1=-1.0, scalar2=None, op0=ALU.add), [are], [t6])
    V(E.tensor_tensor(out=t7.t[:], in0=t6.t[:], in1=lr.t[:], op=ALU.mult), [t6, lr], [t7])
    V(E.tensor_tensor(out=t0.t[:], in0=aim.t[:], in1=lamim.t[:], op=ALU.mult), [aim, lamim], [t0])
    V(E.tensor_tensor(out=t7.t[:], in0=t7.t[:], in1=t0.t[:], op=ALU.add), [t7, t0], [t7])
    V(E.tensor_tensor(out=gre.t[:], in0=t7.t[:], in1=t5.t[:], op=ALU.mult), [t7, t5], [gre])
    V(E.tensor_tensor(out=t7.t[:], in0=aim.t[:], in1=lr.t[:], op=ALU.mult), [aim, lr], [t7])
    V(E.tensor_tensor(out=t0.t[:], in0=t6.t[:], in1=lamim.t[:], op=ALU.mult), [t6, lamim], [t0])
    V(E.tensor_tensor(out=t7.t[:], in0=t7.t[:], in1=t0.t[:], op=ALU.subtract), [t7, t0], [t7])
    V(E.tensor_tensor(out=gim.t[:], in0=t7.t[:], in1=t5.t[:], op=ALU.mult), [t7, t5], [gim])

    CUT(2, gim)
    Ec = mk("Ec", [128, 16, 128]); Es = mk("Es", [128, 16, 128])
    pc = mk("pc", [128, 16]); psn = mk("psn", [128, 16])
    G(E.memset(Ec.t[:, :, 0:1], 1.0), [], [Ec])
    G(E.memset(Es.t[:, :, 0:1], 0.0), [], [Es])
    V(E.tensor_copy(out=pc.t[:], in_=cs1.t[:]), [cs1], [pc])
    V(E.tensor_copy(out=psn.t[:], in_=sn1.t[:]), [sn1], [psn])
    etmp = TL(stg[0].t[:, 0:1024].rearrange("p (j c) -> p j c", j=16), stg[0].b); etmp2 = TL(stg[1].t[:, 0:1024].rearrange("p (j c) -> p j c", j=16), stg[1].b)
    for k in range(7):
        L = 1 << k
        pcb = pc.t[:, :].unsqueeze(2).to_broadcast([128, 16, L])
        psb_ = psn.t[:, :].unsqueeze(2).to_broadcast([128, 16, L])
        V(E.tensor_tensor(out=etmp.t[:, :, 0:L], in0=Ec.t[:, :, 0:L], in1=pcb, op=ALU.mult), [Ec, pc], [etmp])
        V(E.tensor_tensor(out=etmp2.t[:, :, 0:L], in0=Es.t[:, :, 0:L], in1=psb_, op=ALU.mult), [Es, psn], [etmp2])
        V(E.tensor_tensor(out=Ec.t[:, :, L:2 * L], in0=etmp.t[:, :, 0:L], in1=etmp2.t[:, :, 0:L], op=ALU.subtract), [etmp, etmp2], [Ec])
        V(E.tensor_tensor(out=etmp.t[:, :, 0:L], in0=Ec.t[:, :, 0:L], in1=psb_, op=ALU.mult), [Ec, psn], [etmp])
        V(E.tensor_tensor(out=etmp2.t[:, :, 0:L], in0=Es.t[:, :, 0:L], in1=pcb, op=ALU.mult), [Es, pc], [etmp2])
        V(E.tensor_tensor(out=Es.t[:, :, L:2 * L], in0=etmp.t[:, :, 0:L], in1=etmp2.t[:, :, 0:L], op=ALU.add), [etmp, etmp2], [Es])
        if k < 6:
            V(E.tensor_tensor(out=t0.t[:], in0=pc.t[:], in1=pc.t[:], op=ALU.mult), [pc], [t0])
            V(E.tensor_tensor(out=t1.t[:], in0=psn.t[:], in1=psn.t[:], op=ALU.mult), [psn], [t1])
            V(E.tensor_tensor(out=t2.t[:], in0=pc.t[:], in1=psn.t[:], op=ALU.mult), [pc, psn], [t2])
            V(E.tensor_tensor(out=pc.t[:], in0=t0.t[:], in1=t1.t[:], op=ALU.subtract), [t0, t1], [pc])
            V(E.tensor_scalar(out=psn.t[:], in0=t2.t[:], scalar1=2.0, scalar2=None, op0=ALU.mult), [t2], [psn])

    CUT(3, TL(Es.t[:, 3, :], Es.b))
    BTre = mk("BTre", [128, 16, 128], BF16); BTim = mk("BTim", [128, 16, 128], BF16)
    CTre = mk("CTre", [128, 16, 128], BF16); CTimn = mk("CTimn", [128, 16, 128], BF16)
    padA = TL(stg[0].t[:, 0:2048].rearrange("p (j c) -> p j c", j=16), stg[0].b)
    padB = TL(stg[1].t[:, 0:2048].rearrange("p (j c) -> p j c", j=16), stg[1].b)
    ytile = mk("ytile", [128, 1088])
    btmp = TL(ytile.t[:, 0:128], ytile.b); btmp2 = TL(ytile.t[:, 128:256], Buf("btmp2"))
    G(E.memset(stg[0].t[:, 0:2048], 0.0), [], [stg[0]])
    G(E.memset(stg[1].t[:, 0:2048], 0.0), [], [stg[1]])
    Brv = Bre_d.rearrange("(m r) q c -> r q m c", r=8)
    Biv = Bim_d.rearrange("(m r) q c -> r q m c", r=8)
    for q in range(4):
        for g2 in range(2):
            r = 2 * q + g2
            off = 32 * q + 16 * g2
            dma_in(padA.t[g2 * 64:(g2 + 1) * 64, q::4, off:off + 16], Brv[r], stg[0].b, slow=True, join=True)
            dma_in(padB.t[g2 * 64:(g2 + 1) * 64, q::4, off:off + 16], Biv[r], stg[1].b, slow=True, join=True)
    for j in range(16):
        V(E.tensor_scalar(out=btmp.t[:], in0=padB.t[:, j, :], scalar1=gim.t[:, j:j + 1], scalar2=None, op0=ALU.mult), [stg[1], gim], [btmp])
        V(E.scalar_tensor_tensor(out=btmp.t[:], in0=padA.t[:, j, :], scalar=gre.t[:, j:j + 1], in1=btmp.t[:], op0=ALU.mult, op1=ALU.subtract), [stg[0], gre, btmp], [btmp])
        ps = PS()
        T(E.transpose(out=ps.t[:, 0:128], in_=btmp.t[:], identity=ident.t[:]), [btmp, ident], [ps])
        A(E.activation(out=BTre.t[:, j, :], in_=ps.t[:, 0:128], func=AF.Copy), [ps], [BTre])
        V(E.tensor_scalar(out=btmp2.t[:], in0=padA.t[:, j, :], scalar1=gim.t[:, j:j + 1], scalar2=None, op0=ALU.mult), [stg[0], gim], [btmp2])
        V(E.scalar_tensor_tensor(out=btmp2.t[:], in0=padB.t[:, j, :], scalar=gre.t[:, j:j + 1], in1=btmp2.t[:], op0=ALU.mult, op1=ALU.add), [stg[1], gre, btmp2], [btmp2])
        ps = PS()
        T(E.transpose(out=ps.t[:, 0:128], in_=btmp2.t[:], identity=ident.t[:]), [btmp2, ident], [ps])
        A(E.activation(out=BTim.t[:, j, :], in_=ps.t[:, 0:128], func=AF.Copy), [ps], [BTim])
    CUT(4, btmp2)
    G(E.memset(stg[0].t[:, 0:2048], 0.0), [stg[0]], [stg[0]])
    G(E.memset(stg[1].t[:, 0:2048], 0.0), [stg[1]], [stg[1]])
    Crv = Cre_d.rearrange("(m r) c q -> r c m q", r=8)
    Civ = Cim_d.rearrange("(m r) c q -> r c m q", r=8)
    for q in range(4):
        for g2 in range(2):
            r = 2 * q + g2
            off = 32 * q + 16 * g2
            dma_in(padA.t[off:off + 16, q::4, g2 * 64:(g2 + 1) * 64], Crv[r], stg[0].b, slow=True, join=True)
            dma_in(padB.t[off:off + 16, q::4, g2 * 64:(g2 + 1) * 64], Civ[r], stg[1].b, slow=True, join=True)
    for j in range(16):
        ps = PS()
        T(E.transpose(out=ps.t[:, 0:128], in_=padA.t[:, j, :], identity=ident.t[:]), [stg[0], ident], [ps])
        A(E.activation(out=CTre.t[:, j, :], in_=ps.t[:, 0:128], func=AF.Copy), [ps], [CTre])
        ps = PS()
        T(E.transpose(out=ps.t[:, 0:128], in_=padB.t[:, j, :], identity=ident.t[:]), [stg[1], ident], [ps])
        A(E.activation(out=CTimn.t[:, j, :], in_=ps.t[:, 0:128], func=AF.Copy, scale=-1.0), [ps], [CTimn])

    CUT(5, TL(stg[1].t[:, 0:16], stg[1].b))
    win = mk("win", [128, 8, INC], BF16)
    wout = mk("wout", [128, 8, DM], BF16); wgate = mk("wgate", [128, 8, DM], BF16)
    wple = mk("wple", [128, 2, DM], BF16); wglu = mk("wglu", [128, 4, 512], BF16)
    si = [0]

    def load_cast(dst_ap, src_ap, ncols, dstb, scale_ap=None, scale_b=None):
        s = stg[si[0] % 2]
        si[0] += 1
        dma_in(s.t[:, 0:ncols], src_ap, s.b)
        if scale_ap is not None:
            A(E.activation(out=dst_ap, in_=s.t[:, 0:ncols], func=AF.Copy, scale=scale_ap), [s, scale_b], [dstb])
        elif si[0] % 2 == 0:
            A(E.activation(out=dst_ap, in_=s.t[:, 0:ncols], func=AF.Copy), [s], [dstb])
        else:
            G(E.tensor_copy(out=dst_ap, in_=s.t[:, 0:ncols]), [s], [dstb])
    for k in range(8):
        load_cast(win.t[:, k, :], w_in_d[k * 128:(k + 1) * 128, :], INC, win.b, gmix.t[:, k:k + 1], gmix)
    for k in range(8):
        load_cast(wout.t[:, k, :], wout_d[k * 128:(k + 1) * 128, :], DM, wout.b)
        load_cast(wgate.t[:, k, :], wgate_d[k * 128:(k + 1) * 128, :], DM, wgate.b)
    for k in range(2):
        load_cast(wple.t[:, k, :], wple_d[k * 128:(k + 1) * 128, :], DM, wple.b)
    for k in range(4):
        load_cast(wglu.t[:, k, :], wglu_d[k * 128:(k + 1) * 128, :], 512, wglu.b)

    CUT(6, TL(stg[1].t[:, 0:16], stg[1].b))
    for wt_, src_, sc_ in ((wq, wq_d, 1.0), (wk, wk_d, float(128 ** -0.5)), (wv, wv_d, 1.0)):
        s_ = stg[si[0] % 2]
        si[0] += 1
        dma_in(s_.t[:, 0:512].rearrange("p (h e) -> p h e", h=4), src_.rearrange("h d e -> d h e"), s_.b)
        V(E.tensor_scalar(out=wt_.t[:, :, :], in0=s_.t[:, 0:512].rearrange("p (h e) -> p h e", h=4), scalar1=sc_, scalar2=None, op0=ALU.mult), [s_], [wt_])
    Caug = [mk("Caug%d" % h, [128, 129]) for h in range(4)]
    for h in range(4):
        V(E.memset(Caug[h].t[:], 0.0), [], [Caug[h]])
    Bn_c = mk("Bn_c", [4, 1]); M_c = mk("M_c", [4, 1])
    V(E.memset(Bn_c.t[:], 0.0), [], [Bn_c])
    V(E.memset(M_c.t[:], 0.0), [], [M_c])
    azin = mk("azin", [128, 16, 2])
    V(E.memset(azin.t[:], 0.0), [], [azin])
    xl_re = mk("xl_re", [128, 16]); xl_im = mk("xl_im", [128, 16])
    xmh = mk("xmh", [128, 4, 131])
    V(E.memset(xmh.t[:], 0.0), [], [xmh])

    def alias(name, ap, like):
        return TL(ap, Buf(name, like=like))
    xtok = alias("xtok", stg[0].t[:, 0:1024], [stg[0].b])
    h2 = alias("h2", stg[0].t[:, 1024:2048], [stg[0].b])
    sgate = alias("sgate", stg[1].t[:, 0:1024], [stg[1].b])
    esb = alias("esb", stg[1].t[:, 1024:2048], [stg[1].b])
    scr = ytile
    ptok = alias("ptok", stg[0].t[:, 2048:2304], [stg[0].b])
    aT = mk("aT", [128, 8, 128], BF16); pT = mk("pT", [128, 2, 128], BF16)
    szm = mk("szm", [128, 4, 128]); so = mk("so", [128, 512])
    tm = TL(so.t, so.b)
    xs5 = mk("xs5", [128, 4, 128]); xs5b = mk("xs5b", [128, 4, 128], BF16); szs = mk("szs", [128, 4, 128])
    xc = mk("xc", [128, 4, 128])
    xms = mk("xms", [128, 4, 16, 7]); xmc = TL(xmh.t[:, :, 0:64], xmh.b)
    qT = mk("qT", [128, 2, 128], BF16, nb=2); qTd = mk("qTd", [128, 2, 128], BF16, nb=2); kT = mk("kT", [128, 2, 128], BF16, nb=2)
    kw = mk("kw", [128, 2, 128], BF16, nb=2); vaug = mk("vaug", [128, 2, 130], BF16, nb=2)
    xcb = mk("xcb", [128, 4, 128], BF16); xmb_ = mk("xmb_", [128, 4, 128], BF16)
    sdb = [mk("sdb%d" % i, [128, 128], BF16) for i in range(2)]
    Cb = [mk("Cb%d" % h, [128, 130], BF16) for h in range(4)]
    for h in range(4):
        V(E.memset(Cb[h].t[:], 0.0), [], [Cb[h]])
    V(E.memset(vaug.t[:], 1.0), [], [vaug])
    narg = [alias("narg%d" % i, stg[0].t[:, 2304 + i * 128:2432 + i * 128], [stg[0].b]) for i in range(2)]
    DTt = narg
    SDt = sdb
    hA = mk("hA", [128, 2, 128], nb=2); hn = [alias("hn%d" % i, stg[1].t[:, 2048 + i * 128:2176 + i * 128], [stg[1].b]) for i in range(2)]
    mtmp = [alias("mtmp%d" % i, stg[1].t[:, 2304 + i * 128:2432 + i * 128], [stg[1].b]) for i in range(2)]
    mix = mk("mix", [128, 8, 128], BF16, nb=8)
    bnst = mk("bnst", [128, 4, 6]); mv = mk("mv", [128, 4, 2]); rstdh = mk("rstdh", [128, 4])
    den = mk("den", [128, 4]); rden = mk("rden", [128, 4]); dl = mk("dl", [128, 4])
    colq = mk("colq", [128, 12])
    ssq = mk("ssq", [128, 4]); rstd = mk("rstd", [128, 4])
    gi = mk("gi", [4, 128]); gsp = mk("gsp", [4, 128]); gBn = mk("gBn", [4, 128]); ga = gi
    gM = mk("gM", [4, 128]); gdec = mk("gdec", [4, 128]); gwl = mk("gwl", [4, 128]); gem = mk("gem", [4, 128])
    gtmp = mk("gtmp", [4, 128]); ga2 = gtmp; gneg = mk("gneg", [4, 1]); mnew = mk("mnew", [4, 16])
    m0c = mk("m0c", [4, 16]); m0row = mk("m0row", [4, 64])
    NS5 = 2
    s5P = [mk("s5P%d" % r, [128, 512]) for r in range(NS5)]
    s5W = [TL(hA.t[:, :, :].rearrange("p a b -> p (a b)"), hA.b)] + [mk("s5W%d" % r, [128, 256]) for r in range(1, NS5)]
    s5Z = [mk("s5Z%d" % r, [128, 256]) for r in range(NS5)]
    s5xb = [mk("s5xb%d" % r, [128, 256], BF16) for r in range(NS5)]
    rmat = [mk("rmat%d" % r, [128, 256]) for r in range(NS5)]
    ysb = mk("ysb", [128, 4, 128], nb=4); yg = ysb; ygb = mk("ygb", [128, 4, 128], BF16)
    gl1 = [mk("gl1_%d" % i, [128, 128]) for i in range(2)]; gl2 = gl1
    h2T = mk("h2T", [128, 8, 128], BF16)
    qz = ytile
    kwm = mk("kwm", [64, 8, 128], BF16); wlm = mk("wlm", [64, 16])
    Cst = Caug
    h0tmp = TL(ytile.t[:, 0:256].rearrange("p (j b) -> p j b", j=16), ytile.b)
    ah0j = [mk("ah0j", [128, 2, 16])] * NS5
    naim = mk("naim", [128, 16])
    sore = mk("sore", [128, 16, 16]); soim = mk("soim", [128, 16, 16])
    nT = mk("nT", [128, 64]); nOut = nT
    cnt = {"n": 0}
    CUT(7, TL(xmh.t[:, 0, :], xmh.b))

    def tile_src(kind, ti):
        if kind == "s":
            return xs_d[:, :], psm_d[:, :], 64
        return xp_d[ti * 128:(ti + 1) * 128, :], pp_d[ti * 128:(ti + 1) * 128, :], 128

    def load_inputs(kind, ti):
        xs_, ps_, n_ = tile_src(kind, ti)
        dma_in(xtok.t[0:n_, :], xs_, xtok.b)
        dma_in(ptok.t[0:n_, :], ps_, ptok.b)

    def front(kind, ti, nxt=None):
        smp = kind == "s"
        NT = 64 if smp else 128
        A(E.activation(out=scr.t[0:NT, 0:DM], in_=xtok.t[0:NT, :], func=AF.Square, accum_out=ssq.t[0:NT, 0:1]), [xtok], [scr, ssq])
        A(E.activation(out=rstd.t[0:NT, 0:1], in_=ssq.t[0:NT, 0:1], func=AF.Sqrt, bias=epsc.t[0:NT, :], scale=1.0 / DM), [ssq, epsc], [rstd])
        V(E.reciprocal(out=rstd.t[0:NT, 0:1], in_=rstd.t[0:NT, 0:1]), [rstd], [rstd])
        V(E.tensor_scalar(out=scr.t[0:NT, 0:DM], in0=xtok.t[0:NT, :], scalar1=rstd.t[0:NT, 0:1], scalar2=None, op0=ALU.mult), [xtok, rstd], [scr])
        for half in range(2):
            ps = PS()
            for kk in range(4):
                k = half * 4 + kk
                T(E.transpose(out=ps.t[:, kk * 128:kk * 128 + NT], in_=scr.t[0:NT, k * 128:(k + 1) * 128], identity=ident.t[0:NT, 0:NT]), [scr, ident], [ps])
            A(E.activation(out=aT.t[:, half * 4:half * 4 + 4, 0:NT], in_=ps.t[:, :].rearrange("p (k t) -> p k t", k=4)[:, :, 0:NT], func=AF.Copy), [ps], [aT])
        ps = PS()
        for k in range(2):
            T(E.transpose(out=ps.t[:, k * 128:k * 128 + NT], in_=ptok.t[0:NT, k * 128:(k + 1) * 128], identity=ident.t[0:NT, 0:NT]), [ptok, ident], [ps])
        V(E.tensor_copy(out=pT.t[:, :, 0:NT], in_=ps.t[:, 0:256].rearrange("p (k t) -> p k t", k=2)[:, :, 0:NT]), [ps], [pT])

        if nxt is not None:
            load_inputs(*nxt)

    def do_tile(kind, ti, carry=(), nxt=None, glu_prev=None):
        smp = kind == "s"
        NT = 64 if smp else 128
        x_src = xs_d[:, :] if smp else xp_d[ti * 128:(ti + 1) * 128, :]
        p_src = psm_d[:, :] if smp else pp_d[ti * 128:(ti + 1) * 128, :]
        y_dst = ys_o[:, :] if smp else yp_o[ti * 128:(ti + 1) * 128, :]
        psM = psb[5]; psD = psb[6]
        def proj_fm(col0):
            ps = PS()
            for k in range(8):
                T(E.matmul(ps.t[:, 0:NT], win.t[:, k, col0:col0 + 128], aT.t[:, k, 0:NT], start=(k == 0), stop=(k == 7)), [win, aT], [ps])
            return ps

        def g_block(col0, kind):
            ps = PS()
            for k in range(8):
                T(E.matmul(ps.t[0:NT, :], aT.t[:, k, 0:NT], win.t[:, k, col0:col0 + 512], start=(k == 0), stop=(k == 7)), [win, aT], [ps])
                yield
            if kind == "om":
                A(E.activation(out=so.t[0:NT, :], in_=ps.t[0:NT, :], func=AF.Sigmoid), [ps], [so])
                return
            A(E.activation(out=tm.t[0:NT, :], in_=ps.t[0:NT, :], func=AF.Copy), [ps], [tm])
            yield
            pt = PS()
            for c in range(4):
                T(E.transpose(out=pt.t[:, c * 128:c * 128 + NT], in_=tm.t[0:NT, c * 128:(c + 1) * 128], identity=ident.t[0:NT, 0:NT]), [tm, ident], [pt])
            ptv = pt.t[:, :].rearrange("p (c t) -> p c t", c=4)[:, :, 0:NT]
            if kind == "xm":
                if smp:
                    A(E.activation(out=xmc.t[:, :, 0:NT], in_=ptv, func=AF.Copy), [pt], [xmc])
                    G(E.tensor_copy(out=xmb_.t[:, :, 0:NT], in_=xmc.t[:, :, 0:NT]), [xmc], [xmb_])
                else:
                    A(E.activation(out=xmh.t[:, :, 3:3 + NT], in_=ptv, func=AF.Copy), [pt], [xmh])
                    G(E.tensor_copy(out=xmb_.t[:, :, 0:NT], in_=xmh.t[:, :, 3:3 + NT]), [xmh], [xmb_])
            elif kind == "zm":
                A(E.activation(out=szm.t[:, :, 0:NT], in_=ptv, func=AF.Silu), [pt], [szm])
            elif kind == "xs":
                A(E.activation(out=xs5.t[:, :, 0:NT], in_=ptv, func=AF.Copy), [pt], [xs5])
                V(E.tensor_copy(out=xs5b.t[:, :, 0:NT], in_=xs5.t[:, :, 0:NT]), [xs5], [xs5b])
            elif kind == "zs":
                A(E.activation(out=szs.t[:, :, 0:NT], in_=ptv, func=AF.Silu), [pt], [szs])

        def g_gates():
            psg = PS()
            for k in range(8):
                T(E.matmul(psg.t[0:4, 0:NT], win.t[:, k, 1536:1540], aT.t[:, k, 0:NT], start=(k == 0), stop=(k == 7)), [win, aT], [psg])
            for k in range(8):
                T(E.matmul(psg.t[0:4, 128:128 + NT], win.t[:, k, 1540:1544], aT.t[:, k, 0:NT], start=(k == 0), stop=(k == 7)), [win, aT], [psg])
            A(E.activation(out=gi.t[:, 0:NT], in_=psg.t[0:4, 0:NT], func=AF.Identity, bias=big_.t[:, :], scale=1.0), [psg, big_], [gi])
            A(E.activation(out=gsp.t[:, 0:NT], in_=psg.t[0:4, 128:128 + NT], func=AF.Exp, bias=nbfg.t[:, :], scale=-1.0), [psg, nbfg], [gsp])
        def g_eproj():
            pse = [PS(), PS()]
            for half in range(2):
                for k in range(2):
                    T(E.matmul(pse[half].t[0:NT, :], pT.t[:, k, 0:NT], wple.t[:, k, half * 512:(half + 1) * 512], start=(k == 0), stop=(k == 1)), [pT, wple], [pse[half]])
                A(E.activation(out=esb.t[0:NT, half * 512:(half + 1) * 512], in_=pse[half].t[0:NT, :], func=AF.Square, accum_out=ssq.t[0:NT, 1 + half:2 + half]), [pse[half]], [esb, ssq])
            V(E.tensor_tensor(out=ssq.t[0:NT, 1:2], in0=ssq.t[0:NT, 1:2], in1=ssq.t[0:NT, 2:3], op=ALU.add), [ssq], [ssq])
            A(E.activation(out=rstd.t[0:NT, 1:2], in_=ssq.t[0:NT, 1:2], func=AF.Sqrt, bias=epsc.t[0:NT, :], scale=1.0 / DM), [ssq, epsc], [rstd])
            V(E.reciprocal(out=rstd.t[0:NT, 1:2], in_=rstd.t[0:NT, 1:2]), [rstd], [rstd])
            for half in range(2):
                sl = slice(half * 512, (half + 1) * 512)
                V(E.scalar_tensor_tensor(out=esb.t[0:NT, sl], in0=pse[half].t[0:NT, :], scalar=rstd.t[0:NT, 1:2], in1=lnple.t[0:NT, sl], op0=ALU.mult, op1=ALU.mult), [pse[half], rstd, lnple], [esb])

        for _ in g_block(1544, "xs"):
            pass
        own = [(lambda: g_block(0, "xm"), None), (g_gates, None), (lambda: g_block(512, "zm"), None),
               (lambda: g_block(2056, "zs"), None), (lambda: g_block(1024, "om"), None)]
        pending = own
        cin = list(carry) + [g_eproj]
        if nxt is not None:
            cin.append(lambda: front(*nxt))

        def pop_carry(n=1):
            for _ in range(n):
                if cin:
                    cin.pop(0)()

        def pop_pending(background=True):
            if pending:
                f, a = pending.pop(0)
                r_ = f() if a is None else f(a)
                if r_ is not None:
                    if background:
                        bg.append(r_)
                    else:
                        for _ in r_:
                            pass

        def gates_gen():
            A(E.activation(out=gsp.t[:, 0:NT], in_=gsp.t[:, 0:NT], func=AF.Ln, bias=1.0), [gsp], [gsp])
            if smp:
                V(E.tensor_tensor_scan(out=gBn.t[:, 0:NT], data0=seqm.t[0:4, :], data1=gsp.t[:, 0:NT], initial=0.0, op0=ALU.mult, op1=ALU.add), [seqm, gsp], [gBn])
                yield
            else:
                V(E.tensor_tensor_scan(out=gBn.t[:, 0:NT], data0=ones4.t[0:4, 0:NT], data1=gsp.t[:, 0:NT], initial=Bn_c.t[:, :], op0=ALU.mult, op1=ALU.add), [ones4, gsp, Bn_c], [gBn])
                yield
            V(E.tensor_tensor(out=ga.t[:, 0:NT], in0=gi.t[:, 0:NT], in1=gBn.t[:, 0:NT], op=ALU.add), [gi, gBn], [ga])
            yield
            if smp:
                V(E.tensor_copy(out=ga2.t[:, 0:NT], in_=ga.t[:, 0:NT]), [ga], [ga2])
                yield
                a2v = ga2.t[:, 0:64].rearrange("h (b t) -> h b t", t=4)
                V(E.tensor_tensor(out=a2v[:, :, 0:1], in0=a2v[:, :, 0:1], in1=m0c.t[:, :].unsqueeze(2), op=ALU.max), [ga2, m0c], [ga2])
                yield
                V(E.tensor_tensor_scan(out=gM.t[:, 0:NT], data0=snb.t[:, :], data1=ga2.t[:, 0:NT], initial=0.0, op0=ALU.add, op1=ALU.max), [snb, ga2], [gM])
                yield
                V(E.tensor_tensor(out=gtmp.t[:, 0:NT], in0=m0row.t[:, :], in1=gM.t[:, 0:NT], op=ALU.subtract), [m0row, gM], [gtmp])
                yield
                A(E.activation(out=gdec.t[:, 0:NT], in_=gtmp.t[:, 0:NT], func=AF.Exp), [gtmp], [gdec])
                yield
                Mv = gM.t[:, 0:64].rearrange("h (b t) -> h b t", t=4)
                V(E.tensor_tensor(out=gtmp.t[:, 0:64].rearrange("h (b t) -> h b t", t=4), in0=ga.t[:, 0:64].rearrange("h (b t) -> h b t", t=4),
                                            in1=Mv[:, :, 3:4].to_broadcast([4, 16, 4]), op=ALU.subtract), [ga, gM], [gtmp])
                A(E.activation(out=gwl.t[:, 0:NT], in_=gtmp.t[:, 0:NT], func=AF.Exp), [gtmp], [gwl])
                yield
                V(E.tensor_tensor(out=mnew.t[:, :].unsqueeze(2), in0=Mv[:, :, 3:4], in1=gBn.t[:, 0:64].rearrange("h (b t) -> h b t", t=4)[:, :, 3:4], op=ALU.subtract), [gM, gBn], [mnew])
                yield
            else:
                V(E.tensor_tensor_scan(out=gM.t[:, 0:NT], data0=ones4.t[0:4, 0:NT], data1=ga.t[:, 0:NT], initial=M_c.t[:, :], op0=ALU.mult, op1=ALU.max), [ones4, ga, M_c], [gM])
                yield
                A(E.activation(out=gdec.t[:, 0:NT], in_=gM.t[:, 0:NT], func=AF.Exp, bias=M_c.t[:, :], scale=-1.0), [gM, M_c], [gdec])
                yield
                V(E.tensor_scalar(out=gneg.t[:, :], in0=gM.t[:, NT - 1:NT], scalar1=-1.0, scalar2=None, op0=ALU.mult), [gM], [gneg])
                yield
                A(E.activation(out=gwl.t[:, 0:NT], in_=ga.t[:, 0:NT], func=AF.Exp, bias=gneg.t[:, :], scale=1.0), [ga, gneg], [gwl])
                yield
                V(E.tensor_tensor(out=mnew.t[:, 0:1], in0=gM.t[:, NT - 1:NT], in1=gBn.t[:, NT - 1:NT], op=ALU.subtract), [gM, gBn], [mnew])
                yield
            V(E.tensor_tensor(out=gtmp.t[:, 0:NT], in0=gBn.t[:, 0:NT], in1=gM.t[:, 0:NT], op=ALU.subtract), [gBn, gM, gwl], [gtmp])
            yield
            A(E.activation(out=gem.t[:, 0:NT], in_=gtmp.t[:, 0:NT], func=AF.Exp), [gtmp], [gem])
            yield
            if not smp:
                V(E.tensor_copy(out=Bn_c.t[:, :], in_=gBn.t[:, NT - 1:NT]), [gBn], [Bn_c])
                yield
                V(E.tensor_copy(out=M_c.t[:, :], in_=gM.t[:, NT - 1:NT]), [gM], [M_c])
                yield
            psc = PS()
            for qi, gt in enumerate((ga, gwl, gem)):
                T(E.transpose(out=psc.t[0:NT, qi * 4:qi * 4 + 4], in_=gt.t[:, 0:NT], identity=ident.t[0:4, 0:4]), [gt, ident], [psc])
                yield
            V(E.tensor_copy(out=colq.t[0:NT, :], in_=psc.t[0:NT, 0:12]), [psc], [colq])
            yield
            for h in range(4):
                T(E.matmul(psM.t[:, h * 128:h * 128 + NT], sel.t[:, h * 128:(h + 1) * 128], gM.t[:, 0:NT], start=True, stop=True), [sel, gM], [psM])
                yield
            for h in range(4):
                T(E.matmul(psD.t[:, h * 128:h * 128 + NT], sel.t[:, h * 128:(h + 1) * 128], gdec.t[:, 0:NT], start=True, stop=True), [sel, gdec], [psD])
                yield


        def conv_gen():
            for c in range(4):
                if smp:
                    G(E.tensor_copy(out=xms.t[:, c, :, 3:7], in_=xmc.t[:, c, :].rearrange("p (b t) -> p b t", t=4)), [xmc], [xms])
                    yield
                    src = lambda j, c=c: xms.t[:, c, :, j:j + 4]
                    dstv = xc.t[:, c, 0:64].rearrange("p (b t) -> p b t", t=4)
                    rb = [xms]
                else:
                    src = lambda j, c=c: xmh.t[:, c, j:j + NT]
                    dstv = xc.t[:, c, 0:NT]
                    rb = [xmh]
                V(E.tensor_scalar(out=dstv, in0=src(0), scalar1=convw.t[:, c, 0:1], scalar2=convb.t[:, c:c + 1], op0=ALU.mult, op1=ALU.add), rb + [convw, convb], [xc])
                yield
                for j in range(1, 4):
                    V(E.scalar_tensor_tensor(out=dstv, in0=src(j), scalar=convw.t[:, c, j:j + 1], in1=dstv, op0=ALU.mult, op1=ALU.add), rb + [convw, xc], [xc])
                    yield
                A(E.activation(out=xc.t[:, c, 0:NT], in_=xc.t[:, c, 0:NT], func=AF.Silu), [xc], [xc])
                yield
                G(E.tensor_copy(out=xcb.t[:, c, 0:NT], in_=xc.t[:, c, 0:NT]), [xc], [xcb])
                yield
            if smp:
                G(E.memset(qz.t[:, :], 0.0), [], [qz])
                yield

        def pr(ap, c):
            if smp:
                return ap.rearrange("p (c b t) -> p c b t", c=c, t=4)
            return ap.rearrange("p (c t) -> p c t", c=c)

        def tb(tab, j):
            if smp:
                return tab.t[:, j, 0:4].unsqueeze(1).unsqueeze(1).to_broadcast([128, 2, 16, 4])
            return tab.t[:, j, 0:NT].unsqueeze(1).to_broadcast([128, 2, NT])
        N2, N3, N4 = 2 * NT, 3 * NT, 4 * NT
        def mlstm_head(h):
                xmcur = xmb_.t[:, h, 0:NT]
                xmb = xmb_
                ps = PS()
                T(E.matmul(ps.t[:, 0:NT], wq.t[:, h, :], xcb.t[:, h, 0:NT], start=True, stop=True), [wq, xcb], [ps])
                T(E.matmul(ps.t[:, 128:128 + NT], wk.t[:, h, :], xcb.t[:, h, 0:NT], start=True, stop=True), [wk, xcb], [ps])
                psk = PS()
                T(E.matmul(psk.t[0:NT, 256:384], xcb.t[:, h, 0:NT], wk.t[:, h, :], start=True, stop=True), [wk, xcb], [psk])
                T(E.matmul(psk.t[0:NT, 384:512], xmcur, wv.t[:, h, :], start=True, stop=True), [wv, xmb], [psk])
                A(E.activation(out=qT.t[:, h % 2, 0:NT], in_=ps.t[:, 0:NT], func=AF.Copy), [ps], [qT.b[h % 2]])
                A(E.activation(out=kT.t[:, h % 2, 0:NT], in_=ps.t[:, 128:128 + NT], func=AF.Copy), [ps], [kT.b[h % 2]])
                V(E.tensor_copy(out=vaug.t[0:NT, h % 2, 0:128], in_=psk.t[0:NT, 384:512]), [psk], [vaug.b[h % 2]])
                V(E.tensor_tensor(out=qTd.t[:, h % 2, 0:NT], in0=qT.t[:, h % 2, 0:NT], in1=psD.t[:, h * 128:h * 128 + NT], op=ALU.mult), [qT.b[h % 2], psD], [qTd.b[h % 2]])
                if smp:
                    V(E.tensor_scalar(out=wlm.t[:, :], in0=oneh.t[:, :], scalar1=colq.t[0:64, 4 + h:5 + h], scalar2=None, op0=ALU.mult), [oneh, colq], [wlm])
                    V(E.tensor_copy(out=kw.t[0:64, h % 2, :], in_=psk.t[0:64, 256:384]), [psk], [kw.b[h % 2]])
                else:
                    V(E.tensor_scalar(out=kw.t[0:NT, h % 2, :], in0=psk.t[0:NT, 256:384], scalar1=colq.t[0:NT, 4 + h:5 + h], scalar2=None, op0=ALU.mult), [psk, colq], [kw.b[h % 2]])
                yield

                r2 = cnt["n"] % 2
                cnt["n"] += 1
                na, dtt_, sd, hn_, mt = narg[r2], DTt[r2], SDt[r2], hn[r2], mtmp[r2]
                pmask = pms if smp else pmp
                V(E.scalar_tensor_tensor(out=na.t[0:NT, 0:NT], in0=psM.t[0:NT, h * 128:h * 128 + NT], scalar=colq.t[0:NT, h:h + 1],
                                                                           in1=pmask.t[0:NT, 0:NT], op0=ALU.subtract, op1=ALU.add), [psM, colq, pmask], [na])
                A(E.activation(out=dtt_.t[0:NT, 0:NT], in_=na.t[0:NT, 0:NT], func=AF.Exp, scale=-1.0), [na], [dtt_])
                yield
                ps2 = PS()
                T(E.matmul(ps2.t[0:NT, 0:NT], kT.t[:, h % 2, 0:NT], qT.t[:, h % 2, 0:NT], start=True, stop=True), [kT.b[h % 2], qT.b[h % 2]], [ps2])
                V(E.tensor_tensor(out=sd.t[0:NT, 0:NT], in0=ps2.t[0:NT, 0:NT], in1=dtt_.t[0:NT, 0:NT], op=ALU.mult), [ps2, dtt_], [sd])
                V(E.tensor_copy(out=dl.t[:, h:h + 1], in_=psD.t[:, h * 128 + NT - 1:h * 128 + NT]), [psD], [dl])
                yield
                psn_ = psb[7] if h % 2 == 0 else psb[4]
                if not smp:
                    T(E.matmul(psn_.t[0:NT, 0:129], sd.t[0:NT, 0:NT], vaug.t[0:NT, h % 2, 0:129], start=True, stop=False), [sd, vaug.b[h % 2]], [psn_])
                    T(E.matmul(psn_.t[0:NT, 0:129], qTd.t[:, h % 2, 0:NT], Cb[h].t[:, 0:129], start=False, stop=True), [qTd.b[h % 2], Cb[h]], [psn_])
                    psu = PS()
                    T(E.matmul(psu.t[:, 0:129], kw.t[0:NT, h % 2, :], vaug.t[0:NT, h % 2, 0:129], start=True, stop=True), [kw.b[h % 2], vaug.b[h % 2]], [psu])
                    V(E.scalar_tensor_tensor(out=Caug[h].t[:, :], in0=Caug[h].t[:, :], scalar=dl.t[:, h:h + 1], in1=psu.t[:, 0:129], op0=ALU.mult, op1=ALU.add), [Caug[h], dl, psu], [Caug[h]])
                    A(E.activation(out=Cb[h].t[:, 0:129], in_=Caug[h].t[:, :], func=AF.Copy), [Caug[h]], [Cb[h]])
                else:
                    V(E.tensor_copy(out=qz.t[:, :].rearrange("p (b x) -> p b x", x=68)[:, :, 0:4], in_=qTd.t[:, h % 2, 0:64].rearrange("p (b t) -> p b t", t=4)), [qTd.b[h % 2]], [qz])
                    T(E.matmul(psn_.t[0:NT, 0:129], sd.t[0:NT, 0:NT], vaug.t[0:NT, h % 2, 0:129], start=True, stop=False), [sd, vaug.b[h % 2]], [psn_])
                    for b in range(SB):
                        if b % 8 == 0:
                            V(E.tensor_tensor(out=kwm.t[:, :, :], in0=kw.t[0:64, h % 2, :].unsqueeze(1).to_broadcast([64, 8, 128]),
                                              in1=wlm.t[:, b:b + 8].unsqueeze(2).to_broadcast([64, 8, 128]), op=ALU.mult), [kw.b[h % 2], wlm], [kwm])
                        cs = Cst[(b + h * SB) % 4]
                        cn = cs

                        def fetch_state(b_):
                            cs_ = Cst[(b_ + h * SB) % 4]
                            dma_in(cs_.t[:, 0:128], sC_d[b_, h, :, :], cs_.b)
                            A(E.activation(out=cs_.t[:, 128:129], in_=nT.t[:, b_ * 4 + h:b_ * 4 + h + 1], func=AF.Copy), [nT], [cs_], join=True)
                        if b == 0:
                            fetch_state(0)
                        if b + 1 < SB:
                            fetch_state(b + 1)
                        T(E.matmul(psn_.t[0:NT, 0:129], qz.t[:, b * 64:b * 64 + 64], cs.t[:, :], start=False, stop=(b == SB - 1)), [qz, cs], [psn_])
                        psu = PS()
                        T(E.matmul(psu.t[:, 0:129], kwm.t[:, b % 8, :], vaug.t[0:64, h % 2, 0:129], start=True, stop=True), [kwm, vaug.b[h % 2]], [psu])
                        V(E.scalar_tensor_tensor(out=cn.t[:, :], in0=cs.t[:, :], scalar=psD.t[:, h * 128 + 4 * b + 3:h * 128 + 4 * b + 4], in1=psu.t[:, 0:129],
                                                                                           op0=ALU.mult, op1=ALU.add), [cs, psD, psu], [cn])
                        A(E.activation(out=nOut.t[:, b * 4 + h:b * 4 + h + 1], in_=cn.t[:, 128:129], func=AF.Copy), [cn], [nOut])
                        D2(E.dma_start(out=oC_o[b, h, :, :], in_=cn.t[:, 0:128]), [cn.b], [], final=True)
                yield
                pop_carry(2 if h == 0 else 1)
                A(E.activation(out=den.t[0:NT, h:h + 1], in_=psn_.t[0:NT, 128:129], func=AF.Abs), [psn_], [den])
                V(E.tensor_tensor(out=den.t[0:NT, h:h + 1], in0=den.t[0:NT, h:h + 1], in1=colq.t[0:NT, 8 + h:9 + h], op=ALU.max), [den, colq], [den])
                V(E.reciprocal(out=rden.t[0:NT, h:h + 1], in_=den.t[0:NT, h:h + 1]), [den], [rden])
                V(E.scalar_tensor_tensor(out=hA.t[0:NT, h % 2, :], in0=psn_.t[0:NT, 0:128], scalar=rden.t[0:NT, h:h + 1], in1=so.t[0:NT, h * 128:(h + 1) * 128], op0=ALU.mult, op1=ALU.mult), [psn_, rden, so], [hA.b[h % 2]])
                yield
                V(E.bn_stats(out=bnst.t[0:NT, h, :], in_=hA.t[0:NT, h % 2, :]), [hA.b[h % 2]], [bnst])
                V(E.bn_aggr(out=mv.t[0:NT, h, :], in_=bnst.t[0:NT, h, :]), [bnst], [mv])
                A(E.activation(out=rstdh.t[0:NT, h:h + 1], in_=mv.t[0:NT, h, 1:2], func=AF.Sqrt, bias=epsc.t[0:NT, :], scale=1.0), [mv, epsc], [rstdh])
                V(E.reciprocal(out=rstdh.t[0:NT, h:h + 1], in_=rstdh.t[0:NT, h:h + 1]), [rstdh], [rstdh])
                V(E.tensor_scalar(out=hn_.t[0:NT, :], in0=hA.t[0:NT, h % 2, :], scalar1=mv.t[0:NT, h, 0:1], scalar2=rstdh.t[0:NT, h:h + 1], op0=ALU.subtract, op1=ALU.mult), [hA.b[h % 2], mv, rstdh], [hn_])
                yield
                pst = PS()
                T(E.transpose(out=pst.t[:, 0:NT], in_=hn_.t[0:NT, :], identity=ident.t[0:NT, 0:NT]), [hn_, ident], [pst])
                pop_carry()
                A(E.activation(out=mt.t[:, 0:NT], in_=pst.t[:, 0:NT], func=AF.Copy, scale=lnh.t[:, h:h + 1]), [pst, lnh], [mt])
                V(E.scalar_tensor_tensor(out=mt.t[:, 0:NT], in0=xc.t[:, h, 0:NT], scalar=skp.t[:, h:h + 1], in1=mt.t[:, 0:NT], op0=ALU.mult, op1=ALU.add), [xc, skp, mt], [mt])
                G(E.tensor_tensor(out=mix.t[:, h, 0:NT], in0=mt.t[:, 0:NT], in1=szm.t[:, h, 0:NT], op=ALU.mult), [mt, szm], [mix.b[h]])

        s5ps = {}

        def s5_B(j):
            ct = j // 4
            psb2 = psb[4 + j % 2]
            T(E.matmul(psb2.t[:, 0:NT], BTre.t[:, j, :], xs5b.t[:, ct, 0:NT], start=True, stop=True), [BTre, xs5b], [psb2])
            T(E.matmul(psb2.t[:, NT:N2], BTim.t[:, j, :], xs5b.t[:, ct, 0:NT], start=True, stop=True), [BTim, xs5b], [psb2])
            s5ps[j] = psb2

        def s5_chain1(j):
            ct = j // 4; q = j % 4; r = j % NS5
            P_, W_, Z_, xb, rm = s5P[r], s5W[r], s5Z[r], s5xb[r], rmat[r]
            psb2 = s5ps.pop(j)
            Bv = pr(psb2.t[:, 0:N2], 2)
            V(E.tensor_tensor(out=pr(P_.t[:, 0:N2], 2), in0=Bv, in1=tb(Ec, j), op=ALU.mult), [psb2, Ec], [P_])
            yield
            V(E.tensor_tensor(out=pr(P_.t[:, N2:N4], 2), in0=Bv, in1=tb(Es, j), op=ALU.mult), [psb2, Es], [P_])
            yield
            V(E.tensor_tensor(out=W_.t[:, 0:NT], in0=P_.t[:, 0:NT], in1=P_.t[:, N3:N4], op=ALU.add), [P_], [W_])
            yield
            V(E.tensor_tensor(out=W_.t[:, NT:N2], in0=P_.t[:, NT:N2], in1=P_.t[:, N2:N3], op=ALU.subtract), [P_], [W_])
            yield
            if smp:
                Wv = pr(W_.t[:, 0:N2], 2)
                ah = ah0j[r]
                V(E.tensor_scalar(out=ah.t[:, 0, :], in0=sore.t[:, :, j], scalar1=are.t[:, j:j + 1], scalar2=None, op0=ALU.mult), [sore, are], [ah])
                V(E.scalar_tensor_tensor(out=ah.t[:, 0, :], in0=soim.t[:, :, j], scalar=naim.t[:, j:j + 1], in1=ah.t[:, 0, :], op0=ALU.mult, op1=ALU.add), [soim, naim, ah], [ah])
                V(E.tensor_scalar(out=ah.t[:, 1, :], in0=soim.t[:, :, j], scalar1=are.t[:, j:j + 1], scalar2=None, op0=ALU.mult), [soim, are], [ah])
                V(E.scalar_tensor_tensor(out=ah.t[:, 1, :], in0=sore.t[:, :, j], scalar=aim.t[:, j:j + 1], in1=ah.t[:, 1, :], op0=ALU.mult, op1=ALU.add), [sore, aim, ah], [ah])
                V(E.tensor_tensor(out=Wv[:, :, :, 0:1], in0=Wv[:, :, :, 0:1], in1=ah.t[:, :, :].unsqueeze(3), op=ALU.add), [W_, ah], [W_])
                A(E.activation(out=pr(rm.t[:, 0:N2], 2), in_=seqm.t[:, :].rearrange("p (b t) -> p b t", t=4).unsqueeze(1).to_broadcast([128, 2, 16, 4]),
                               func=AF.Copy, scale=mag.t[:, j:j + 1]), [seqm, mag], [rm])
            else:
                Wv = pr(W_.t[:, 0:N2], 2)
                V(E.tensor_tensor(out=Wv[:, :, 0:1], in0=Wv[:, :, 0:1], in1=azin.t[:, j, :].unsqueeze(2), op=ALU.add), [W_, azin], [W_])
                A(E.activation(out=pr(rm.t[:, 0:N2], 2), in_=ones4.t[:, 0:NT].unsqueeze(1).to_broadcast([128, 2, NT]), func=AF.Copy, scale=mag.t[:, j:j + 1]), [ones4, mag], [rm])
                A(E.activation(out=pr(rm.t[:, 0:N2], 2)[:, :, 0:1], in_=pr(rm.t[:, 0:N2], 2)[:, :, 0:1], func=AF.Copy, scale=0.0), [rm], [rm])
            V(E.tensor_tensor_scan(out=Z_.t[:, 0:N2], data0=rm.t[:, 0:N2], data1=W_.t[:, 0:N2], initial=0.0, op0=ALU.mult, op1=ALU.add), [rm, W_], [Z_])
            Zv = pr(Z_.t[:, 0:N2], 2)
            G(E.tensor_tensor(out=pr(P_.t[:, 0:N2], 2), in0=Zv, in1=tb(Ec, j), op=ALU.mult), [Z_, Ec], [P_])
            G(E.tensor_tensor(out=pr(P_.t[:, N2:N4], 2), in0=Zv, in1=tb(Es, j), op=ALU.mult), [Z_, Es], [P_])

        def s5_chain2(j):
            ct = j // 4; q = j % 4; r = j % NS5
            P_, W_, Z_, xb, rm = s5P[r], s5W[r], s5Z[r], s5xb[r], rmat[r]
            V(E.tensor_tensor(out=xb.t[:, 0:NT], in0=P_.t[:, 0:NT], in1=P_.t[:, N3:N4], op=ALU.subtract), [P_], [xb])
            yield
            V(E.tensor_tensor(out=xb.t[:, NT:N2], in0=P_.t[:, N2:N3], in1=P_.t[:, NT:N2], op=ALU.add), [P_], [xb])
            yield
            if smp:
                def l3(a, b):
                    return P_.t[:, a:b].rearrange("p (b t) -> p b t", t=4)[:, :, 3:4]
                V(E.tensor_tensor(out=sore.t[:, :, j:j + 1], in0=l3(0, NT), in1=l3(N3, N4), op=ALU.subtract), [P_], [sore])
                V(E.tensor_tensor(out=soim.t[:, :, j:j + 1], in0=l3(N2, N3), in1=l3(NT, N2), op=ALU.add), [P_], [soim])
            else:
                V(E.tensor_tensor(out=xl_re.t[:, j:j + 1], in0=P_.t[:, NT - 1:NT], in1=P_.t[:, N4 - 1:N4], op=ALU.subtract), [P_], [xl_re])
                V(E.tensor_tensor(out=xl_im.t[:, j:j + 1], in0=P_.t[:, N3 - 1:N3], in1=P_.t[:, N2 - 1:N2], op=ALU.add), [P_], [xl_im])
            yield
            s5_C(j)

        def s5_C(j):
            ct = j // 4; q = j % 4; r = j % NS5
            xb = s5xb[r]; psy = psb[6 + ct % 2]
            T(E.matmul(psy.t[:, 0:NT], CTre.t[:, j, :], xb.t[:, 0:NT], start=(q == 0), stop=False), [CTre, xb], [psy])
            T(E.matmul(psy.t[:, 0:NT], CTimn.t[:, j, :], xb.t[:, NT:N2], start=False, stop=(q == 3)), [CTimn, xb], [psy])

        def s5_epi(ct):
            psy = psb[6 + ct % 2]
            V(E.scalar_tensor_tensor(out=ysb.t[:, ct, 0:NT], in0=xs5.t[:, ct, 0:NT], scalar=s5D.t[:, ct:ct + 1], in1=psy.t[:, 0:NT], op0=ALU.mult, op1=ALU.add), [xs5, s5D, psy], [ysb.b[ct]])
            yield
            g1 = gl1[ct % 2]; g2_ = gl2[ct % 2]
            A(E.activation(out=g1.t[:, 0:NT], in_=ysb.t[:, ct, 0:NT], func=AF.Square), [ysb.b[ct]], [g1])
            yield
            V(E.tensor_scalar(out=g1.t[:, 0:NT], in0=g1.t[:, 0:NT], scalar1=0.044715, scalar2=1.0, op0=ALU.mult, op1=ALU.add), [g1], [g1])
            yield
            V(E.tensor_tensor(out=g1.t[:, 0:NT], in0=g1.t[:, 0:NT], in1=ysb.t[:, ct, 0:NT], op=ALU.mult), [g1, ysb.b[ct]], [g1])
            yield
            A(E.activation(out=g2_.t[:, 0:NT], in_=g1.t[:, 0:NT], func=AF.Sigmoid, scale=1.5957691216057308), [g1], [g2_])
            yield
            V(E.tensor_tensor(out=yg.t[:, ct, 0:NT], in0=ysb.t[:, ct, 0:NT], in1=g2_.t[:, 0:NT], op=ALU.mult), [ysb.b[ct], g2_], [yg.b[ct]])
            yield
            G(E.tensor_copy(out=ygb.t[:, ct, 0:NT], in_=yg.t[:, ct, 0:NT]), [yg.b[ct]], [ygb])

        bg = []
        if glu_prev is not None:
            bg.append(glu_prev())
        s5_B(0)
        s5_B(1)
        for _ in s5_chain1(0):
            pass
        for j in range(16):
            if j + 2 < 16:
                s5_B(j + 2)
            if j % 2 == 1:
                pop_pending()
            gens = ([s5_chain1(j + 1)] if j + 1 < 16 else []) + [s5_chain2(j)] + bg
            del bg[:]
            while gens:
                for g_ in list(gens):
                    try:
                        next(g_)
                    except StopIteration:
                        gens.remove(g_)
            if j % 4 == 3:
                bg.append(s5_epi(j // 4))

        for g_ in bg:
            for _ in g_:
                pass
        while pending:
            pop_pending(background=False)
        gens = [gates_gen(), conv_gen()]
        while gens:
            for g_ in list(gens):
                try:
                    next(g_)
                except StopIteration:
                    gens.remove(g_)
        for pair in (((0,), (1,), (2,), (3,)) if smp else ((0, 1), (2, 3))):
            gens = [mlstm_head(h_) for h_ in pair]
            while gens:
                for g_ in list(gens):
                    try:
                        next(g_)
                    except StopIteration:
                        gens.remove(g_)
        pop_carry(len(cin))
        if not smp:
            V(E.tensor_copy(out=xmh.t[:, :, 0:3], in_=xmh.t[:, :, NT:NT + 3]), [xmh], [xmh])

        if not smp:
            V(E.tensor_tensor(out=t0.t[:], in0=are.t[:], in1=xl_re.t[:], op=ALU.mult), [are, xl_re], [t0])
            V(E.tensor_tensor(out=t1.t[:], in0=aim.t[:], in1=xl_im.t[:], op=ALU.mult), [aim, xl_im], [t1])
            V(E.tensor_tensor(out=azin.t[:, :, 0], in0=t0.t[:], in1=t1.t[:], op=ALU.subtract), [t0, t1], [azin])
            V(E.tensor_tensor(out=t0.t[:], in0=aim.t[:], in1=xl_re.t[:], op=ALU.mult), [aim, xl_re], [t0])
            V(E.tensor_tensor(out=t1.t[:], in0=are.t[:], in1=xl_im.t[:], op=ALU.mult), [are, xl_im], [t1])
            V(E.tensor_tensor(out=azin.t[:, :, 1], in0=t0.t[:], in1=t1.t[:], op=ALU.add), [t0, t1], [azin])
        def glu_gen():
            for oc in range(4):
                ps = PS()
                for kc in range(4):
                    T(E.matmul(ps.t[:, 0:NT], wglu.t[:, kc, oc * 128:(oc + 1) * 128], ygb.t[:, kc, 0:NT], start=(kc == 0), stop=(kc == 3)), [wglu, ygb], [ps])
                    yield
                g1 = gl1[oc % 2]
                A(E.activation(out=g1.t[:, 0:NT], in_=ps.t[:, 0:NT], func=AF.Sigmoid, bias=bglu.t[:, oc:oc + 1], scale=1.0), [ps, bglu], [g1])
                yield
                V(E.tensor_tensor(out=g1.t[:, 0:NT], in0=g1.t[:, 0:NT], in1=yg.t[:, oc, 0:NT], op=ALU.mult), [g1, yg.b[oc]], [g1])
                yield
                G(E.tensor_tensor(out=mix.t[:, 4 + oc, 0:NT], in0=g1.t[:, 0:NT], in1=szs.t[:, oc, 0:NT], op=ALU.mult), [g1, szs], [mix.b[4 + oc]])
                yield

        def c_outproj(half):
            if half == 0:
                dma_in(h2.t[0:NT, :], x_src, h2.b)
            ps = PS()
            for k in range(8):
                T(E.matmul(ps.t[0:NT, :], mix.t[:, k, 0:NT], wout.t[:, k, half * 512:(half + 1) * 512], start=(k == 0), stop=(k == 7)), [mix.b[k], wout], [ps])
            V(E.tensor_tensor(out=h2.t[0:NT, half * 512:(half + 1) * 512], in0=ps.t[0:NT, :], in1=h2.t[0:NT, half * 512:(half + 1) * 512], op=ALU.add), [ps, h2], [h2])

        def c_h2T():
            for half in range(2):
                ps = PS()
                for kk in range(4):
                    k = half * 4 + kk
                    T(E.transpose(out=ps.t[:, kk * 128:kk * 128 + NT], in_=h2.t[0:NT, k * 128:(k + 1) * 128], identity=ident.t[0:NT, 0:NT]), [h2, ident], [ps])
                A(E.activation(out=h2T.t[:, half * 4:half * 4 + 4, 0:NT], in_=ps.t[:, :].rearrange("p (k t) -> p k t", k=4)[:, :, 0:NT], func=AF.Copy), [ps], [h2T])

        def c_gate(half):
            ps = PS()
            for k in range(8):
                T(E.matmul(ps.t[0:NT, :], h2T.t[:, k, 0:NT], wgate.t[:, k, half * 512:(half + 1) * 512], start=(k == 0), stop=(k == 7)), [h2T, wgate], [ps])
            A(E.activation(out=sgate.t[0:NT, half * 512:(half + 1) * 512], in_=ps.t[0:NT, :], func=AF.Sigmoid), [ps], [sgate])

        def c_tail():
            G(E.tensor_tensor(out=esb.t[0:NT, :], in0=esb.t[0:NT, :], in1=sgate.t[0:NT, :], op=ALU.mult), [esb, sgate], [esb])
            G(E.tensor_tensor(out=esb.t[0:NT, :], in0=esb.t[0:NT, :], in1=h2.t[0:NT, :], op=ALU.add), [esb, h2], [esb])
            A(E.activation(out=sgate.t[0:NT, :], in_=esb.t[0:NT, :], func=AF.Square, accum_out=ssq.t[0:NT, 3:4]), [esb], [sgate, ssq])
            A(E.activation(out=rstd.t[0:NT, 3:4], in_=ssq.t[0:NT, 3:4], func=AF.Sqrt, bias=epsc.t[0:NT, :], scale=1.0 / DM), [ssq, epsc], [rstd])
            V(E.reciprocal(out=rstd.t[0:NT, 3:4], in_=rstd.t[0:NT, 3:4]), [rstd], [rstd])
            V(E.scalar_tensor_tensor(out=sgate.t[0:NT, :], in0=esb.t[0:NT, :], scalar=rstd.t[0:NT, 3:4], in1=lnfin.t[0:NT, :], op0=ALU.mult, op1=ALU.mult), [esb, rstd, lnfin], [sgate])
            dma_out(y_dst, sgate.t[0:NT, :], [sgate.b])
        return [lambda: c_outproj(0), lambda: c_outproj(1), c_h2T, lambda: c_gate(0), lambda: c_gate(1), c_tail], glu_gen

    if DBG_STAGE == "setup":
        dma_out(yp_o[0:128, 0:128], Ec.t[:, 3, :], [Ec.b])
        dma_out(yp_o[0:128, 128:256], Es.t[:, 15, :], [Es.b])
        V(E.tensor_copy(out=ytile.t[:, 0:128], in_=BTre.t[:, 5, :]), [BTre], [ytile])
        V(E.tensor_copy(out=ytile.t[:, 128:256], in_=CTimn.t[:, 9, :]), [CTimn], [ytile])
        V(E.tensor_copy(out=ytile.t[:, 256:512], in_=win.t[:, 7, 1400:1656]), [win], [ytile])
        V(E.tensor_copy(out=ytile.t[:, 512:768], in_=wgate.t[:, 7, 100:356]), [wgate], [ytile])
        V(E.tensor_copy(out=ytile.t[:, 768:1024], in_=wglu.t[:, 3, 256:512]), [wglu], [ytile])
        dma_out(yp_o[0:128, 256:1280 - 256], ytile.t[:, 0:768], [ytile.b])
        p.build()
        return nc, p
    dma_in(m0c.t[:, :], sm_d.rearrange("b h -> h b"), m0c.b, slow=True)
    V(E.tensor_copy(out=m0row.t[:, :].rearrange("h (b t) -> h b t", t=4), in_=m0c.t[:, :].unsqueeze(2).to_broadcast([4, 16, 4])), [m0c], [m0row])
    cv = sconv_d.rearrange("b j (c q) -> (b j c) q", q=128)
    dma_in(ytile.t[:, 0:128], cv[0:128, :], ytile.b)
    dma_in(ytile.t[0:64, 128:256], cv[128:192, :], ytile.b, join=True)
    hvr = sre_d.rearrange("b (j g) q -> (b j) (g q)", g=2); hvi = sim_d.rearrange("b (j g) q -> (b j) (g q)", g=2)
    for hf in range(2):
        dma_in(ytile.t[:, 256 + hf * 128:384 + hf * 128], hvr[hf * 128:(hf + 1) * 128, :], ytile.b, join=True)
        dma_in(ytile.t[:, 512 + hf * 128:640 + hf * 128], hvi[hf * 128:(hf + 1) * 128, :], ytile.b, join=True)
    dma_in(ytile.t[0:64, 768:896], sn_d.rearrange("b h d -> (b h) d"), ytile.b, join=True)
    psp = PS()
    T(E.transpose(out=psp.t[:, 0:128], in_=ytile.t[:, 0:128], identity=ident.t[:, :]), [ytile, ident], [psp])
    T(E.transpose(out=psp.t[:, 128:192], in_=ytile.t[0:64, 128:256], identity=ident.t[0:64, 0:64]), [ytile, ident], [psp])
    for c in range(4):
        V(E.tensor_copy(out=xms.t[:, c, :, 0:3], in_=psp.t[:, 0:192].rearrange("p (b j c) -> p c b j", j=3, c=4)[:, c]), [psp], [xms])
    for src0, dstt in ((256, sore), (512, soim)):
        psp = PS()
        for hf in range(2):
            T(E.transpose(out=psp.t[:, hf * 128:(hf + 1) * 128], in_=ytile.t[:, src0 + hf * 128:src0 + (hf + 1) * 128], identity=ident.t[:, :]), [ytile, ident], [psp])
        V(E.tensor_copy(out=dstt.t[:, :, :].rearrange("p b j -> p (b j)"), in_=psp.t[:, 0:256]), [psp], [dstt])
    psp = PS()
    T(E.transpose(out=psp.t[:, 0:64], in_=ytile.t[0:64, 768:896], identity=ident.t[0:64, 0:64]), [ytile, ident], [psp])
    V(E.tensor_copy(out=nT.t[:, :], in_=psp.t[:, 0:64]), [psp], [nT])
    V(E.tensor_scalar(out=naim.t[:], in0=aim.t[:], scalar1=-1.0, scalar2=None, op0=ALU.mult), [aim], [naim])
    carry = []
    glu_prev = None
    def tile_id(n):
        return ("p", n) if n < NPT else (("s", 0) if n == NPT else None)
    load_inputs("p", 0)
    front("p", 0, nxt=tile_id(1))
    for ti in range(NPT):
        nf = tile_id(ti + 1)
        carry, glu_prev = do_tile("p", ti, carry, nxt=(nf[0], nf[1], tile_id(ti + 2)), glu_prev=glu_prev)
    if DBG_STAGE == "prompt":
        p.build()
        return nc, p
    for h in range(4):
        dma_out(pC_o[h, :, :], Caug[h].t[:, 0:128], [Caug[h].b])
        dma_out(pn_o[h, :].rearrange("(d o) -> d o", o=1), Caug[h].t[:, 128:129], [Caug[h].b], slow=True)
    dma_out(pm_o.rearrange("(h o) -> h o", o=1), mnew.t[:, 0:1], [mnew.b], slow=True)
    for c in range(4):
        dma_out(pconv_o[:, c * 128:(c + 1) * 128].rearrange("j q -> q j"), xmh.t[:, c, 0:3], [xmh.b], slow=True)
    dma_out(pre_o.rearrange("(j g) q -> (g q) j", g=2), xl_re.t[:, :], [xl_re.b], slow=True)
    dma_out(pim_o.rearrange("(j g) q -> (g q) j", g=2), xl_im.t[:, :], [xl_im.b], slow=True)

    carry, glu_prev = do_tile("s", 0, carry, glu_prev=glu_prev)
    for _ in glu_prev():
        pass
    for f in carry:
        f()
    dma_out(om_o.rearrange("b h -> h b"), mnew.t[:, :], [mnew.b], slow=True)
    ost = TL(s5P[0].t, s5P[0].b); ost2 = TL(s5P[1].t, s5P[1].b)
    for nm, src, dst in (("re", sore, ore_o), ("im", soim, oim_o)):
        pso = PS()
        for hf in range(2):
            T(E.transpose(out=pso.t[:, hf * 128:(hf + 1) * 128], in_=src.t[:, hf * 8:(hf + 1) * 8, :].rearrange("p b j -> p (b j)"), identity=ident.t[:, :]), [src, ident], [pso])
        o_ = ost if nm == "re" else ost2
        V(E.tensor_copy(out=o_.t[:, 0:256], in_=pso.t[:, 0:256]), [pso], [o_])
        dv = dst.rearrange("b (j g) q -> (b j) (g q)", g=2)
        for hf in range(2):
            dma_out(dv[hf * 128:(hf + 1) * 128, :], o_.t[:, hf * 128:(hf + 1) * 128], [o_.b])
    for c in range(4):
        V(E.tensor_copy(out=ost.t[:, 256:448].rearrange("p (b j c) -> p c b j", j=3, c=4)[:, c], in_=xms.t[:, c, :, 4:7]), [xms], [ost])
    pso = PS()
    T(E.transpose(out=pso.t[:, 0:128], in_=ost.t[:, 256:384], identity=ident.t[:, :]), [ost, ident], [pso])
    T(E.transpose(out=pso.t[0:64, 128:256], in_=ost.t[:, 384:448], identity=ident.t[:, :]), [ost, ident], [pso])
    V(E.tensor_copy(out=ost2.t[:, 256:384], in_=pso.t[:, 0:128]), [pso], [ost2])
    V(E.tensor_copy(out=ost2.t[0:64, 384:512], in_=pso.t[0:64, 128:256]), [pso], [ost2])
    cvo = oconv_o.rearrange("b j (c q) -> (b j c) q", q=128)
    dma_out(cvo[0:128, :], ost2.t[:, 256:384], [ost2.b])
    dma_out(cvo[128:192, :], ost2.t[0:64, 384:512], [ost2.b])
    pso = PS()
    T(E.transpose(out=pso.t[0:64, 0:128], in_=nOut.t[:, :], identity=ident.t[:, :]), [nOut, ident], [pso])
    V(E.tensor_copy(out=ost.t[0:64, 0:128], in_=pso.t[0:64, 0:128]), [pso], [ost])
    dma_out(on_o.rearrange("b h d -> (b h) d"), ost.t[0:64, 0:128], [ost.b])

    p.build()
    return nc, p


_CACHE = {}


def kernel(**inputs):
    f = lambda a: np.ascontiguousarray(np.asarray(a, dtype=np.float32))
    if "nc" not in _CACHE:
        _CACHE["nc"] = build_program()
    nc, prog = _CACHE["nc"]
    consts = make_consts()
    shared = {}
    for k in ("ln_mix", "w_in", "b_igate", "b_fgate", "conv_w", "conv_b", "w_q", "w_k", "w_v", "ln_head", "skip_a",
              "s5_lam_re", "s5_lam_im", "s5_log_dt", "s5_B_re", "s5_B_im", "s5_C_re", "s5_C_im", "s5_D", "w_glu",
              "b_glu", "w_out", "w_ple", "ln_ple", "w_ple_gate"):
        shared[k] = f(inputs[k])[0]
    shared["ln_final"] = f(inputs["ln_final"])
    for k, v in consts.items():
        shared["c_" + k] = v
    xp = f(inputs["x_prompt"]); xs = f(inputs["x_sample"])
    pp = f(inputs["p_prompt"])[0]; ps = f(inputs["p_sample"])[0]
    sC = f(inputs["state_mlstm_C"])[0]; sn = f(inputs["state_mlstm_n"])[0]; sm = f(inputs["state_mlstm_m"])[0]
    sconv = f(inputs["state_conv"])[0]; sre = f(inputs["state_s5_re"])[0]; sim = f(inputs["state_s5_im"])[0]
    in_maps = []
    for c in range(NCORES):
        sl = slice(c * SB, (c + 1) * SB)
        m = dict(shared)
        m["xp"] = xp[c]; m["xs"] = np.ascontiguousarray(xs[sl].reshape(64, DM))
        m["pp"] = pp[c]; m["psm"] = np.ascontiguousarray(ps[sl].reshape(64, 256))
        m["sC"] = np.ascontiguousarray(sC[sl]); m["sn"] = np.ascontiguousarray(sn[sl]); m["sm"] = np.ascontiguousarray(sm[sl])
        m["sconv"] = np.ascontiguousarray(sconv[sl]); m["sre"] = np.ascontiguousarray(sre[sl]); m["sim"] = np.ascontiguousarray(sim[sl])
        in_maps.append(m)
    res = run_bass_kernel_spmd(nc, in_maps, core_ids=list(range(NCORES)))
    R = res.results
    g = lambda k: [np.asarray(R[c][k], dtype=np.float32) for c in range(NCORES)]
    y_prompt = np.stack(g("yp"), 0)
    y_sample = np.concatenate([a.reshape(SB, 4, DM) for a in g("ys")], 0)
    pC = np.stack(g("pC"), 0)[None]; pn = np.stack(g("pn"), 0)[None]; pm = np.stack(g("pm"), 0)[None]
    pconv = np.stack(g("pconv"), 0)[None]; pre = np.stack(g("pre"), 0)[None]; pim = np.stack(g("pim"), 0)[None]
    oC = np.concatenate(g("oC"), 0)[None]; on = np.concatenate(g("on"), 0)[None]; om = np.concatenate(g("om"), 0)[None]
    oconv = np.concatenate(g("oconv"), 0)[None]; ore = np.concatenate(g("ore"), 0)[None]; oim = np.concatenate(g("oim"), 0)[None]
    return (y_prompt, y_sample, pC, pn, pm, pconv, pre, pim, oC, on, om, oconv, ore, oim)
```
